# Optimizing a Trainium2 kernel written in Bass

```python
import math
import jax, jax.numpy as jnp
from jax import lax
import numpy as np

D_MODEL = 1024
BATCH = 8
SEQ = 2048
DEPTH = 1
DEC_BATCH = 128
DEC_SEQ = 1
PAST_LEN = 2048
PAGE_SIZE = 128

ATT_HEADS = 16
HEAD_DIM = 64
D_ATT = ATT_HEADS * HEAD_DIM
Q_BLOCK = 128
SB_BIAS_INIT = -7.0
SSM_GROUP = 16
D_SSM = D_MODEL
SSM_GROUPS = D_SSM // SSM_GROUP
SSM_STATE = 64
DT_MIN = 1e-3
DT_MAX = 1e-1
LN_EPS = 1e-5
DEEPNORM_ALPHA = (2.0 * DEPTH) ** 0.25
DEEPNORM_BETA = (8.0 * DEPTH) ** -0.25
D_IN = 4 * D_ATT + 2 * D_SSM + 2 * D_MODEL
IN_SPLITS = (D_ATT, 2 * D_ATT, 3 * D_ATT, 4 * D_ATT,
             4 * D_ATT + D_SSM, 4 * D_ATT + 2 * D_SSM,
             4 * D_ATT + 2 * D_SSM + D_MODEL)

kernel_name = "stickbreak_s5_gated_hybrid_step"


def _layer_norm(x, g, b):
    xf = x.astype(jnp.float32)
    mu = xf.mean(-1, keepdims=True)
    var = jnp.square(xf - mu).mean(-1, keepdims=True)
    return ((xf - mu) * lax.rsqrt(var + LN_EPS) * g + b).astype(x.dtype)


def _modulate(x, c, w_cond, b_cond):
    mod = c @ w_cond + b_cond
    shift, scale, gate = jnp.split(mod, 3, axis=-1)
    h = x * (1.0 + scale[:, None]) + shift[:, None]
    return h, gate[:, None]


def _project_in(h, w_in):
    bsz, s = h.shape[:2]
    z = h @ w_in
    q, k, v, g_att, u, g_ssm, m_att, m_ssm = jnp.split(z, IN_SPLITS, axis=-1)
    heads = lambda t: t.reshape(bsz, s, ATT_HEADS, HEAD_DIM)
    return heads(q), heads(k), heads(v), g_att, u, g_ssm, m_att, m_ssm


def _stick_breaking(q, k, v, q_pos, k_pos, sb_bias):
    z = jnp.einsum("bqhd,bkhd->bhqk", q.astype(jnp.float32), k.astype(jnp.float32)) * (HEAD_DIM ** -0.5)
    z = z + sb_bias.astype(jnp.float32)[None, :, None, None]
    causal = (k_pos[None, :] < q_pos[:, None])[None, None]
    log_beta = jax.nn.log_sigmoid(z)
    log_keep = jnp.where(causal, log_beta - z, 0.0)
    log_w = log_beta + lax.cumsum(log_keep, axis=3, reverse=True) - log_keep
    w = jnp.where(causal, jnp.exp(log_w), 0.0)
    return jnp.einsum("bhqk,bkhd->bqhd", w, v.astype(jnp.float32)).astype(v.dtype)


def _sb_prompt(q, k, v, sb_bias):
    b, s, h, d = q.shape
    nb = s // Q_BLOCK
    q_blocks = q.reshape(b, nb, Q_BLOCK, h, d).swapaxes(0, 1)
    k_pos = jnp.arange(s, dtype=jnp.int32)
    starts = jnp.arange(nb, dtype=jnp.int32) * Q_BLOCK

    def one_block(args):
        qb, start = args
        return _stick_breaking(qb, k, v, start + jnp.arange(Q_BLOCK, dtype=jnp.int32), k_pos, sb_bias)

    o = lax.map(one_block, (q_blocks, starts))
    return o.swapaxes(0, 1).reshape(b, s, h, d)


def _ssm_discretize(a_re, a_im, log_dt, b_re, b_im):
    f = jnp.float32
    dt = jnp.exp(log_dt.astype(f))[:, None]
    lr, li = a_re.astype(f), a_im.astype(f)
    mag = jnp.exp(lr * dt)
    ab_re, ab_im = mag * jnp.cos(li * dt), mag * jnp.sin(li * dt)
    den = lr * lr + li * li
    nr = ab_re - 1.0
    co_re = (nr * lr + ab_im * li) / den
    co_im = (ab_im * lr - nr * li) / den
    br, bi = b_re.astype(f), b_im.astype(f)
    bb_re = co_re[..., None] * br - co_im[..., None] * bi
    bb_im = co_re[..., None] * bi + co_im[..., None] * br
    return ab_re, ab_im, bb_re, bb_im


def _complex_affine_combine(e1, e2):
    a1r, a1i, b1r, b1i = e1
    a2r, a2i, b2r, b2i = e2
    return (a2r * a1r - a2i * a1i,
            a2r * a1i + a2i * a1r,
            a2r * b1r - a2i * b1i + b2r,
            a2r * b1i + a2i * b1r + b2i)


def _ssm_branch(u, gate, h0_re, h0_im, a_re, a_im, log_dt, b_re, b_im, c_re, c_im,
                d_skip, w_glu, b_glu, w_ssm_out):
    f = jnp.float32
    bsz, s, _ = u.shape
    uf = u.astype(f)
    ug = uf.reshape(bsz, s, SSM_GROUPS, SSM_GROUP)
    ab_re, ab_im, bb_re, bb_im = _ssm_discretize(a_re, a_im, log_dt, b_re, b_im)
    bu_re = jnp.einsum("gpc,bsgc->bsgp", bb_re, ug)
    bu_im = jnp.einsum("gpc,bsgc->bsgp", bb_im, ug)
    h0r, h0i = h0_re.astype(f), h0_im.astype(f)
    bu_re = bu_re.at[:, 0].add(ab_re * h0r - ab_im * h0i)
    bu_im = bu_im.at[:, 0].add(ab_re * h0i + ab_im * h0r)
    a_re_t = jnp.broadcast_to(ab_re, (1, s) + ab_re.shape)
    a_im_t = jnp.broadcast_to(ab_im, (1, s) + ab_im.shape)
    _, _, xr, xi = lax.associative_scan(_complex_affine_combine,
                                        (a_re_t, a_im_t, bu_re, bu_im), axis=1)
    y = (jnp.einsum("gcp,bsgp->bsgc", c_re.astype(f), xr)
         - jnp.einsum("gcp,bsgp->bsgc", c_im.astype(f), xi)).reshape(bsz, s, D_SSM)
    y = y + d_skip.astype(f) * uf
    g = jax.nn.gelu(y)
    y = g * jax.nn.sigmoid(g @ w_glu.astype(f) + b_glu.astype(f))
    y = (y * jax.nn.silu(gate.astype(f))).astype(u.dtype)
    return y @ w_ssm_out, xr[:, -1].astype(h0_re.dtype), xi[:, -1].astype(h0_re.dtype)


def _finish(x, gate, o_att, g_att, y_ssm, m_att, m_ssm, w_att_out, w_out, ln_g, ln_b):
    bsz, s = x.shape[:2]
    y_att = (o_att.reshape(bsz, s, D_ATT) * jax.nn.silu(g_att)) @ w_att_out
    merged = jax.nn.sigmoid(m_att) * y_att + jax.nn.sigmoid(m_ssm) * y_ssm
    return _layer_norm(DEEPNORM_ALPHA * x + gate * (merged @ w_out), ln_g, ln_b)


def setup_inputs(seed: int = 0) -> dict:
    key = jax.random.key(seed)
    ks = jax.random.split(key, 32)
    f = jnp.float32
    n_pages = PAST_LEN // PAGE_SIZE
    n_phys = (DEC_BATCH * n_pages * 5) // 4
    nrm = lambda k, shape, s: jax.random.normal(k, shape, f) * s

    x_prompt = nrm(ks[0], (BATCH, SEQ, D_MODEL), 1.0)
    x_sample = nrm(ks[1], (DEC_BATCH, DEC_SEQ, D_MODEL), 1.0)
    c_prompt = nrm(ks[2], (BATCH, D_MODEL), 1.0)
    c_sample = nrm(ks[3], (DEC_BATCH, D_MODEL), 1.0)
    cache_k = nrm(ks[4], (DEPTH, n_phys, PAGE_SIZE, ATT_HEADS, HEAD_DIM), 1.0)
    cache_v = nrm(ks[5], (DEPTH, n_phys, PAGE_SIZE, ATT_HEADS, HEAD_DIM), DEEPNORM_BETA)
    state_ssm_re = nrm(ks[6], (DEPTH, DEC_BATCH, SSM_GROUPS, SSM_STATE), 0.3)
    state_ssm_im = nrm(ks[7], (DEPTH, DEC_BATCH, SSM_GROUPS, SSM_STATE), 0.3)
    perm = jax.random.permutation(ks[8], n_phys)
    page_table = perm[:DEC_BATCH * n_pages].reshape(DEC_BATCH, n_pages).astype(jnp.int32)

    w_cond = nrm(ks[9], (DEPTH, D_MODEL, 3 * D_MODEL), 0.5 * D_MODEL ** -0.5)
    b_cond = nrm(ks[10], (DEPTH, 3 * D_MODEL), 0.01)
    col_scale = jnp.concatenate([jnp.ones((2 * D_ATT,), f),
                                 jnp.full((D_ATT,), DEEPNORM_BETA, f),
                                 jnp.ones((D_IN - 3 * D_ATT,), f)])
    w_in = nrm(ks[11], (DEPTH, D_MODEL, D_IN), D_MODEL ** -0.5) * col_scale
    sb_bias = SB_BIAS_INIT + nrm(ks[27], (DEPTH, ATT_HEADS), 0.1)

    ssm_a_re = -0.5 + nrm(ks[12], (DEPTH, SSM_GROUPS, SSM_STATE), 0.01)
    ssm_a_im = (math.pi * jnp.arange(SSM_STATE, dtype=f))[None, None, :] \
        + nrm(ks[13], (DEPTH, SSM_GROUPS, SSM_STATE), 0.01)
    ssm_log_dt = jax.random.uniform(ks[14], (DEPTH, SSM_GROUPS), f,
                                    math.log(DT_MIN), math.log(DT_MAX))
    ssm_b_re = nrm(ks[15], (DEPTH, SSM_GROUPS, SSM_STATE, SSM_GROUP), (2 * SSM_GROUP) ** -0.5)
    ssm_b_im = nrm(ks[16], (DEPTH, SSM_GROUPS, SSM_STATE, SSM_GROUP), (2 * SSM_GROUP) ** -0.5)
    ssm_c_re = nrm(ks[17], (DEPTH, SSM_GROUPS, SSM_GROUP, SSM_STATE), (2 * SSM_STATE) ** -0.5)
    ssm_c_im = nrm(ks[18], (DEPTH, SSM_GROUPS, SSM_GROUP, SSM_STATE), (2 * SSM_STATE) ** -0.5)
    ssm_d = nrm(ks[19], (DEPTH, D_SSM), 1.0)
    w_glu = nrm(ks[20], (DEPTH, D_SSM, D_SSM), D_SSM ** -0.5)
    b_glu = nrm(ks[21], (DEPTH, D_SSM), 0.01)
    w_att_out = nrm(ks[22], (DEPTH, D_ATT, D_MODEL), DEEPNORM_BETA * D_ATT ** -0.5)
    w_ssm_out = nrm(ks[23], (DEPTH, D_SSM, D_MODEL), DEEPNORM_BETA * D_SSM ** -0.5)
    w_out = nrm(ks[24], (DEPTH, D_MODEL, D_MODEL), DEEPNORM_BETA * D_MODEL ** -0.5)
    ln_g = 1.0 + nrm(ks[25], (DEPTH, D_MODEL), 0.02)
    ln_b = nrm(ks[26], (DEPTH, D_MODEL), 0.02)
    return {"x_prompt": x_prompt, "x_sample": x_sample,
            "c_prompt": c_prompt, "c_sample": c_sample,
            "cache_k": cache_k, "cache_v": cache_v,
            "state_ssm_re": state_ssm_re, "state_ssm_im": state_ssm_im,
            "page_table": page_table,
            "w_cond": w_cond, "b_cond": b_cond, "w_in": w_in, "sb_bias": sb_bias,
            "ssm_a_re": ssm_a_re, "ssm_a_im": ssm_a_im, "ssm_log_dt": ssm_log_dt,
            "ssm_b_re": ssm_b_re, "ssm_b_im": ssm_b_im,
            "ssm_c_re": ssm_c_re, "ssm_c_im": ssm_c_im, "ssm_d": ssm_d,
            "w_glu": w_glu, "b_glu": b_glu,
            "w_att_out": w_att_out, "w_ssm_out": w_ssm_out, "w_out": w_out,
            "ln_g": ln_g, "ln_b": ln_b}


def reference(x_prompt, x_sample, c_prompt, c_sample, cache_k, cache_v,
              state_ssm_re, state_ssm_im, page_table,
              w_cond, b_cond, w_in, sb_bias, ssm_a_re, ssm_a_im, ssm_log_dt,
              ssm_b_re, ssm_b_im, ssm_c_re, ssm_c_im, ssm_d, w_glu, b_glu,
              w_att_out, w_ssm_out, w_out, ln_g, ln_b):
    bsz = x_prompt.shape[0]
    dbsz, dseq = x_sample.shape[:2]
    past_len = page_table.shape[1] * cache_k.shape[2]
    q_pos_s = past_len + jnp.arange(dseq, dtype=jnp.int32)
    k_pos_s = jnp.arange(past_len + dseq, dtype=jnp.int32)
    h0_zero = jnp.zeros((bsz, SSM_GROUPS, SSM_STATE), state_ssm_re.dtype)

    xp, xs = x_prompt, x_sample
    kp_l, vp_l, srp_l, sip_l, ks_l, vs_l, srs_l, sis_l = [], [], [], [], [], [], [], []
    for l in range(DEPTH):
        ssm_p = (ssm_a_re[l], ssm_a_im[l], ssm_log_dt[l], ssm_b_re[l], ssm_b_im[l],
                 ssm_c_re[l], ssm_c_im[l], ssm_d[l], w_glu[l], b_glu[l], w_ssm_out[l])
        hp, gate_p = _modulate(xp, c_prompt, w_cond[l], b_cond[l])
        q, k, v, g_att, u, g_ssm, m_att, m_ssm = _project_in(hp, w_in[l])
        o_att = _sb_prompt(q, k, v, sb_bias[l])
        y_ssm, sr, si = _ssm_branch(u, g_ssm, h0_zero, h0_zero, *ssm_p)
        xp = _finish(xp, gate_p, o_att, g_att, y_ssm, m_att, m_ssm,
                     w_att_out[l], w_out[l], ln_g[l], ln_b[l])
        kp_l.append(k); vp_l.append(v); srp_l.append(sr); sip_l.append(si)

        hs, gate_s = _modulate(xs, c_sample, w_cond[l], b_cond[l])
        q, k, v, g_att, u, g_ssm, m_att, m_ssm = _project_in(hs, w_in[l])
        k_past = cache_k[l][page_table].reshape(dbsz, past_len, ATT_HEADS, HEAD_DIM)
        v_past = cache_v[l][page_table].reshape(dbsz, past_len, ATT_HEADS, HEAD_DIM)
        k_all = jnp.concatenate([k_past, k.astype(k_past.dtype)], axis=1)
        v_all = jnp.concatenate([v_past, v.astype(v_past.dtype)], axis=1)
        o_att = _stick_breaking(q, k_all, v_all, q_pos_s, k_pos_s, sb_bias[l]).astype(xs.dtype)
        y_ssm, sr, si = _ssm_branch(u, g_ssm, state_ssm_re[l], state_ssm_im[l], *ssm_p)
        xs = _finish(xs, gate_s, o_att, g_att, y_ssm, m_att, m_ssm,
                     w_att_out[l], w_out[l], ln_g[l], ln_b[l])
        ks_l.append(k); vs_l.append(v); srs_l.append(sr); sis_l.append(si)

    k_prompt, v_prompt = jnp.stack(kp_l), jnp.stack(vp_l)
    ssm_re_prompt, ssm_im_prompt = jnp.stack(srp_l), jnp.stack(sip_l)
    k_sample, v_sample = jnp.stack(ks_l), jnp.stack(vs_l)
    ssm_re_sample, ssm_im_sample = jnp.stack(srs_l), jnp.stack(sis_l)
    return (xp, xs, k_prompt, v_prompt, ssm_re_prompt, ssm_im_prompt,
            k_sample, v_sample, ssm_re_sample, ssm_im_sample)
```

```python
import math
import ml_dtypes
from concourse.bass_utils import run_bass_kernel_spmd
import numpy as np
from contextlib import ExitStack
import concourse.bass as bass
import concourse.mybir as mybir

F32 = mybir.dt.float32
BF16 = mybir.dt.bfloat16
I32 = mybir.dt.int32
AF = mybir.ActivationFunctionType
ALU = mybir.AluOpType
AX = mybir.AxisListType


import types


def _snap(fn):
    if fn.__closure__ is None:
        return fn
    cells = []
    for c in fn.__closure__:
        try:
            cells.append(types.CellType(c.cell_contents))
        except ValueError:
            cells.append(c)
    return types.FunctionType(fn.__code__, fn.__globals__, fn.__name__, fn.__defaults__, tuple(cells))


class _Op:
    __slots__ = ("fn", "waits", "dwaits", "idx", "dma", "milestone")

    def __init__(self, fn, idx, dma=None):
        self.fn = _snap(fn)
        self.waits = []
        self.dwaits = []
        self.idx = idx
        self.dma = dma
        self.milestone = False


class Prog:
    ENGS = ("pe", "act", "dve", "pool", "sp")

    def __init__(self, nc, stack):
        self.nc = nc
        self.stack = stack
        self.gstack = stack
        self.h = {"pe": nc.tensor, "act": nc.scalar, "dve": nc.vector,
                  "pool": nc.gpsimd, "sp": nc.sync}
        self.ops = {e: [] for e in self.ENGS}
        self.seen = {e: {} for e in self.ENGS}
        self.last_w = {}
        self.readers = {}
        self.dsem_cnt = {}
        self.out_dma_keys = set()
        self.same_engine_sync = {"act", "dve", "pool"}

    def sb(self, name, shape, dt):
        self._uid = getattr(self, "_uid", 0) + 1
        return self.stack.enter_context(self.nc.sbuf_tensor("s%d_%s" % (self._uid, name), list(shape), dt))

    def ps(self, name, shape, dt=F32):
        return self.stack.enter_context(self.nc.psum_tensor(name, list(shape), dt))

    def _deps(self, reads, writes):
        deps = []
        for r in reads:
            t = self.last_w.get(r)
            if t is not None:
                deps.append(t)
        for w in writes:
            t = self.last_w.get(w)
            if t is not None:
                deps.append(t)
            deps.extend(self.readers.get(w, ()))
        return deps

    def _add_waits(self, eng, op, deps):
        seen = self.seen[eng]
        for t in deps:
            kind, key, val = t
            if kind == "e":
                if key == eng and eng not in self.same_engine_sync:
                    continue
                if key == eng and val >= op.idx:
                    continue
                if seen.get(("e", key), -1) >= val:
                    continue
                seen[("e", key)] = val
                op.waits.append((key, val))
                self.ops[key][val].milestone = True
            else:
                if seen.get(("d", key), -1) >= val:
                    continue
                seen[("d", key)] = val
                op.dwaits.append((key, val))

    def _commit(self, tok, reads, writes):
        for r in reads:
            self.readers.setdefault(r, []).append(tok)
        for w in writes:
            self.last_w[w] = tok
            self.readers[w] = []

    def op(self, eng, fn, reads=(), writes=()):
        lst = self.ops[eng]
        o = _Op(fn, len(lst))
        self._add_waits(eng, o, self._deps(reads, writes))
        lst.append(o)
        self._commit(("e", eng, o.idx), reads, writes)
        return o

    def dma(self, fn, semkey, reads=(), writes=(), q="sp", is_out=False):
        lst = self.ops[q]
        o = _Op(fn, len(lst), dma=semkey)
        self._add_waits(q, o, self._deps(reads, writes))
        lst.append(o)
        c = self.dsem_cnt.get(semkey, 0) + 16
        self.dsem_cnt[semkey] = c
        self._commit(("d", semkey, c), reads, writes)
        if is_out:
            self.out_dma_keys.add(semkey)
        return o

    def dma_multi(self, fns, semkey, reads=(), writes=(), q="sp", is_out=False):
        lst = self.ops[q]
        deps = self._deps(reads, writes)
        first = True
        for fn in fns:
            o = _Op(fn, len(lst), dma=semkey)
            if first:
                self._add_waits(q, o, deps)
                first = False
            lst.append(o)
            self.dsem_cnt[semkey] = self.dsem_cnt.get(semkey, 0) + 16
        self._commit(("d", semkey, self.dsem_cnt[semkey]), reads, writes)
        if is_out:
            self.out_dma_keys.add(semkey)

    def barrier(self):
        last = {}
        for e in self.ENGS:
            for o in reversed(self.ops[e]):
                if o.dma is None:
                    last[e] = o.idx
                    break
        dtoks = [("d", k, v) for k, v in self.dsem_cnt.items()]
        for e in self.ENGS:
            o = _Op(lambda h: h.nop(), len(self.ops[e]))
            deps = [("e", k, v) for k, v in last.items() if k != e] + dtoks
            self._add_waits(e, o, deps)
            self.ops[e].append(o)

    def emit(self, final=False):
        nc = self.nc
        self.barrier()
        if not hasattr(self, "esem"):
            self.esem = {e: self.gstack.enter_context(nc.semaphore("es_" + e)) for e in self.ENGS}
            self.dsem = {}
            self.mbase = {e: 0 for e in self.ENGS}
        for k in self.dsem_cnt:
            if k not in self.dsem:
                self.dsem[k] = self.gstack.enter_context(nc.semaphore("ds_%d" % len(self.dsem)))
        esem, dsem = self.esem, self.dsem
        fin = [(k, self.dsem_cnt[k]) for k in self.out_dma_keys] if final else []
        mcount = {}
        for e in self.ENGS:
            n = self.mbase[e]
            m = {}
            for o in self.ops[e]:
                if o.milestone:
                    assert o.dma is None
                    n += 1
                    m[o.idx] = n
            mcount[e] = m
            self.mbase[e] = n
        ops = self.ops

        def run(e, h):
            for o in ops[e]:
                for (k, v) in o.waits:
                    h.wait_ge(esem[k], mcount[k][v])
                for (k, v) in o.dwaits:
                    h.wait_ge(dsem[k], v)
                ins = o.fn(h)
                if o.dma is not None:
                    ins.then_inc(dsem[o.dma], 16)
                elif o.milestone:
                    ins.then_inc(esem[e], 1)
            if e == "sp":
                for (k, v) in fin:
                    h.wait_ge(dsem[k], v)

        with nc.Block() as block:
            @block.tensor
            def _(h):
                run("pe", h)

            @block.scalar
            def _(h):
                run("act", h)

            @block.vector
            def _(h):
                run("dve", h)

            @block.gpsimd
            def _(h):
                run("pool", h)

            @block.sync
            def _(h):
                run("sp", h)
        self.ops = {e: [] for e in self.ENGS}
        self.seen = {e: {} for e in self.ENGS}
        self.last_w = {}
        self.readers = {}
        self.nstage = getattr(self, "nstage", 0) + 1

PI = math.pi
ALPHA = 2.0 ** 0.25
LN_EPS = 1e-5
GK = 2.0 * math.sqrt(2.0 / PI)


def build(S, NSEQ, NS, NPG, NPHYS):
    nc = bass.Bass("TRN2", target_bir_lowering=False)
    NT = S // 512
    NB = S // 128
    NCOL = NSEQ + NS

    def din(name, shape, dt=F32):
        return nc.dram_tensor(name, list(shape), dt, kind="ExternalInput").ap()

    def dout(name, shape, dt=F32):
        return nc.dram_tensor(name, list(shape), dt, kind="ExternalOutput").ap()

    xT = din("xT", [NSEQ, 128, 8, S]); xtok = din("xtok", [NSEQ, S, 1024])
    cT = din("cT", [128, 8, NCOL]); crep = din("crep", [NSEQ, 128, 8, 128])
    xsT = din("xsT", [128, 8, NS]); xstok = din("xstok", [NS, 1024])
    w_cond = din("w_cond", [1024, 3072]); bcT = din("bcT", [128, 24]); bgrow = din("bgrow", [128, 1024])
    w_in = din("w_in", [1024, 8192]); w_glu = din("w_glu", [1024, 1024]); w_ao = din("w_ao", [1024, 1024])
    w_so = din("w_so", [1024, 1024]); w_out = din("w_out", [1024, 1024])
    bgluT = din("bgluT", [128, 8]); dT_d = din("dT", [128, 8]); sbb_d = din("sbb", [128, 16])
    are_d = din("a_re", [128, 32]); aim_d = din("a_im", [128, 32]); ldt_d = din("ldt", [128, 32])
    BTr_d = din("BTr", [128, 32, 128]); BTi_d = din("BTi", [128, 32, 128])
    CTr_d = din("CTr", [128, 32, 128]); CTi_d = din("CTi", [128, 32, 128])
    lng_d = din("lng", [128, 1024]); lnb_d = din("lnb", [128, 1024])
    Ubf_d = din("Ubf", [128, 128], BF16); Lcbf_d = din("Lcbf", [128, 128], BF16)
    mask01_d = din("mask01", [128, 128]); tvec_d = din("tvec", [128, 128]); iotap_d = din("iotap", [128, 1])
    U32_d = din("U32", [128, 128]); ones32_d = din("ones32", [128, 128])
    bmask_d = din("bmask", [16, 1024])
    ck = din("cache_k", [NPHYS * 128, 1024]); cv = din("cache_v", [NPHYS * 128, 1024])
    pt_d = din("pt", [1, NS * NPG], I32)
    sstr_d = din("sst_re", [128, 32, NS]); ssti_d = din("sst_im", [128, 32, NS])

    y_o = dout("y", [NSEQ, S, 1024]); kp_o = dout("kp", [NSEQ, S, 1024]); vp_o = dout("vp", [NSEQ, S, 1024])
    sre_o = dout("sre", [NSEQ, 128, 32]); sim_o = dout("sim", [NSEQ, 128, 32])
    ys_o = dout("ys", [NS, 1024]); ks_o = dout("ks", [NS, 1024]); vs_o = dout("vs", [NS, 1024])
    ssre_o = dout("ssre", [128, 32, NS]); ssim_o = dout("ssim", [128, 32, NS])

    with ExitStack() as gst:
        p = Prog(nc, gst)
        OP = lambda eng, fn, r=(), w=(): p.op(eng, fn, reads=r, writes=w)
        ps = [p.ps("psb%d" % i, [128, 512]) for i in range(8)]
        A = [p.sb("A%d" % i, [128, 8, S], BF16) for i in range(3)]
        stg = [p.sb("stg%d" % i, [128, 8, 128], F32) for i in range(2)]
        wbf = [p.sb("wbf%d" % i, [128, 8, 1024], BF16) for i in range(2)]
        Ubf = p.sb("Ubf", [128, 128], BF16); Lcbf = p.sb("Lcbf", [128, 128], BF16)
        zbf = p.sb("zbf", [128, 128], BF16)
        mask01 = p.sb("mask01", [128, 128], F32); tvec = p.sb("tvec", [128, 128], F32)
        modT = [p.sb("modT%d" % i, [128, 8, NCOL], F32) for i in range(2)]
        grow = p.sb("grow", [128, NSEQ, 1024], F32)
        gate_s = p.sb("gate_s", [128, 1024], F32)
        bcTs = p.sb("bcTs", [128, 24], F32); bgluTs = p.sb("bgluTs", [128, 8], F32)
        dTs = p.sb("dTs", [128, 8], F32); sbb = p.sb("sbbs", [128, 16], F32)
        mag = p.sb("mag", [128, 32], F32); ang = p.sb("ang", [128, 32], F32)
        abr = p.sb("abr", [128, 32], F32); abi = p.sb("abi", [128, 32], F32)
        cor = p.sb("cor", [128, 32], F32); coi = p.sb("coi", [128, 32], F32)
        rhor = p.sb("rhor", [128, 32], F32); rhoi = p.sb("rhoi", [128, 32], F32)
        xendr = p.sb("xendr", [128, 32], F32); xendi = p.sb("xendi", [128, 32], F32)
        psrr = [0]

        def bank(n=4, base=0):
            b = base + psrr[0] % n
            psrr[0] += 1
            return b

        def sin_of(out, arg, shift, tmp, ra, rt, ro, itile=None, t2=None, r2=None):
            r2 = r2 or (rt + "_2")
            OP("dve", lambda h: h.tensor_scalar(tmp, arg, shift, 1.0 / (2.0 * PI), ALU.add, ALU.mult), [ra], [rt])
            OP("dve", lambda h: h.tensor_copy(itile, tmp), [rt], [rt + "_i"])
            OP("dve", lambda h: h.tensor_copy(tmp, itile), [rt + "_i"], [rt])
            OP("dve", lambda h: h.tensor_scalar(t2, arg, shift, None, ALU.add), [ra], [r2])
            OP("dve", lambda h: h.scalar_tensor_tensor(tmp, tmp, -2.0 * PI, t2, ALU.mult, ALU.add), [rt, r2], [rt])
            OP("dve", lambda h: h.tensor_scalar(t2, tmp, PI, -2.0 * PI, ALU.is_gt, ALU.mult), [rt], [r2])
            OP("dve", lambda h: h.tensor_tensor(tmp, tmp, t2, ALU.add), [rt, r2], [rt])
            OP("act", lambda h: h.activation(out, tmp, AF.Sin), [rt], [ro])

        def load_w(src, slot):
            for q in range(8):
                sl = q % 2
                p.dma(lambda h, q=q, sl=sl: h.dma_start(out=stg[sl][:], in_=src[:, q * 128:(q + 1) * 128].rearrange("(kc p) n -> p kc n", p=128)),
                      "stg%d" % sl, writes=[("stg", sl)])
                OP("pool", lambda h, q=q, sl=sl: h.tensor_copy(wbf[slot][:, :, q * 128:(q + 1) * 128], stg[sl][:]),
                   [("stg", sl)], [("wbf", slot, q)])

        def proj_fm(slots, ins, tiles, epi):
            for ti, (t0, tw) in enumerate(tiles):
                for c in range(8):
                    bs = []
                    for j, (slot, (ai, ab)) in enumerate(zip(slots, ins)):
                        b = bank(6)
                        bs.append(b)
                        for kc in range(8):
                            OP("pe", lambda h, b=b, slot=slot, ab=ab, kc=kc, c=c, t0=t0, tw=tw: h.matmul(
                                ps[b][:, 0:tw], lhsT=wbf[slot][:, kc, c * 128:(c + 1) * 128], rhs=ab[:, kc, t0:t0 + tw],
                                start=(kc == 0), stop=(kc == 7)),
                               [("wbf", slot, c), ("A", ai, kc, ti)], [("ps", b)])
                    epi(c, ti, t0, tw, bs)

        with ExitStack() as st:
            p.stack = st
            cTa = p.sb("cTa", [128, 8, NCOL], F32)
            creps = p.sb("creps", [128, NSEQ, 8, 128], F32)
            are = p.sb("are", [128, 32], F32); aim = p.sb("aim", [128, 32], F32); ldt = p.sb("ldt", [128, 32], F32)
            wst = [p.sb("wst%d" % i, [128, 8, 512], F32) for i in range(2)]
            t32 = [p.sb("t32_%d" % i, [128, 32], F32) for i in range(6)]
            i32t = p.sb("i32t", [128, 32], I32); t2s = p.sb("t2s", [128, 32], F32)
            bgr = p.sb("bgr", [128, 1024], F32)
            loads = [(Ubf[:], Ubf_d), (Lcbf[:], Lcbf_d), (mask01[:], mask01_d), (tvec[:], tvec_d),
                     (bcTs[:], bcT), (bgluTs[:], bgluT), (dTs[:], dT_d), (sbb[:], sbb_d),
                     (cTa[:], cT), (are[:], are_d), (aim[:], aim_d),
                     (ldt[:], ldt_d), (bgr[:], bgrow)]
            for s_ in range(NSEQ):
                loads.append((creps[:, s_, :, :], crep[s_]))
            p.dma_multi([(lambda h, o=o, i=i: h.dma_start(out=o, in_=i)) for o, i in loads], "setup", writes=["setup"])
            OP("dve", lambda h: h.memset(zbf[:], 0.0), [], ["zbf"])
            for part in range(3):
                for half in range(2):
                    sl = (part * 2 + half) % 2
                    col0 = part * 1024 + half * 512
                    p.dma(lambda h, sl=sl, col0=col0: h.dma_start(out=wst[sl][:], in_=w_cond[:, col0:col0 + 512].rearrange("(kc p) n -> p kc n", p=128)),
                          "wst%d" % sl, writes=[("wst", sl)])
                    if part < 2:
                        for fcl in range(4):
                            fc = half * 4 + fcl
                            b = bank()
                            for kc in range(8):
                                OP("pe", lambda h, b=b, sl=sl, kc=kc, fcl=fcl: h.matmul(ps[b][:, 0:NCOL], lhsT=wst[sl][:, kc, fcl * 128:(fcl + 1) * 128],
                                                                                      rhs=cTa[:, kc, :], start=(kc == 0), stop=(kc == 7)),
                                   [("wst", sl), "setup"], [("ps", b)])
                            OP("dve", lambda h, b=b, part=part, fc=fc: h.tensor_scalar(modT[part][:, fc, :], ps[b][:, 0:NCOL], bcTs[:, part * 8 + fc:part * 8 + fc + 1],
                                                                                  1.0 if part == 1 else 0.0, ALU.add, ALU.add),
                               [("ps", b), "setup"], [("modT", part)])
                    else:
                        for s_ in range(NSEQ):
                            b = bank()
                            for kc in range(8):
                                OP("pe", lambda h, b=b, sl=sl, kc=kc, s_=s_: h.matmul(ps[b][:, :], lhsT=creps[:, s_, kc, :], rhs=wst[sl][:, kc, :],
                                                                                    start=(kc == 0), stop=(kc == 7)),
                                   [("wst", sl), "setup"], [("ps", b)])
                            OP("dve", lambda h, b=b, s_=s_, half=half: h.tensor_tensor(grow[:, s_, half * 512:(half + 1) * 512], ps[b][:, :], bgr[:, half * 512:(half + 1) * 512], ALU.add),
                               [("ps", b), "setup"], [("grow", s_, half)])
                        b = bank()
                        for kc in range(8):
                            OP("pe", lambda h, b=b, sl=sl, kc=kc: h.matmul(ps[b][0:NS, :], lhsT=cTa[:, kc, NSEQ:NCOL], rhs=wst[sl][:, kc, :],
                                                                         start=(kc == 0), stop=(kc == 7)),
                               [("wst", sl), "setup"], [("ps", b)])
                        OP("dve", lambda h, b=b, half=half: h.tensor_tensor(gate_s[0:NS, half * 512:(half + 1) * 512], ps[b][0:NS, :], bgr[0:NS, half * 512:(half + 1) * 512], ALU.add),
                           [("ps", b), "setup"], [("gate_s", half)])
            dt_, lm, c_, s_t, tmp, den = t32
            OP("act", lambda h: h.activation(dt_[:], ldt[:], AF.Exp), ["setup"], ["dt"])
            OP("dve", lambda h: h.tensor_tensor(lm[:], are[:], dt_[:], ALU.mult), ["dt", "setup"], ["lm"])
            OP("act", lambda h: h.activation(mag[:], lm[:], AF.Exp), ["lm"], ["mag"])
            OP("dve", lambda h: h.tensor_tensor(ang[:], aim[:], dt_[:], ALU.mult), ["dt", "setup"], ["ang"])
            sin_of(s_t[:], ang[:], 0.0, tmp[:], "ang", "tmp", "s_t", i32t[:], t2s[:])
            OP("dve", lambda h: h.tensor_tensor(abi[:], mag[:], s_t[:], ALU.mult), ["mag", "s_t"], ["abi"])
            sin_of(c_[:], ang[:], PI / 2, tmp[:], "ang", "tmp", "c_", i32t[:], t2s[:])
            OP("dve", lambda h: h.tensor_tensor(abr[:], mag[:], c_[:], ALU.mult), ["mag", "c_"], ["abr"])
            OP("dve", lambda h: h.tensor_tensor(den[:], are[:], are[:], ALU.mult), ["setup"], ["den"])
            OP("dve", lambda h: h.tensor_tensor(tmp[:], aim[:], aim[:], ALU.mult), ["setup"], ["tmp"])
            OP("dve", lambda h: h.tensor_tensor(den[:], den[:], tmp[:], ALU.add), ["den", "tmp"], ["den"])
            OP("dve", lambda h: h.reciprocal(den[:], den[:]), ["den"], ["den"])
            OP("dve", lambda h: h.tensor_scalar_add(c_[:], abr[:], -1.0), ["abr"], ["c_"])
            OP("dve", lambda h: h.tensor_tensor(tmp[:], c_[:], are[:], ALU.mult), ["c_"], ["tmp"])
            OP("dve", lambda h: h.tensor_tensor(s_t[:], abi[:], aim[:], ALU.mult), ["abi"], ["s_t"])
            OP("dve", lambda h: h.tensor_tensor(tmp[:], tmp[:], s_t[:], ALU.add), ["tmp", "s_t"], ["tmp"])
            OP("dve", lambda h: h.tensor_tensor(cor[:], tmp[:], den[:], ALU.mult), ["tmp", "den"], ["cor"])
            OP("dve", lambda h: h.tensor_tensor(tmp[:], abi[:], are[:], ALU.mult), ["abi"], ["tmp"])
            OP("dve", lambda h: h.tensor_tensor(s_t[:], c_[:], aim[:], ALU.mult), ["c_"], ["s_t"])
            OP("dve", lambda h: h.tensor_tensor(tmp[:], tmp[:], s_t[:], ALU.subtract), ["tmp", "s_t"], ["tmp"])
            OP("dve", lambda h: h.tensor_tensor(coi[:], tmp[:], den[:], ALU.mult), ["tmp", "den"], ["coi"])
            OP("dve", lambda h: h.tensor_scalar_mul(lm[:], ang[:], 128.0), ["ang"], ["lm"])
            sin_of(rhoi[:], lm[:], 0.0, tmp[:], "lm", "tmp", "rhoi", i32t[:], t2s[:])
            sin_of(rhor[:], lm[:], PI / 2, tmp[:], "lm", "tmp", "rhor", i32t[:], t2s[:])
            p.emit()

        def run_pass(seq):
            sample = seq is None
            T = NS if sample else S
            tiles = [(0, NS)] if sample else [(i * 512, 512) for i in range(NT)]
            col = NSEQ if sample else seq

            def fm_names(ai, ti):
                return [("A", ai, c, ti) for c in range(8)]

            with ExitStack() as st:
                p.stack = st
                if sample:
                    xs_ = p.sb("xs_", [128, 8, NS], F32)
                    p.dma(lambda h: h.dma_start(out=xs_[:], in_=xsT), "xst0", writes=["xs_"])
                    OP("dve", lambda h: h.tensor_tensor(xs_[:], xs_[:], modT[1][:, :, NSEQ:NCOL], ALU.mult), ["xs_"], ["xs_"])
                    OP("dve", lambda h: h.tensor_tensor(A[0][:, :, 0:NS], xs_[:], modT[0][:, :, NSEQ:NCOL], ALU.add), ["xs_"], fm_names(0, 0))
                else:
                    xst = [p.sb("xst%d" % i, [128, S], F32) for i in range(2)]
                    for c in range(8):
                        sl = c % 2
                        p.dma(lambda h, c=c, sl=sl: h.dma_start(out=xst[sl][:], in_=xT[seq, :, c, :]), "xst%d" % sl, writes=[("xst", sl)])
                        OP("dve", lambda h, c=c, sl=sl: h.tensor_scalar(A[0][:, c, :], xst[sl][:], modT[1][:, c, col:col + 1], modT[0][:, c, col:col + 1], ALU.mult, ALU.add),
                           [("xst", sl)], [("A", 0, c, ti) for ti in range(NT)])
                p.emit()

            with ExitStack() as st:
                p.stack = st
                load_w(w_in[:, 4096:5120], 0)

                def epi(c, ti, t0, tw, bs):
                    OP("dve", lambda h: h.tensor_copy(A[1][:, c, t0:t0 + tw], ps[bs[0]][:, 0:tw]), [("ps", bs[0])], [("A", 1, c, ti)])
                proj_fm([0], [(0, A[0])], tiles, epi)
                p.emit()

            with ExitStack() as st:
                p.stack = st
                W = 4 * 128
                ctr = p.sb("ctr", [128, 4, 128], F32); cti = p.sb("cti", [128, 4, 128], F32)
                btr = p.sb("btr", [128, 4, 128], F32); bti = p.sb("bti", [128, 4, 128], F32)
                Bre = p.sb("Bre", [128, 4, 128], BF16); Bim = p.sb("Bim", [128, 4, 128], BF16)
                Cr = p.sb("Cr", [128, 4, 128], BF16); nCr = p.sb("nCr", [128, 4, 128], BF16); nCi = p.sb("nCi", [128, 4, 128], BF16)
                tA = p.sb("tA", [128, 4, 128], F32); tB = p.sb("tB", [128, 4, 128], F32)
                tI = p.sb("tI", [128, 4, 128], I32); tC = p.sb("tC", [128, 4, 128], F32)
                ybuf = p.sb("ybuf", [128, 512], F32); y2b = p.sb("y2b", [128, 512], F32); y3b = p.sb("y3b", [128, 512], F32)
                if not sample:
                    cosT = p.sb("cosT", [128, 4, 128], F32); sinT = p.sb("sinT", [128, 4, 128], F32)
                    Str = p.sb("Str", [128, 4, 128], F32); Sti = p.sb("Sti", [128, 4, 128], F32)
                    rbr = [p.sb("rbr%d" % i, [128, 4, 128], F32) for i in range(2)]; rbi = [p.sb("rbi%d" % i, [128, 4, 128], F32) for i in range(2)]
                    tD = p.sb("tD", [128, 4, 128], F32)
                    P = [p.sb("P%d" % i, [128, 4, 128], BF16) for i in range(4)]
                    inr = p.sb("inr", [128, 4], F32); ini = p.sb("ini", [128, 4], F32)
                    sm = [p.sb("sm%d" % i, [128, 4], F32) for i in range(4)]
                else:
                    xr = p.sb("xr", [128, 32, NS], F32); xi = p.sb("xi", [128, 32, NS], F32)
                    xnr = p.sb("xnr", [128, 32, NS], F32); xni = p.sb("xni", [128, 32, NS], F32)
                    bur = p.sb("bur", [128, 4, NS], F32); bui = p.sb("bui", [128, 4, NS], F32)
                    Xrb = p.sb("Xrb", [128, 4, NS], BF16); Xib = p.sb("Xib", [128, 4, NS], BF16)
                    t4a = p.sb("t4a", [128, 4, NS], F32); t4b = p.sb("t4b", [128, 4, NS], F32)
                    p.dma(lambda h: h.dma_start(out=xr[:], in_=sstr_d), "sst0", writes=["xr"])
                    p.dma(lambda h: h.dma_start(out=xi[:], in_=ssti_d), "sst1", writes=["xi"])

                def gelu_store(F, t0, tw):
                    OP("dve", lambda h: h.tensor_tensor(y2b[:, 0:tw], ybuf[:, 0:tw], ybuf[:, 0:tw], ALU.mult), ["ybuf"], ["y2b"])
                    OP("dve", lambda h: h.tensor_scalar(y2b[:, 0:tw], y2b[:, 0:tw], 0.044715, 1.0, ALU.mult, ALU.add), ["y2b"], ["y2b"])
                    OP("dve", lambda h: h.tensor_tensor(y2b[:, 0:tw], y2b[:, 0:tw], ybuf[:, 0:tw], ALU.mult), ["y2b", "ybuf"], ["y2b"])
                    OP("act", lambda h: h.activation(y3b[:, 0:tw], y2b[:, 0:tw], AF.Sigmoid, scale=GK), ["y2b"], ["y3b"])
                    OP("dve", lambda h: h.tensor_tensor(A[2][:, F, t0:t0 + tw], ybuf[:, 0:tw], y3b[:, 0:tw], ALU.mult), ["y3b", "ybuf"],
                       [("A", 2, F, t0 // 512)])

                for F in range(8):
                    G0 = 4 * F
                    cob_r = cor[:, G0:G0 + 4].unsqueeze(2).broadcast_to([128, 4, 128])
                    cob_i = coi[:, G0:G0 + 4].unsqueeze(2).broadcast_to([128, 4, 128])
                    p.dma_multi([lambda h: h.dma_start(out=ctr[:], in_=CTr_d[:, G0:G0 + 4, :]),
                                 lambda h: h.dma_start(out=cti[:], in_=CTi_d[:, G0:G0 + 4, :]),
                                 lambda h: h.dma_start(out=btr[:], in_=BTr_d[:, G0:G0 + 4, :]),
                                 lambda h: h.dma_start(out=bti[:], in_=BTi_d[:, G0:G0 + 4, :])], "ssmtab", writes=["tabs"])
                    OP("pool", lambda h: h.tensor_copy(Bre[:], btr[:]), ["tabs"], ["Bre"])
                    OP("pool", lambda h: h.tensor_copy(Bim[:], bti[:]), ["tabs"], ["Bim"])
                    if not sample:
                        OP("dve", lambda h: h.tensor_tensor(tA[:], ctr[:], cob_r, ALU.mult), ["tabs"], ["tA"])
                        OP("dve", lambda h: h.tensor_tensor(tB[:], cti[:], cob_i, ALU.mult), ["tabs"], ["tB"])
                        OP("dve", lambda h: h.tensor_tensor(Cr[:], tA[:], tB[:], ALU.subtract), ["tA", "tB"], ["Cr"])
                        OP("dve", lambda h: h.tensor_tensor(nCr[:], tB[:], tA[:], ALU.subtract), ["tA", "tB"], ["nCr"])
                        OP("dve", lambda h: h.tensor_tensor(tA[:], ctr[:], cob_i, ALU.mult), ["tabs", "Cr", "nCr"], ["tA"])
                        OP("dve", lambda h: h.tensor_tensor(tB[:], cti[:], cob_r, ALU.mult), ["tabs", "Cr", "nCr"], ["tB"])
                        OP("dve", lambda h: h.tensor_tensor(tA[:], tA[:], tB[:], ALU.add), ["tA", "tB"], ["tA"])
                        OP("dve", lambda h: h.tensor_scalar_mul(nCi[:], tA[:], -1.0), ["tA"], ["nCi"])
                        OP("dve", lambda h: h.tensor_tensor(tA[:], ang[:, G0:G0 + 4].unsqueeze(2).broadcast_to([128, 4, 128]),
                                                            tvec[:, :].unsqueeze(1).broadcast_to([128, 4, 128]), ALU.mult), ["nCi"], ["tA"])
                        sin_of(sinT[:], tA[:], 0.0, tB[:], "tA", "tB", "sinT", tI[:], tC[:], "tC")
                        sin_of(cosT[:], tA[:], PI / 2, tB[:], "tA", "tB", "cosT", tI[:], tC[:], "tC")
                        OP("dve", lambda h: h.memset(inr[:], 0.0), [], ["inr"])
                        OP("dve", lambda h: h.memset(ini[:], 0.0), [], ["ini"])
                        c2 = cosT[:].rearrange("p a b -> p (a b)"); s2 = sinT[:].rearrange("p a b -> p (a b)")
                        scr = [t_[:].rearrange("p a b -> p (a b)") for t_ in (tA, tB, tC, tD)]
                        bR, bI = 6, 7

                        def stA(n):
                            t0 = n * 128
                            ti = t0 // 512
                            k = n % 2
                            for gl in range(4):
                                OP("pe", lambda h, gl=gl: h.matmul(ps[bR][:, gl * 128:(gl + 1) * 128], lhsT=Bre[:, gl, :], rhs=A[1][:, F, t0:t0 + 128], start=True, stop=True),
                                   ["Bre", ("A", 1, F, ti)], [("ps", bR)])
                                OP("pe", lambda h, gl=gl: h.matmul(ps[bI][:, gl * 128:(gl + 1) * 128], lhsT=Bim[:, gl, :], rhs=A[1][:, F, t0:t0 + 128], start=True, stop=True),
                                   ["Bim", ("A", 1, F, ti)], [("ps", bI)])
                            OP("dve", lambda h: h.tensor_tensor(scr[0], ps[bR][:, :], c2, ALU.mult), [("ps", bR), "cosT", "sinT"], ["tA"])
                            OP("dve", lambda h: h.tensor_tensor(scr[1], ps[bI][:, :], s2, ALU.mult), [("ps", bI), "cosT", "sinT"], ["tB"])
                            OP("dve", lambda h: h.tensor_tensor(scr[2], ps[bI][:, :], c2, ALU.mult), [("ps", bI), "cosT", "sinT"], ["tC"])
                            OP("dve", lambda h: h.tensor_tensor(scr[3], ps[bR][:, :], s2, ALU.mult), [("ps", bR), "cosT", "sinT"], ["tD"])
                            OP("pool", lambda h: h.tensor_tensor(rbr[k][:].rearrange("p a b -> p (a b)"), scr[0], scr[1], ALU.add), ["tA", "tB"], [("rbr", k)])
                            OP("pool", lambda h: h.tensor_tensor(rbi[k][:].rearrange("p a b -> p (a b)"), scr[2], scr[3], ALU.subtract), ["tC", "tD"], [("rbi", k)])

                        def stB(n):
                            t0 = n * 128
                            ti = t0 // 512
                            k = n % 2
                            bY = 4 + n % 2
                            if n > 0:
                                lr_ = Str[:, :, 127]; li_ = Sti[:, :, 127]
                                rr = rhor[:, G0:G0 + 4]; ri = rhoi[:, G0:G0 + 4]
                                OP("dve", lambda h: h.tensor_tensor(sm[0][:], rr, lr_, ALU.mult), ["Str"], ["sm0"])
                                OP("dve", lambda h: h.tensor_tensor(sm[1][:], ri, li_, ALU.mult), ["Sti"], ["sm1"])
                                OP("dve", lambda h: h.tensor_tensor(sm[2][:], rr, li_, ALU.mult), ["Sti"], ["sm2"])
                                OP("dve", lambda h: h.tensor_tensor(sm[3][:], ri, lr_, ALU.mult), ["Str"], ["sm3"])
                                OP("dve", lambda h: h.tensor_tensor(inr[:], sm[0][:], sm[1][:], ALU.subtract), ["sm0", "sm1"], ["inr"])
                                OP("dve", lambda h: h.tensor_tensor(ini[:], sm[2][:], sm[3][:], ALU.add), ["sm2", "sm3"], ["ini"])
                            for gl in range(4):
                                G = G0 + gl
                                OP("dve", lambda h, gl=gl, G=G: h.tensor_tensor_scan(Str[:, gl, :], mag[:, G:G + 1].to_broadcast([128, 128]), rbr[k][:, gl, :],
                                                                                      inr[:, gl:gl + 1], ALU.mult, ALU.add), [("rbr", k), "inr"], ["Str"])
                                OP("dve", lambda h, gl=gl, G=G: h.tensor_tensor_scan(Sti[:, gl, :], mag[:, G:G + 1].to_broadcast([128, 128]), rbi[k][:, gl, :],
                                                                                      ini[:, gl:gl + 1], ALU.mult, ALU.add), [("rbi", k), "ini"], ["Sti"])
                            OP("pool", lambda h: h.tensor_tensor(P[0][:], cosT[:], Str[:], ALU.mult), ["Str", "cosT", "sinT"], ["P"])
                            OP("pool", lambda h: h.tensor_tensor(P[1][:], sinT[:], Sti[:], ALU.mult), ["Sti", "cosT", "sinT"], ["P"])
                            OP("pool", lambda h: h.tensor_tensor(P[2][:], cosT[:], Sti[:], ALU.mult), ["Sti", "cosT", "sinT"], ["P"])
                            OP("pool", lambda h: h.tensor_tensor(P[3][:], sinT[:], Str[:], ALU.mult), ["Str", "cosT", "sinT"], ["P"])
                            kq = 0
                            for gl in range(4):
                                for (Wt, Pi) in ((Cr, 0), (nCr, 1), (nCi, 2), (nCi, 3)):
                                    OP("pe", lambda h, gl=gl, Wt=Wt, Pi=Pi, kq=kq: h.matmul(ps[bY][:, 0:128], lhsT=Wt[:, gl, :], rhs=P[Pi][:, gl, :], start=(kq == 0), stop=(kq == 15)),
                                       ["P", "Cr", "nCr", "nCi"], [("ps", bY)])
                                    kq += 1
                            o0 = (n % 4) * 128
                            OP("dve", lambda h: h.scalar_tensor_tensor(ybuf[:, o0:o0 + 128], A[1][:, F, t0:t0 + 128], dTs[:, F:F + 1], ps[bY][:, 0:128], ALU.mult, ALU.add),
                               [("ps", bY), ("A", 1, F, ti)], ["ybuf"])
                            if n % 4 == 3:
                                gelu_store(F, ti * 512, 512)

                        stA(0)
                        for n in range(NB):
                            if n + 1 < NB:
                                stA(n + 1)
                            stB(n)
                        lr_ = Str[:, :, 127]; li_ = Sti[:, :, 127]
                        c1 = cosT[:, :, 127]; s1 = sinT[:, :, 127]
                        OP("dve", lambda h: h.tensor_tensor(sm[0][:], c1, lr_, ALU.mult), ["Str", "cosT", "sinT"], ["sm0"])
                        OP("dve", lambda h: h.tensor_tensor(sm[1][:], s1, li_, ALU.mult), ["Sti", "cosT", "sinT"], ["sm1"])
                        OP("dve", lambda h: h.tensor_tensor(sm[0][:], sm[0][:], sm[1][:], ALU.subtract), ["sm0", "sm1"], ["sm0"])
                        OP("dve", lambda h: h.tensor_tensor(sm[2][:], c1, li_, ALU.mult), ["Sti", "cosT", "sinT"], ["sm2"])
                        OP("dve", lambda h: h.tensor_tensor(sm[3][:], s1, lr_, ALU.mult), ["Str", "cosT", "sinT"], ["sm3"])
                        OP("dve", lambda h: h.tensor_tensor(sm[2][:], sm[2][:], sm[3][:], ALU.add), ["sm2", "sm3"], ["sm2"])
                        cr4 = cor[:, G0:G0 + 4]; ci4 = coi[:, G0:G0 + 4]
                        OP("dve", lambda h: h.tensor_tensor(sm[1][:], cr4, sm[0][:], ALU.mult), ["sm0"], ["sm1"])
                        OP("dve", lambda h: h.tensor_tensor(sm[3][:], ci4, sm[2][:], ALU.mult), ["sm2"], ["sm3"])
                        OP("dve", lambda h: h.tensor_tensor(xendr[:, G0:G0 + 4], sm[1][:], sm[3][:], ALU.subtract), ["sm1", "sm3"], ["xendr"])
                        OP("dve", lambda h: h.tensor_tensor(sm[1][:], cr4, sm[2][:], ALU.mult), ["sm2", "xendr"], ["sm1"])
                        OP("dve", lambda h: h.tensor_tensor(sm[3][:], ci4, sm[0][:], ALU.mult), ["sm0", "xendr"], ["sm3"])
                        OP("dve", lambda h: h.tensor_tensor(xendi[:, G0:G0 + 4], sm[1][:], sm[3][:], ALU.add), ["sm1", "sm3"], ["xendi"])
                    else:
                        OP("pool", lambda h: h.tensor_copy(Cr[:], ctr[:]), ["tabs"], ["Cr"])
                        OP("dve", lambda h: h.tensor_scalar_mul(nCi[:], cti[:], -1.0), ["tabs"], ["nCi"])
                        bR, bI, bY = 6, 7, bank(2, 4)
                        for gl in range(4):
                            OP("pe", lambda h, gl=gl: h.matmul(ps[bR][:, gl * NS:(gl + 1) * NS], lhsT=Bre[:, gl, :], rhs=A[1][:, F, 0:NS], start=True, stop=True),
                               ["Bre", ("A", 1, F, 0)], [("ps", bR)])
                            OP("pe", lambda h, gl=gl: h.matmul(ps[bI][:, gl * NS:(gl + 1) * NS], lhsT=Bim[:, gl, :], rhs=A[1][:, F, 0:NS], start=True, stop=True),
                               ["Bim", ("A", 1, F, 0)], [("ps", bI)])
                        pr = ps[bR][:, 0:4 * NS].rearrange("p (a b) -> p a b", a=4); pi_ = ps[bI][:, 0:4 * NS].rearrange("p (a b) -> p a b", a=4)
                        cbr = cor[:, G0:G0 + 4].unsqueeze(2).broadcast_to([128, 4, NS]); cbi = coi[:, G0:G0 + 4].unsqueeze(2).broadcast_to([128, 4, NS])
                        abr4 = abr[:, G0:G0 + 4].unsqueeze(2).broadcast_to([128, 4, NS]); abi4 = abi[:, G0:G0 + 4].unsqueeze(2).broadcast_to([128, 4, NS])
                        xor_ = xr[:, G0:G0 + 4, :]; xoi_ = xi[:, G0:G0 + 4, :]
                        xnr_ = xnr[:, G0:G0 + 4, :]; xni_ = xni[:, G0:G0 + 4, :]
                        OP("dve", lambda h: h.tensor_tensor(t4a[:], pr, cbr, ALU.mult), [("ps", bR)], ["t4a"])
                        OP("dve", lambda h: h.tensor_tensor(t4b[:], pi_, cbi, ALU.mult), [("ps", bI)], ["t4b"])
                        OP("dve", lambda h: h.tensor_tensor(bur[:], t4a[:], t4b[:], ALU.subtract), ["t4a", "t4b"], ["bur"])
                        OP("dve", lambda h: h.tensor_tensor(t4a[:], pi_, cbr, ALU.mult), [("ps", bI), "bur"], ["t4a"])
                        OP("dve", lambda h: h.tensor_tensor(t4b[:], pr, cbi, ALU.mult), [("ps", bR), "bur"], ["t4b"])
                        OP("dve", lambda h: h.tensor_tensor(bui[:], t4a[:], t4b[:], ALU.add), ["t4a", "t4b"], ["bui"])
                        OP("dve", lambda h: h.tensor_tensor(t4a[:], xor_, abr4, ALU.mult), ["xr", "bui"], ["t4a"])
                        OP("dve", lambda h: h.tensor_tensor(t4b[:], xoi_, abi4, ALU.mult), ["xi", "bui"], ["t4b"])
                        OP("dve", lambda h: h.tensor_tensor(t4a[:], t4a[:], t4b[:], ALU.subtract), ["t4a", "t4b"], ["t4a"])
                        OP("dve", lambda h: h.tensor_tensor(xnr_, t4a[:], bur[:], ALU.add), ["t4a", "bur"], ["xnr"])
                        OP("dve", lambda h: h.tensor_tensor(t4a[:], xoi_, abr4, ALU.mult), ["xi", "xnr"], ["t4a"])
                        OP("dve", lambda h: h.tensor_tensor(t4b[:], xor_, abi4, ALU.mult), ["xr", "xnr"], ["t4b"])
                        OP("dve", lambda h: h.tensor_tensor(t4a[:], t4a[:], t4b[:], ALU.add), ["t4a", "t4b"], ["t4a"])
                        OP("dve", lambda h: h.tensor_tensor(xni_, t4a[:], bui[:], ALU.add), ["t4a", "bui"], ["xni"])
                        OP("pool", lambda h: h.tensor_copy(Xrb[:], xnr_), ["xnr"], ["Xrb"])
                        OP("pool", lambda h: h.tensor_copy(Xib[:], xni_), ["xni"], ["Xib"])
                        for gl in range(4):
                            OP("pe", lambda h, gl=gl: h.matmul(ps[bY][:, 0:NS], lhsT=Cr[:, gl, :], rhs=Xrb[:, gl, :], start=(gl == 0), stop=False),
                               ["Cr", "Xrb"], [("ps", bY)])
                            OP("pe", lambda h, gl=gl: h.matmul(ps[bY][:, 0:NS], lhsT=nCi[:, gl, :], rhs=Xib[:, gl, :], start=False, stop=(gl == 3)),
                               ["nCi", "Xib"], [("ps", bY)])
                        OP("dve", lambda h: h.scalar_tensor_tensor(ybuf[:, 0:NS], A[1][:, F, 0:NS], dTs[:, F:F + 1], ps[bY][:, 0:NS], ALU.mult, ALU.add),
                           [("ps", bY), ("A", 1, F, 0)], ["ybuf"])
                        gelu_store(F, 0, NS)
                if sample:
                    p.dma(lambda h: h.dma_start(out=ssre_o, in_=xnr[:]), "o_ss0", reads=["xnr"], is_out=True)
                    p.dma(lambda h: h.dma_start(out=ssim_o, in_=xni[:]), "o_ss1", reads=["xni"], is_out=True)
                else:
                    p.dma(lambda h: h.dma_start(out=sre_o[seq], in_=xendr[:]), "o_s0", reads=["xendr"], is_out=True)
                    p.dma(lambda h: h.dma_start(out=sim_o[seq], in_=xendi[:]), "o_s1", reads=["xendi"], is_out=True)
                p.emit()

            with ExitStack() as st:
                p.stack = st
                tmp = [p.sb("tmpf%d" % i, [128, 512], F32) for i in range(2)]
                load_w(w_glu, 0)

                def epi(c, ti, t0, tw, bs):
                    k = c % 2
                    OP("act", lambda h: h.activation(tmp[k][:, 0:tw], ps[bs[0]][:, 0:tw], AF.Sigmoid, bias=bgluTs[:, c:c + 1]), [("ps", bs[0])], [("tmp", k)])
                    OP("dve", lambda h: h.tensor_tensor(A[1][:, c, t0:t0 + tw], A[2][:, c, t0:t0 + tw], tmp[k][:, 0:tw], ALU.mult),
                       [("tmp", k), ("A", 2, c, ti)], [("A", 1, c, ti)])
                proj_fm([0], [(2, A[2])], tiles, epi)
                p.emit()

            with ExitStack() as st:
                p.stack = st
                tmp = [p.sb("tmpf%d" % i, [128, 512], F32) for i in range(2)]
                load_w(w_in[:, 5120:6144], 1)

                def epi(c, ti, t0, tw, bs):
                    k = c % 2
                    OP("act", lambda h: h.activation(tmp[k][:, 0:tw], ps[bs[0]][:, 0:tw], AF.Sigmoid), [("ps", bs[0])], [("tmp", k)])
                    OP("dve", lambda h: h.tensor_tensor(tmp[k][:, 0:tw], tmp[k][:, 0:tw], ps[bs[0]][:, 0:tw], ALU.mult), [("ps", bs[0]), ("tmp", k)], [("tmp", k)])
                    OP("dve", lambda h: h.tensor_tensor(A[1][:, c, t0:t0 + tw], A[1][:, c, t0:t0 + tw], tmp[k][:, 0:tw], ALU.mult),
                       [("tmp", k)], [("A", 1, c, ti)])
                proj_fm([1], [(0, A[0])], tiles, epi)
                p.emit()

            with ExitStack() as st:
                p.stack = st
                tmp = [p.sb("tmpf%d" % i, [128, 512], F32) for i in range(2)]
                load_w(w_so, 0)
                load_w(w_in[:, 7168:8192], 1)

                def epi(c, ti, t0, tw, bs):
                    k = c % 2
                    OP("act", lambda h: h.activation(tmp[k][:, 0:tw], ps[bs[1]][:, 0:tw], AF.Sigmoid), [("ps", bs[1])], [("tmp", k)])
                    OP("dve", lambda h: h.tensor_tensor(A[2][:, c, t0:t0 + tw], ps[bs[0]][:, 0:tw], tmp[k][:, 0:tw], ALU.mult),
                       [("tmp", k), ("ps", bs[0])], [("A", 2, c, ti)])
                proj_fm([0, 1], [(1, A[1]), (0, A[0])], tiles, epi)
                p.emit()

            if sample:
                sample_attention()
            else:
                prompt_attention(seq)

            with ExitStack() as st:
                p.stack = st
                tmp = [p.sb("tmpf%d" % i, [128, 512], F32) for i in range(2)]
                load_w(w_in[:, 3072:4096], 0)

                def epi(c, ti, t0, tw, bs):
                    k = c % 2
                    OP("act", lambda h: h.activation(tmp[k][:, 0:tw], ps[bs[0]][:, 0:tw], AF.Sigmoid), [("ps", bs[0])], [("tmp", k)])
                    OP("dve", lambda h: h.tensor_tensor(tmp[k][:, 0:tw], tmp[k][:, 0:tw], ps[bs[0]][:, 0:tw], ALU.mult), [("ps", bs[0]), ("tmp", k)], [("tmp", k)])
                    OP("dve", lambda h: h.tensor_tensor(A[1][:, c, t0:t0 + tw], A[1][:, c, t0:t0 + tw], tmp[k][:, 0:tw], ALU.mult),
                       [("tmp", k)], [("A", 1, c, ti)])
                proj_fm([0], [(0, A[0])], tiles, epi)
                p.emit()

            with ExitStack() as st:
                p.stack = st
                tmp = [p.sb("tmpf%d" % i, [128, 512], F32) for i in range(2)]
                load_w(w_ao, 1)
                load_w(w_in[:, 6144:7168], 0)

                def epi(c, ti, t0, tw, bs):
                    k = c % 2
                    OP("act", lambda h: h.activation(tmp[k][:, 0:tw], ps[bs[1]][:, 0:tw], AF.Sigmoid), [("ps", bs[1])], [("tmp", k)])
                    OP("dve", lambda h: h.tensor_tensor(tmp[k][:, 0:tw], ps[bs[0]][:, 0:tw], tmp[k][:, 0:tw], ALU.mult),
                       [("tmp", k), ("ps", bs[0])], [("tmp", k)])
                    OP("dve", lambda h: h.tensor_tensor(A[2][:, c, t0:t0 + tw], A[2][:, c, t0:t0 + tw], tmp[k][:, 0:tw], ALU.add),
                       [("tmp", k)], [("A", 2, c, ti)])
                proj_fm([1, 0], [(1, A[1]), (0, A[0])], tiles, epi)
                p.emit()

            with ExitStack() as st:
                p.stack = st
                xt = [p.sb("xt%d" % i, [128, 1024], F32) for i in range(2)]
                rr = [p.sb("rr%d" % i, [128, 1024], F32) for i in range(2)]
                stats = p.sb("stats", [128, 2, 6], F32); mv = p.sb("mv", [128, 2], F32); rstd = p.sb("rstd", [128, 1], F32)
                lng = p.sb("lngs", [128, 1024], F32); lnb = p.sb("lnbs", [128, 1024], F32)
                p.dma_multi([lambda h: h.dma_start(out=lng[:], in_=lng_d), lambda h: h.dma_start(out=lnb[:], in_=lnb_d)], "lnld", writes=["ln"])
                load_w(w_out, 0)
                ntile = 1 if sample else NB
                for i in range(ntile):
                    n = NS if sample else 128
                    k = i % 2
                    ti = 0 if sample else (i * 128) // 512
                    src = xstok if sample else xtok[seq, i * 128:(i + 1) * 128, :]
                    p.dma(lambda h, k=k, src=src, n=n: h.dma_start(out=xt[k][0:n, :], in_=src), "xt%d" % k, writes=[("xt", k)])
                    for half in range(2):
                        b = bank(4)
                        for kc in range(8):
                            OP("pe", lambda h, b=b, kc=kc, half=half, i=i, n=n: h.matmul(ps[b][0:n, :], lhsT=A[2][:, kc, i * 128:i * 128 + n],
                                                                                          rhs=wbf[0][:, kc, half * 512:(half + 1) * 512], start=(kc == 0), stop=(kc == 7)),
                               [("A", 2, kc, ti)] + [("wbf", 0, q) for q in range(half * 4, half * 4 + 4)], [("ps", b)])
                        g_ap = gate_s[0:n, half * 512:(half + 1) * 512] if sample else grow[:, seq, half * 512:(half + 1) * 512]
                        OP("dve", lambda h, b=b, k=k, half=half, n=n, g_ap=g_ap: h.tensor_tensor(rr[k][0:n, half * 512:(half + 1) * 512], ps[b][0:n, :], g_ap, ALU.mult),
                           [("ps", b)], [("rr", k)])
                    OP("dve", lambda h, k=k, n=n: h.scalar_tensor_tensor(rr[k][0:n, :], xt[k][0:n, :], ALPHA, rr[k][0:n, :], ALU.mult, ALU.add), [("xt", k), ("rr", k)], [("rr", k)])
                    for half in range(2):
                        OP("dve", lambda h, k=k, n=n, half=half: h.bn_stats(stats[0:n, half, :], rr[k][0:n, half * 512:(half + 1) * 512]), [("rr", k)], ["stats"])
                    OP("dve", lambda h, n=n: h.bn_aggr(mv[0:n, :], stats[0:n, :, :].rearrange("p a b -> p (a b)")), ["stats"], ["mv"])
                    OP("dve", lambda h, n=n: h.tensor_scalar_add(rstd[0:n, :], mv[0:n, 1:2], LN_EPS), ["mv"], ["rstd"])
                    OP("act", lambda h, n=n: h.activation(rstd[0:n, :], rstd[0:n, :], AF.Ln), ["rstd"], ["rstd"])
                    OP("act", lambda h, n=n: h.activation(rstd[0:n, :], rstd[0:n, :], AF.Exp, scale=-0.5), ["rstd"], ["rstd"])
                    OP("dve", lambda h, k=k, n=n: h.tensor_scalar(rr[k][0:n, :], rr[k][0:n, :], mv[0:n, 0:1], rstd[0:n, 0:1], ALU.subtract, ALU.mult), ["mv", "rstd", ("rr", k)], [("rr", k)])
                    OP("dve", lambda h, k=k, n=n: h.tensor_tensor(rr[k][0:n, :], rr[k][0:n, :], lng[0:n, :], ALU.mult), [("rr", k), "ln"], [("rr", k)])
                    OP("pool", lambda h, k=k, n=n: h.tensor_tensor(rr[k][0:n, :], rr[k][0:n, :], lnb[0:n, :], ALU.add), [("rr", k), "ln"], [("rr", k)])
                    dst = ys_o if sample else y_o[seq, i * 128:(i + 1) * 128, :]
                    p.dma(lambda h, k=k, n=n, dst=dst: h.dma_start(out=dst, in_=rr[k][0:n, :]), "o_y%d" % k, reads=[("rr", k)], is_out=True)
                p.emit()

        def prompt_attention(seq):
            with ExitStack() as st:
                p.stack = st
                wq = [p.sb("wq%d" % i, [128, 8, 128], BF16) for i in range(3)]
                qT = p.sb("qTp", [128, S], BF16); kT = p.sb("kTp", [128, S], BF16)
                vb = p.sb("vbp", [128, NB, 128], BF16)
                kst = p.sb("kst", [128, 4, 128], F32); vst = p.sb("vst", [128, 4, 128], F32)
                E = [p.sb("E%d" % i, [128, 512], F32) for i in range(4)]
                SP = [p.sb("SP%d" % i, [128, 512], BF16) for i in range(4)]
                EC = [p.sb("EC%d" % i, [128, 512], F32) for i in range(2)]
                Wb = [p.sb("Wb%d" % i, [128, 512], BF16) for i in range(2)]
                for c in range(8):
                    for j in range(3):
                        sl = j % 2
                        col0 = j * 1024 + c * 128
                        p.dma(lambda h, sl=sl, col0=col0: h.dma_start(out=stg[sl][:], in_=w_in[:, col0:col0 + 128].rearrange("(kc p) n -> p kc n", p=128)),
                              "stg%d" % sl, writes=[("stg", sl)])
                        OP("pool", lambda h, sl=sl, j=j: h.tensor_copy(wq[j][:], stg[sl][:]), [("stg", sl)], [("wq", j)])
                    for j, dst in ((0, qT), (1, kT)):
                        for ti in range(NT):
                            b = bank(4)
                            for kc in range(8):
                                OP("pe", lambda h, b=b, kc=kc, j=j, ti=ti: h.matmul(ps[b][:, :], lhsT=wq[j][:, kc, :], rhs=A[0][:, kc, ti * 512:(ti + 1) * 512],
                                                                                  start=(kc == 0), stop=(kc == 7)), [("wq", j), ("A", 0, kc, ti)], [("ps", b)])
                            OP("dve", lambda h, b=b, dst=dst, ti=ti: h.tensor_copy(dst[:, ti * 512:(ti + 1) * 512], ps[b][:, :]), [("ps", b)], [("qk", j)])
                    for j, stb, dro in ((1, kst, kp_o), (2, vst, vp_o)):
                        for i4 in range(NB // 4):
                            b = bank(4)
                            for bl in range(4):
                                i = i4 * 4 + bl
                                for kc in range(8):
                                    OP("pe", lambda h, b=b, kc=kc, j=j, i=i, bl=bl: h.matmul(ps[b][:, bl * 128:(bl + 1) * 128], lhsT=A[0][:, kc, i * 128:(i + 1) * 128],
                                                                                            rhs=wq[j][:, kc, :], start=(kc == 0), stop=(kc == 7)),
                                       [("wq", j), ("A", 0, kc, i // 4)], [("ps", b)])
                            OP("dve", lambda h, b=b, stb=stb: h.tensor_copy(stb[:].rearrange("p a b -> p (a b)"), ps[b][:, :]), [("ps", b)], [("st", j)])
                            if j == 2:
                                OP("dve", lambda h, b=b, i4=i4: h.tensor_copy(vb[:, i4 * 4:(i4 + 1) * 4, :].rearrange("p a b -> p (a b)"), ps[b][:, :]), [("ps", b)], ["vb"])
                            p.dma(lambda h, stb=stb, dro=dro, i4=i4: h.dma_start(
                                out=dro[seq, i4 * 512:(i4 + 1) * 512, c * 128:(c + 1) * 128].rearrange("(a p) n -> p a n", p=128), in_=stb[:]),
                                "o_kv%d" % j, reads=[("st", j)], is_out=True)
                    for Tq in range(NT):
                        psS = (0, 1); acc = (2, 3); psO = 4
                        for hd in range(2):
                            hp = hd * 64
                            OP("pe", lambda h, hd=hd: h.matmul(ps[acc[hd]][:, :], lhsT=zbf[:, :], rhs=qT[:, 0:512], start=True, stop=True), ["zbf", ("qk", 0)], [("ps", acc[hd])])
                            OP("pe", lambda h, hp=hp: h.matmul(ps[psO][hp:hp + 64, :], lhsT=zbf[:, 0:64], rhs=qT[:, 0:512], start=True, stop=True), ["zbf", ("qk", 0)], [("psO", hd)])
                        def geo(j):
                            dj = j - 4 * Tq
                            c0 = max(0, 128 * dj)
                            return dj, c0, 512 * Tq + c0, 512 - c0

                        def stS(j):
                            dj, c0, q0, cw = geo(j)
                            for hd in range(2):
                                hp = hd * 64
                                OP("pe", lambda h, hd=hd, hp=hp: h.matmul(ps[psS[hd]][:, c0:512], lhsT=kT[hp:hp + 64, j * 128:(j + 1) * 128], rhs=qT[hp:hp + 64, q0:q0 + cw],
                                                                       start=True, stop=True), [("qk", 0), ("qk", 1)], [("ps", psS[hd])])

                        def stE(j):
                            dj, c0, q0, cw = geo(j)
                            kb = j % 2
                            for hd in range(2):
                                Eb = E[2 * kb + hd]; SPb = SP[2 * kb + hd]
                                OP("act", lambda h, hd=hd, Eb=Eb: h.activation(Eb[:, c0:512], ps[psS[hd]][:, c0:512], AF.Exp, bias=sbb[:, 2 * c + hd:2 * c + hd + 1], scale=0.125),
                                   [("ps", psS[hd])], [("E", kb, hd)])
                                if dj >= 0:
                                    OP("dve", lambda h, Eb=Eb: h.tensor_tensor(Eb[:, c0:c0 + 128], Eb[:, c0:c0 + 128], mask01[:, :], ALU.mult), [("E", kb, hd)], [("E", kb, hd)])
                                OP("act", lambda h, Eb=Eb, SPb=SPb: h.activation(SPb[:, c0:512], Eb[:, c0:512], AF.Ln, bias=1.0), [("E", kb, hd)], [("SP", kb, hd)])

                        def stU(j):
                            dj, c0, q0, cw = geo(j)
                            kb = j % 2
                            for hd in range(2):
                                SPb = SP[2 * kb + hd]
                                OP("pe", lambda h, hd=hd, SPb=SPb: h.matmul(ps[acc[hd]][:, c0:512], lhsT=Ubf[:, :], rhs=SPb[:, c0:512], start=False, stop=True, skip_group_check=True),
                                   [("SP", kb, hd)], [("ps", acc[hd])])

                        def stC(j):
                            dj, c0, q0, cw = geo(j)
                            for hd in range(2):
                                OP("act", lambda h, hd=hd: h.activation(EC[hd][:, c0:512], ps[acc[hd]][:, c0:512], AF.Exp, scale=-1.0), [("ps", acc[hd])], [("EC", hd)])

                        def stW(j):
                            dj, c0, q0, cw = geo(j)
                            kb = j % 2
                            for hd in range(2):
                                hp = hd * 64
                                Eb = E[2 * kb + hd]; SPb = SP[2 * kb + hd]
                                OP("pe", lambda h, hd=hd, SPb=SPb: h.matmul(ps[acc[hd]][:, c0:512], lhsT=Lcbf[:, :], rhs=SPb[:, c0:512], start=False, stop=True, skip_group_check=True),
                                   [("SP", kb, hd)], [("ps", acc[hd])])
                                OP("dve", lambda h, hd=hd, Eb=Eb: h.tensor_tensor(Wb[hd][:, c0:512], Eb[:, c0:512], EC[hd][:, c0:512], ALU.mult), [("E", kb, hd), ("EC", hd)], [("Wb", hd)])
                                OP("pe", lambda h, hd=hd, hp=hp: h.matmul(ps[psO][hp:hp + 64, c0:512], lhsT=vb[:, j, hd * 64:(hd + 1) * 64], rhs=Wb[hd][:, c0:512],
                                                                       start=False, stop=True, skip_group_check=True), [("Wb", hd), "vb"], [("psO", hd)])

                        jtop = 4 * Tq + 3
                        stS(jtop)
                        stE(jtop)
                        for j in range(jtop, -1, -1):
                            stU(j)
                            if j > 0:
                                stS(j - 1)
                            stC(j)
                            if j > 0:
                                stE(j - 1)
                            stW(j)
                        OP("dve", lambda h, Tq=Tq: h.tensor_copy(A[1][:, c, Tq * 512:(Tq + 1) * 512], ps[psO][:, :]), [("psO", 0), ("psO", 1)], [("A", 1, c, Tq)])
                p.emit()

        def sample_attention():
            GS = 4 if NS >= 4 else NS
            with ExitStack() as st:
                p.stack = st
                qtok = p.sb("qtok", [128, 1024], F32)
                oh = p.sb("oh", [128, 128], F32)
                ptb = p.sb("ptb", [128, NS * NPG], I32); idx = p.sb("idx", [128, NS * NPG], I32)
                iop = p.sb("iop", [128, 1], F32)
                U32 = p.sb("U32s", [128, 128], F32); ones32 = p.sb("ones32s", [128, 128], F32)
                bmask = p.sb("bmasks", [16, 1024], F32)
                qbs = [p.sb("qb%d" % i, [128, 1024], F32) for i in range(2)]
                pg_ = [p.sb("pg%d" % i, [128, 1024], F32) for i in range(3)]
                prod = [p.sb("prod%d" % i, [128, 1024], F32) for i in range(1)]
                sc = p.sb("sc", [128, GS, NPG, 16], F32)
                E8 = sc; SP8 = p.sb("SP8", [128, GS, NPG, 16], F32)
                WL = p.sb("WL", [128, GS, NPG, 16], F32)
                car = p.sb("car", [128, NPG, 16], F32); cum = p.sb("cum", [128, NPG, 16], F32)
                om = p.sb("om", [16, 1024], F32)
                p.dma_multi([lambda h: h.dma_start(out=ptb[:], in_=pt_d.partition_broadcast(128)),
                             lambda h: h.dma_start(out=iop[:], in_=iotap_d), lambda h: h.dma_start(out=U32[:], in_=U32_d),
                             lambda h: h.dma_start(out=ones32[:], in_=ones32_d), lambda h: h.dma_start(out=bmask[:], in_=bmask_d)], "sset", writes=["sset"])
                OP("dve", lambda h: h.tensor_scalar(idx[:], ptb[:], 128.0, iop[:, 0:1], ALU.mult, ALU.add), ["sset"], ["idx"])
                for j, (dst, dro) in enumerate(((qtok, None), (prod[0], ks_o), (prod[0], vs_o))):
                    load_w(w_in[:, j * 1024:(j + 1) * 1024], j % 2)
                    for half in range(2):
                        b = bank(4)
                        for kc in range(8):
                            OP("pe", lambda h, b=b, kc=kc, j=j, half=half: h.matmul(ps[b][0:NS, :], lhsT=A[0][:, kc, 0:NS], rhs=wbf[j % 2][:, kc, half * 512:(half + 1) * 512],
                                                                                  start=(kc == 0), stop=(kc == 7)),
                               [("A", 0, kc, 0)] + [("wbf", j % 2, q) for q in range(half * 4, half * 4 + 4)], [("ps", b)])
                        OP("dve", lambda h, b=b, dst=dst, half=half: h.tensor_copy(dst[0:NS, half * 512:(half + 1) * 512], ps[b][0:NS, :]), [("ps", b)], [("tok", False) if j == 0 else ("prod", 0)])
                    if dro is not None:
                        p.dma(lambda h, dst=dst, dro=dro: h.dma_start(out=dro, in_=dst[0:NS, :]), "o_kvs", reads=[("prod", 0)], writes=[], is_out=True)
                for g0 in range(0, NS, GS):
                    def qb_for(bl):
                        b_ = g0 + bl
                        k2 = 0
                        OP("dve", lambda h, b_=b_: h.tensor_scalar(oh[0:NS, :], iop[0:NS, 0:1].to_broadcast([NS, 128]), float(b_), None, ALU.is_equal), ["sset"], ["oh"])
                        for half in range(2):
                            bq = 6 + half
                            OP("pe", lambda h, bq=bq, b_=b_, half=half: h.matmul(ps[bq][:, :], lhsT=oh[0:NS, :], rhs=qtok[0:NS, half * 512:(half + 1) * 512], start=True, stop=True),
                               ["oh", ("tok", False)], [("ps", bq)])
                            OP("dve", lambda h, bq=bq, bl=bl, half=half: h.tensor_copy(qbs[bl % 2][:, half * 512:(half + 1) * 512], ps[bq][:, :]), [("ps", bq)], [("qbs", bl % 2)])
                        return None
                    NBUF, PF = 3, 2
                    pages = [(bl, pgi) for bl in range(GS) for pgi in range(NPG)]

                    def issue(src, ix):
                        bl, pgi = pages[ix]
                        e = (g0 + bl) * NPG + pgi
                        k3 = ix % NBUF
                        p.dma(lambda h: h.indirect_dma_start(out=pg_[k3][:], out_offset=None, in_=src,
                                                              in_offset=bass.IndirectOffsetOnAxis(ap=idx[:, e:e + 1], axis=0)),
                              "pg%d" % k3, reads=["idx"], writes=[("pg", k3)], q="pool")

                    for ix in range(min(PF, len(pages))):
                        issue(ck, ix)
                    for ix, (bl, pgi) in enumerate(pages):
                        k3 = ix % NBUF
                        if pgi == 0:
                            qb_for(bl)
                        if ix + PF < len(pages):
                            issue(ck, ix + PF)
                        OP("dve", lambda h: h.tensor_tensor(prod[0][:], pg_[k3][:], qbs[bl % 2][:], ALU.mult), [("pg", k3), ("qbs", bl % 2)], [("prod", 0)])
                        OP("dve", lambda h: h.tensor_reduce(sc[:, bl, pgi, :], prod[0][:].rearrange("p (a b) -> p a b", a=16), AX.X, ALU.add),
                           [("prod", 0)], ["sc"])
                    sc3 = sc[:].rearrange("p a b c -> p (a b) c")
                    OP("dve", lambda h: h.scalar_tensor_tensor(E8[:].rearrange("p a b c -> p (a b) c"), sc3, 0.125, sbb[:, :].unsqueeze(1).broadcast_to([128, GS * NPG, 16]), ALU.mult, ALU.add),
                       ["sc"], ["sc"])
                    E8f = E8[:].rearrange("p a b c -> p (a b c)"); SP8f = SP8[:].rearrange("p a b c -> p (a b c)")
                    OP("act", lambda h: h.activation(E8f, E8f, AF.Exp), ["sc"], ["sc"])
                    OP("act", lambda h: h.activation(SP8f, E8f, AF.Ln, bias=1.0), ["sc"], ["SP8"])
                    for bl in range(GS):
                        bC, bT = 4, 5
                        spb = SP8[:, bl, :, :].rearrange("p b c -> p (b c)")
                        OP("pe", lambda h, spb=spb: h.matmul(ps[bC][:, 0:NPG * 16], lhsT=U32[:, :], rhs=spb, start=True, stop=True), ["SP8", "sset"], [("ps", bC)])
                        OP("pe", lambda h, spb=spb: h.matmul(ps[bT][:, 0:NPG * 16], lhsT=ones32[:, :], rhs=spb, start=True, stop=True), ["SP8", "sset"], [("ps", bT)])
                        tot = ps[bT][:, 0:NPG * 16].rearrange("p (b c) -> p b c", c=16)
                        OP("dve", lambda h: h.memset(car[:, NPG - 1, :], 0.0), [], ["car"])
                        for pgi in range(NPG - 2, -1, -1):
                            OP("dve", lambda h, pgi=pgi, tot=tot: h.tensor_tensor(car[:, pgi, :], car[:, pgi + 1, :], tot[:, pgi + 1, :], ALU.add), ["car", ("ps", bT)], ["car"])
                        OP("dve", lambda h: h.tensor_tensor(cum[:].rearrange("p b c -> p (b c)"), ps[bC][:, 0:NPG * 16], car[:].rearrange("p b c -> p (b c)"), ALU.add),
                           [("ps", bC), "car"], ["cum"])
                        OP("act", lambda h: h.activation(cum[:].rearrange("p b c -> p (b c)"), cum[:].rearrange("p b c -> p (b c)"), AF.Exp, scale=-1.0), ["cum"], ["cum"])
                        OP("dve", lambda h, bl=bl: h.tensor_tensor(WL[:, bl, :, :], E8[:, bl, :, :], cum[:], ALU.mult), ["cum", "sc"], ["WL"])
                    for ix in range(min(PF, len(pages))):
                        issue(cv, ix)
                    for bl in range(GS):
                        b_ = g0 + bl
                        for pgi in range(NPG):
                            ix = bl * NPG + pgi
                            k3 = ix % NBUF
                            if ix + PF < len(pages):
                                issue(cv, ix + PF)
                            for half in range(2):
                                OP("pe", lambda h, half=half: h.matmul(ps[4 + half][0:16, :], lhsT=WL[:, bl, pgi, :], rhs=pg_[k3][:, half * 512:(half + 1) * 512],
                                                                     start=(pgi == 0), stop=(pgi == NPG - 1)), [("pg", k3), "WL"], [("ps", 4 + half)])
                        for half in range(2):
                            OP("dve", lambda h, half=half: h.tensor_tensor(om[:, half * 512:(half + 1) * 512], ps[4 + half][0:16, :], bmask[:, half * 512:(half + 1) * 512], ALU.mult),
                               [("ps", 4 + half), "sset"], ["om"])
                        for c8 in range(8):
                            OP("pe", lambda h, c8=c8, b_=b_: h.matmul(ps[3][:, c8 * NS + b_:c8 * NS + b_ + 1], lhsT=om[:, c8 * 128:(c8 + 1) * 128], rhs=ones32[0:16, 0:1], start=True, stop=True),
                               ["om", "sset"], [("ps", 3)])
                OP("dve", lambda h: h.tensor_copy(A[1][:, :, 0:NS], ps[3][:, 0:8 * NS].rearrange("p (a b) -> p a b", a=8)), [("ps", 3)], [("A", 1, c, 0) for c in range(8)])
                p.emit()

        for seq in range(NSEQ):
            run_pass(seq)
        run_pass(None)
        p.emit(final=True)
    return nc


NCORES = 8
_CACHE = {}


def _fm(v):
    return np.ascontiguousarray(v.reshape(8, 128).T)


def kernel(x_prompt, x_sample, c_prompt, c_sample, cache_k, cache_v, state_ssm_re, state_ssm_im, page_table,
           w_cond, b_cond, w_in, sb_bias, ssm_a_re, ssm_a_im, ssm_log_dt, ssm_b_re, ssm_b_im, ssm_c_re, ssm_c_im,
           ssm_d, w_glu, b_glu, w_att_out, w_ssm_out, w_out, ln_g, ln_b):
    f = np.float32
    A_ = lambda a: np.ascontiguousarray(np.asarray(a))
    x_prompt = A_(x_prompt); x_sample = A_(x_sample); c_prompt = A_(c_prompt); c_sample = A_(c_sample)
    B, S, D = x_prompt.shape
    DB = x_sample.shape[0]
    NPG = page_table.shape[1]
    NPHYS = cache_k.shape[1]
    ncores = min(NCORES, B)
    NSEQ = B // ncores
    NS = DB // ncores
    key = (S, NSEQ, NS, NPG, NPHYS)
    if key not in _CACHE:
        _CACHE[key] = build(*key)
    nc = _CACHE[key]
    bf = ml_dtypes.bfloat16
    kk = np.arange(128)
    Ubf = (kk[:, None] >= kk[None, :]).astype(f)
    consts = {
        "Ubf": Ubf.astype(bf), "Lcbf": (kk[:, None] < kk[None, :]).astype(f).astype(bf),
        "mask01": (kk[:, None] < kk[None, :]).astype(f), "tvec": np.tile(np.arange(128, dtype=f)[None, :], (128, 1)),
        "iotap": np.arange(128, dtype=f).reshape(128, 1), "U32": Ubf, "ones32": np.ones((128, 128), f),
        "bmask": np.kron(np.eye(16, dtype=f), np.ones((1, 64), f)),
    }
    bc = A_(b_cond)[0]
    are = A_(ssm_a_re)[0].reshape(32, 128).T
    aim = A_(ssm_a_im)[0].reshape(32, 128).T
    ldt = np.repeat(A_(ssm_log_dt)[0].reshape(32, 2, 1), 64, axis=2).reshape(32, 128).T
    bre, bim, cre, cim = A_(ssm_b_re)[0], A_(ssm_b_im)[0], A_(ssm_c_re)[0], A_(ssm_c_im)[0]
    BTr = np.zeros((128, 32, 128), f); BTi = np.zeros((128, 32, 128), f)
    CTr = np.zeros((128, 32, 128), f); CTi = np.zeros((128, 32, 128), f)
    for G in range(32):
        for g2 in range(2):
            g = 2 * G + g2
            r0 = 32 * (G % 4) + 16 * g2
            BTr[r0:r0 + 16, G, g2 * 64:(g2 + 1) * 64] = bre[g].T
            BTi[r0:r0 + 16, G, g2 * 64:(g2 + 1) * 64] = bim[g].T
            CTr[g2 * 64:(g2 + 1) * 64, G, r0:r0 + 16] = cre[g].T
            CTi[g2 * 64:(g2 + 1) * 64, G, r0:r0 + 16] = cim[g].T
    shared = dict(consts)
    shared.update({
        "w_cond": A_(w_cond)[0], "bcT": np.ascontiguousarray(bc.reshape(24, 128).T), "bgrow": np.tile(bc[None, 2048:3072], (128, 1)),
        "w_in": A_(w_in)[0], "w_glu": A_(w_glu)[0], "w_ao": A_(w_att_out)[0], "w_so": A_(w_ssm_out)[0], "w_out": A_(w_out)[0],
        "bgluT": _fm(A_(b_glu)[0]), "dT": _fm(A_(ssm_d)[0]), "sbb": np.tile(A_(sb_bias)[0][None, :], (128, 1)),
        "a_re": np.ascontiguousarray(are), "a_im": np.ascontiguousarray(aim), "ldt": np.ascontiguousarray(ldt),
        "BTr": BTr, "BTi": BTi, "CTr": CTr, "CTi": CTi,
        "lng": np.tile(A_(ln_g)[0][None, :], (128, 1)), "lnb": np.tile(A_(ln_b)[0][None, :], (128, 1)),
        "cache_k": A_(cache_k)[0].reshape(NPHYS * 128, 1024), "cache_v": A_(cache_v)[0].reshape(NPHYS * 128, 1024),
    })
    shared = {k: np.ascontiguousarray(v) for k, v in shared.items()}
    pt = A_(page_table).astype(np.int32)
    sre_in, sim_in = A_(state_ssm_re)[0], A_(state_ssm_im)[0]
    in_maps = []
    for i in range(ncores):
        seqs = list(range(i * NSEQ, (i + 1) * NSEQ))
        sm = slice(i * NS, (i + 1) * NS)
        m = dict(shared)
        m["xT"] = np.ascontiguousarray(np.stack([x_prompt[s].T.reshape(8, 128, S).transpose(1, 0, 2) for s in seqs]))
        m["xtok"] = np.ascontiguousarray(x_prompt[seqs])
        cols = [c_prompt[s] for s in seqs] + [c_sample[b] for b in range(i * NS, (i + 1) * NS)]
        m["cT"] = np.ascontiguousarray(np.stack([_fm(v) for v in cols], axis=2))
        m["crep"] = np.ascontiguousarray(np.stack([np.repeat(_fm(c_prompt[s])[:, :, None], 128, axis=2) for s in seqs]))
        xs = x_sample[sm, 0, :]
        m["xsT"] = np.ascontiguousarray(np.stack([_fm(v) for v in xs], axis=2))
        m["xstok"] = np.ascontiguousarray(xs)
        m["pt"] = np.ascontiguousarray(pt[sm].reshape(1, -1))
        m["sst_re"] = np.ascontiguousarray(sre_in[sm].reshape(NS, 32, 128).transpose(2, 1, 0))
        m["sst_im"] = np.ascontiguousarray(sim_in[sm].reshape(NS, 32, 128).transpose(2, 1, 0))
        in_maps.append(m)
    res = run_bass_kernel_spmd(nc, in_maps, core_ids=list(range(ncores)))
    R = res.results
    H, Dh = 16, 64
    y = np.concatenate([r["y"] for r in R], 0)
    kp = np.concatenate([r["kp"] for r in R], 0).reshape(1, B, S, H, Dh)
    vp = np.concatenate([r["vp"] for r in R], 0).reshape(1, B, S, H, Dh)
    sre = np.concatenate([r["sre"].transpose(0, 2, 1).reshape(NSEQ, 64, 64) for r in R], 0)[None]
    sim = np.concatenate([r["sim"].transpose(0, 2, 1).reshape(NSEQ, 64, 64) for r in R], 0)[None]
    ys = np.concatenate([r["ys"] for r in R], 0).reshape(DB, 1, D)
    ks = np.concatenate([r["ks"] for r in R], 0).reshape(1, DB, 1, H, Dh)
    vs = np.concatenate([r["vs"] for r in R], 0).reshape(1, DB, 1, H, Dh)
    ssre = np.concatenate([r["ssre"].transpose(2, 1, 0).reshape(NS, 64, 64) for r in R], 0)[None]
    ssim = np.concatenate([r["ssim"].transpose(2, 1, 0).reshape(NS, 64, 64) for r in R], 0)[None]
    outs = (y, ys, kp, vp, sre, sim, ks, vs, ssre, ssim)
    return tuple(np.ascontiguousarray(o.astype(np.float32)) for o in outs)
```

```python
import math
import ml_dtypes
from concourse.bass_utils import run_bass_kernel_spmd
import numpy as np
from contextlib import ExitStack
import concourse.bass as bass
import concourse.mybir as mybir

F32 = mybir.dt.float32
BF16 = mybir.dt.bfloat16
I32 = mybir.dt.int32
AF = mybir.ActivationFunctionType
ALU = mybir.AluOpType
AX = mybir.AxisListType


import types


def _snap(fn):
    if fn.__closure__ is None:
        return fn
    cells = []
    for c in fn.__closure__:
        try:
            cells.append(types.CellType(c.cell_contents))
        except ValueError:
            cells.append(c)
    return types.FunctionType(fn.__code__, fn.__globals__, fn.__name__, fn.__defaults__, tuple(cells))


class _Op:
    __slots__ = ("fn", "waits", "dwaits", "idx", "dma", "milestone")

    def __init__(self, fn, idx, dma=None):
        self.fn = _snap(fn)
        self.waits = []
        self.dwaits = []
        self.idx = idx
        self.dma = dma
        self.milestone = False


class Prog:
    ENGS = ("pe", "act", "dve", "pool", "sp")

    def __init__(self, nc, stack):
        self.nc = nc
        self.stack = stack
        self.gstack = stack
        self.h = {"pe": nc.tensor, "act": nc.scalar, "dve": nc.vector,
                  "pool": nc.gpsimd, "sp": nc.sync}
        self.ops = {e: [] for e in self.ENGS}
        self.seen = {e: {} for e in self.ENGS}
        self.last_w = {}
        self.readers = {}
        self.dsem_cnt = {}
        self.out_dma_keys = set()
        self.same_engine_sync = {"act", "dve", "pool"}

    def sb(self, name, shape, dt):
        self._uid = getattr(self, "_uid", 0) + 1
        return self.stack.enter_context(self.nc.sbuf_tensor("s%d_%s" % (self._uid, name), list(shape), dt))

    def ps(self, name, shape, dt=F32):
        return self.stack.enter_context(self.nc.psum_tensor(name, list(shape), dt))

    def _deps(self, reads, writes):
        deps = []
        for r in reads:
            t = self.last_w.get(r)
            if t is not None:
                deps.append(t)
        for w in writes:
            t = self.last_w.get(w)
            if t is not None:
                deps.append(t)
            deps.extend(self.readers.get(w, ()))
        return deps

    def _add_waits(self, eng, op, deps):
        seen = self.seen[eng]
        for t in deps:
            kind, key, val = t
            if kind == "e":
                if key == eng and eng not in self.same_engine_sync:
                    continue
                if key == eng and val >= op.idx:
                    continue
                if seen.get(("e", key), -1) >= val:
                    continue
                seen[("e", key)] = val
                op.waits.append((key, val))
                self.ops[key][val].milestone = True
            else:
                if seen.get(("d", key), -1) >= val:
                    continue
                seen[("d", key)] = val
                op.dwaits.append((key, val))

    def _commit(self, tok, reads, writes):
        for r in reads:
            self.readers.setdefault(r, []).append(tok)
        for w in writes:
            self.last_w[w] = tok
            self.readers[w] = []

    def op(self, eng, fn, reads=(), writes=()):
        lst = self.ops[eng]
        o = _Op(fn, len(lst))
        self._add_waits(eng, o, self._deps(reads, writes))
        lst.append(o)
        self._commit(("e", eng, o.idx), reads, writes)
        return o

    def dma(self, fn, semkey, reads=(), writes=(), q="sp", is_out=False):
        lst = self.ops[q]
        o = _Op(fn, len(lst), dma=semkey)
        self._add_waits(q, o, self._deps(reads, writes))
        lst.append(o)
        c = self.dsem_cnt.get(semkey, 0) + 16
        self.dsem_cnt[semkey] = c
        self._commit(("d", semkey, c), reads, writes)
        if is_out:
            self.out_dma_keys.add(semkey)
        return o

    def dma_multi(self, fns, semkey, reads=(), writes=(), q="sp", is_out=False):
        lst = self.ops[q]
        deps = self._deps(reads, writes)
        first = True
        for fn in fns:
            o = _Op(fn, len(lst), dma=semkey)
            if first:
                self._add_waits(q, o, deps)
                first = False
            lst.append(o)
            self.dsem_cnt[semkey] = self.dsem_cnt.get(semkey, 0) + 16
        self._commit(("d", semkey, self.dsem_cnt[semkey]), reads, writes)
        if is_out:
            self.out_dma_keys.add(semkey)

    def barrier(self):
        last = {}
        for e in self.ENGS:
            for o in reversed(self.ops[e]):
                if o.dma is None:
                    last[e] = o.idx
                    break
        dtoks = [("d", k, v) for k, v in self.dsem_cnt.items()]
        for e in self.ENGS:
            o = _Op(lambda h: h.nop(), len(self.ops[e]))
            deps = [("e", k, v) for k, v in last.items() if k != e] + dtoks
            self._add_waits(e, o, deps)
            self.ops[e].append(o)

    def emit(self, final=False):
        nc = self.nc
        self.barrier()
        if not hasattr(self, "esem"):
            self.esem = {e: self.gstack.enter_context(nc.semaphore("es_" + e)) for e in self.ENGS}
            self.dsem = {}
            self.mbase = {e: 0 for e in self.ENGS}
        for k in self.dsem_cnt:
            if k not in self.dsem:
                self.dsem[k] = self.gstack.enter_context(nc.semaphore("ds_%d" % len(self.dsem)))
        esem, dsem = self.esem, self.dsem
        fin = [(k, self.dsem_cnt[k]) for k in self.out_dma_keys] if final else []
        mcount = {}
        for e in self.ENGS:
            n = self.mbase[e]
            m = {}
            for o in self.ops[e]:
                if o.milestone:
                    assert o.dma is None
                    n += 1
                    m[o.idx] = n
            mcount[e] = m
            self.mbase[e] = n
        ops = self.ops

        def run(e, h):
            for o in ops[e]:
                for (k, v) in o.waits:
                    h.wait_ge(esem[k], mcount[k][v])
                for (k, v) in o.dwaits:
                    h.wait_ge(dsem[k], v)
                ins = o.fn(h)
                if o.dma is not None:
                    ins.then_inc(dsem[o.dma], 16)
                elif o.milestone:
                    ins.then_inc(esem[e], 1)
            if e == "sp":
                for (k, v) in fin:
                    h.wait_ge(dsem[k], v)

        with nc.Block() as block:
            @block.tensor
            def _(h):
                run("pe", h)

            @block.scalar
            def _(h):
                run("act", h)

            @block.vector
            def _(h):
                run("dve", h)

            @block.gpsimd
            def _(h):
                run("pool", h)

            @block.sync
            def _(h):
                run("sp", h)
        self.ops = {e: [] for e in self.ENGS}
        self.seen = {e: {} for e in self.ENGS}
        self.last_w = {}
        self.readers = {}
        self.nstage = getattr(self, "nstage", 0) + 1

PI = math.pi
ALPHA = 2.0 ** 0.25
LN_EPS = 1e-5
GK = 2.0 * math.sqrt(2.0 / PI)


def build(S, NSEQ, NS, NPG, NPHYS):
    nc = bass.Bass("TRN2", target_bir_lowering=False)
    NT = S // 512
    NB = S // 128
    NCOL = NSEQ + NS

    def din(name, shape, dt=F32):
        return nc.dram_tensor(name, list(shape), dt, kind="ExternalInput").ap()

    def dout(name, shape, dt=F32):
        return nc.dram_tensor(name, list(shape), dt, kind="ExternalOutput").ap()

    xT = din("xT", [NSEQ, 128, 8, S]); xtok = din("xtok", [NSEQ, S, 1024])
    cT = din("cT", [128, 8, NCOL]); crep = din("crep", [NSEQ, 128, 8, 128])
    xsT = din("xsT", [128, 8, NS]); xstok = din("xstok", [NS, 1024])
    w_cond = din("w_cond", [1024, 3072]); bcT = din("bcT", [128, 24]); bgrow = din("bgrow", [128, 1024])
    w_in = din("w_in", [1024, 8192]); w_glu = din("w_glu", [1024, 1024]); w_ao = din("w_ao", [1024, 1024])
    w_so = din("w_so", [1024, 1024]); w_out = din("w_out", [1024, 1024])
    bgluT = din("bgluT", [128, 8]); dT_d = din("dT", [128, 8]); sbb_d = din("sbb", [128, 16])
    are_d = din("a_re", [128, 32]); aim_d = din("a_im", [128, 32]); ldt_d = din("ldt", [128, 32])
    BTr_d = din("BTr", [128, 32, 128]); BTi_d = din("BTi", [128, 32, 128])
    CTr_d = din("CTr", [128, 32, 128]); CTi_d = din("CTi", [128, 32, 128])
    lng_d = din("lng", [128, 1024]); lnb_d = din("lnb", [128, 1024])
    Ubf_d = din("Ubf", [128, 128], BF16); Lcbf_d = din("Lcbf", [128, 128], BF16)
    mask01_d = din("mask01", [128, 128]); tvec_d = din("tvec", [128, 128]); iotap_d = din("iotap", [128, 1])
    U32_d = din("U32", [128, 128]); ones32_d = din("ones32", [128, 128])
    bmask_d = din("bmask", [16, 1024])
    ck = din("cache_k", [NPHYS * 128, 1024]); cv = din("cache_v", [NPHYS * 128, 1024])
    pt_d = din("pt", [1, NS * NPG], I32)
    sstr_d = din("sst_re", [128, 32, NS]); ssti_d = din("sst_im", [128, 32, NS])

    y_o = dout("y", [NSEQ, S, 1024]); kp_o = dout("kp", [NSEQ, S, 1024]); vp_o = dout("vp", [NSEQ, S, 1024])
    sre_o = dout("sre", [NSEQ, 128, 32]); sim_o = dout("sim", [NSEQ, 128, 32])
    ys_o = dout("ys", [NS, 1024]); ks_o = dout("ks", [NS, 1024]); vs_o = dout("vs", [NS, 1024])
    ssre_o = dout("ssre", [128, 32, NS]); ssim_o = dout("ssim", [128, 32, NS])

    with ExitStack() as gst:
        p = Prog(nc, gst)
        OP = lambda eng, fn, r=(), w=(): p.op(eng, fn, reads=r, writes=w)
        ps = [p.ps("psb%d" % i, [128, 512]) for i in range(8)]
        A = [p.sb("A%d" % i, [128, 8, S], BF16) for i in range(3)]
        stg = [p.sb("stg%d" % i, [128, 8, 128], F32) for i in range(2)]
        wbf = [p.sb("wbf%d" % i, [128, 8, 1024], BF16) for i in range(2)]
        Ubf = p.sb("Ubf", [128, 128], BF16); Lcbf = p.sb("Lcbf", [128, 128], BF16)
        zbf = p.sb("zbf", [128, 128], BF16)
        mask01 = p.sb("mask01", [128, 128], F32); tvec = p.sb("tvec", [128, 128], F32)
        modT = [p.sb("modT%d" % i, [128, 8, NCOL], F32) for i in range(2)]
        grow = p.sb("grow", [128, NSEQ, 1024], F32)
        gate_s = p.sb("gate_s", [128, 1024], F32)
        bcTs = p.sb("bcTs", [128, 24], F32); bgluTs = p.sb("bgluTs", [128, 8], F32)
        dTs = p.sb("dTs", [128, 8], F32); sbb = p.sb("sbbs", [128, 16], F32)
        mag = p.sb("mag", [128, 32], F32); ang = p.sb("ang", [128, 32], F32)
        abr = p.sb("abr", [128, 32], F32); abi = p.sb("abi", [128, 32], F32)
        cor = p.sb("cor", [128, 32], F32); coi = p.sb("coi", [128, 32], F32)
        rhor = p.sb("rhor", [128, 32], F32); rhoi = p.sb("rhoi", [128, 32], F32)
        xendr = p.sb("xendr", [128, 32], F32); xendi = p.sb("xendi", [128, 32], F32)
        psrr = [0]

        def bank(n=4, base=0):
            b = base + psrr[0] % n
            psrr[0] += 1
            return b

        def sin_of(out, arg, shift, tmp, ra, rt, ro, itile=None, t2=None, r2=None):
            r2 = r2 or (rt + "_2")
            OP("dve", lambda h: h.tensor_scalar(tmp, arg, shift, 1.0 / (2.0 * PI), ALU.add, ALU.mult), [ra], [rt])
            OP("dve", lambda h: h.tensor_copy(itile, tmp), [rt], [rt + "_i"])
            OP("dve", lambda h: h.tensor_copy(tmp, itile), [rt + "_i"], [rt])
            OP("dve", lambda h: h.tensor_scalar(t2, arg, shift, None, ALU.add), [ra], [r2])
            OP("dve", lambda h: h.scalar_tensor_tensor(tmp, tmp, -2.0 * PI, t2, ALU.mult, ALU.add), [rt, r2], [rt])
            OP("dve", lambda h: h.tensor_scalar(t2, tmp, PI, -2.0 * PI, ALU.is_gt, ALU.mult), [rt], [r2])
            OP("dve", lambda h: h.tensor_tensor(tmp, tmp, t2, ALU.add), [rt, r2], [rt])
            OP("act", lambda h: h.activation(out, tmp, AF.Sin), [rt], [ro])

        def load_w(src, slot):
            for q in range(8):
                sl = q % 2
                p.dma(lambda h, q=q, sl=sl: h.dma_start(out=stg[sl][:], in_=src[:, q * 128:(q + 1) * 128].rearrange("(kc p) n -> p kc n", p=128)),
                      "stg%d" % sl, writes=[("stg", sl)])
                OP("pool", lambda h, q=q, sl=sl: h.tensor_copy(wbf[slot][:, :, q * 128:(q + 1) * 128], stg[sl][:]),
                   [("stg", sl)], [("wbf", slot, q)])

        def proj_fm(slots, ins, tiles, epi):
            for ti, (t0, tw) in enumerate(tiles):
                for c in range(8):
                    bs = []
                    for j, (slot, (ai, ab)) in enumerate(zip(slots, ins)):
                        b = bank(6)
                        bs.append(b)
                        for kc in range(8):
                            OP("pe", lambda h, b=b, slot=slot, ab=ab, kc=kc, c=c, t0=t0, tw=tw: h.matmul(
                                ps[b][:, 0:tw], lhsT=wbf[slot][:, kc, c * 128:(c + 1) * 128], rhs=ab[:, kc, t0:t0 + tw],
                                start=(kc == 0), stop=(kc == 7)),
                               [("wbf", slot, c), ("A", ai, kc, ti)], [("ps", b)])
                    epi(c, ti, t0, tw, bs)

        with ExitStack() as st:
            p.stack = st
            cTa = p.sb("cTa", [128, 8, NCOL], F32)
            creps = p.sb("creps", [128, NSEQ, 8, 128], F32)
            are = p.sb("are", [128, 32], F32); aim = p.sb("aim", [128, 32], F32); ldt = p.sb("ldt", [128, 32], F32)
            wst = [p.sb("wst%d" % i, [128, 8, 512], F32) for i in range(2)]
            t32 = [p.sb("t32_%d" % i, [128, 32], F32) for i in range(6)]
            i32t = p.sb("i32t", [128, 32], I32); t2s = p.sb("t2s", [128, 32], F32)
            bgr = p.sb("bgr", [128, 1024], F32)
            loads = [(Ubf[:], Ubf_d), (Lcbf[:], Lcbf_d), (mask01[:], mask01_d), (tvec[:], tvec_d),
                     (bcTs[:], bcT), (bgluTs[:], bgluT), (dTs[:], dT_d), (sbb[:], sbb_d),
                     (cTa[:], cT), (are[:], are_d), (aim[:], aim_d),
                     (ldt[:], ldt_d), (bgr[:], bgrow)]
            for s_ in range(NSEQ):
                loads.append((creps[:, s_, :, :], crep[s_]))
            p.dma_multi([(lambda h, o=o, i=i: h.dma_start(out=o, in_=i)) for o, i in loads], "setup", writes=["setup"])
            OP("dve", lambda h: h.memset(zbf[:], 0.0), [], ["zbf"])
            for part in range(3):
                for half in range(2):
                    sl = (part * 2 + half) % 2
                    col0 = part * 1024 + half * 512
                    p.dma(lambda h, sl=sl, col0=col0: h.dma_start(out=wst[sl][:], in_=w_cond[:, col0:col0 + 512].rearrange("(kc p) n -> p kc n", p=128)),
                          "wst%d" % sl, writes=[("wst", sl)])
                    if part < 2:
                        for fcl in range(4):
                            fc = half * 4 + fcl
                            b = bank()
                            for kc in range(8):
                                OP("pe", lambda h, b=b, sl=sl, kc=kc, fcl=fcl: h.matmul(ps[b][:, 0:NCOL], lhsT=wst[sl][:, kc, fcl * 128:(fcl + 1) * 128],
                                                                                      rhs=cTa[:, kc, :], start=(kc == 0), stop=(kc == 7)),
                                   [("wst", sl), "setup"], [("ps", b)])
                            OP("dve", lambda h, b=b, part=part, fc=fc: h.tensor_scalar(modT[part][:, fc, :], ps[b][:, 0:NCOL], bcTs[:, part * 8 + fc:part * 8 + fc + 1],
                                                                                  1.0 if part == 1 else 0.0, ALU.add, ALU.add),
                               [("ps", b), "setup"], [("modT", part)])
                    else:
                        for s_ in range(NSEQ):
                            b = bank()
                            for kc in range(8):
                                OP("pe", lambda h, b=b, sl=sl, kc=kc, s_=s_: h.matmul(ps[b][:, :], lhsT=creps[:, s_, kc, :], rhs=wst[sl][:, kc, :],
                                                                                    start=(kc == 0), stop=(kc == 7)),
                                   [("wst", sl), "setup"], [("ps", b)])
                            OP("dve", lambda h, b=b, s_=s_, half=half: h.tensor_tensor(grow[:, s_, half * 512:(half + 1) * 512], ps[b][:, :], bgr[:, half * 512:(half + 1) * 512], ALU.add),
                               [("ps", b), "setup"], [("grow", s_, half)])
                        b = bank()
                        for kc in range(8):
                            OP("pe", lambda h, b=b, sl=sl, kc=kc: h.matmul(ps[b][0:NS, :], lhsT=cTa[:, kc, NSEQ:NCOL], rhs=wst[sl][:, kc, :],
                                                                         start=(kc == 0), stop=(kc == 7)),
                               [("wst", sl), "setup"], [("ps", b)])
                        OP("dve", lambda h, b=b, half=half: h.tensor_tensor(gate_s[0:NS, half * 512:(half + 1) * 512], ps[b][0:NS, :], bgr[0:NS, half * 512:(half + 1) * 512], ALU.add),
                           [("ps", b), "setup"], [("gate_s", half)])
            dt_, lm, c_, s_t, tmp, den = t32
            OP("act", lambda h: h.activation(dt_[:], ldt[:], AF.Exp), ["setup"], ["dt"])
            OP("dve", lambda h: h.tensor_tensor(lm[:], are[:], dt_[:], ALU.mult), ["dt", "setup"], ["lm"])
            OP("act", lambda h: h.activation(mag[:], lm[:], AF.Exp), ["lm"], ["mag"])
            OP("dve", lambda h: h.tensor_tensor(ang[:], aim[:], dt_[:], ALU.mult), ["dt", "setup"], ["ang"])
            sin_of(s_t[:], ang[:], 0.0, tmp[:], "ang", "tmp", "s_t", i32t[:], t2s[:])
            OP("dve", lambda h: h.tensor_tensor(abi[:], mag[:], s_t[:], ALU.mult), ["mag", "s_t"], ["abi"])
            sin_of(c_[:], ang[:], PI / 2, tmp[:], "ang", "tmp", "c_", i32t[:], t2s[:])
            OP("dve", lambda h: h.tensor_tensor(abr[:], mag[:], c_[:], ALU.mult), ["mag", "c_"], ["abr"])
            OP("dve", lambda h: h.tensor_tensor(den[:], are[:], are[:], ALU.mult), ["setup"], ["den"])
            OP("dve", lambda h: h.tensor_tensor(tmp[:], aim[:], aim[:], ALU.mult), ["setup"], ["tmp"])
            OP("dve", lambda h: h.tensor_tensor(den[:], den[:], tmp[:], ALU.add), ["den", "tmp"], ["den"])
            OP("dve", lambda h: h.reciprocal(den[:], den[:]), ["den"], ["den"])
            OP("dve", lambda h: h.tensor_scalar_add(c_[:], abr[:], -1.0), ["abr"], ["c_"])
            OP("dve", lambda h: h.tensor_tensor(tmp[:], c_[:], are[:], ALU.mult), ["c_"], ["tmp"])
            OP("dve", lambda h: h.tensor_tensor(s_t[:], abi[:], aim[:], ALU.mult), ["abi"], ["s_t"])
            OP("dve", lambda h: h.tensor_tensor(tmp[:], tmp[:], s_t[:], ALU.add), ["tmp", "s_t"], ["tmp"])
            OP("dve", lambda h: h.tensor_tensor(cor[:], tmp[:], den[:], ALU.mult), ["tmp", "den"], ["cor"])
            OP("dve", lambda h: h.tensor_tensor(tmp[:], abi[:], are[:], ALU.mult), ["abi"], ["tmp"])
            OP("dve", lambda h: h.tensor_tensor(s_t[:], c_[:], aim[:], ALU.mult), ["c_"], ["s_t"])
            OP("dve", lambda h: h.tensor_tensor(tmp[:], tmp[:], s_t[:], ALU.subtract), ["tmp", "s_t"], ["tmp"])
            OP("dve", lambda h: h.tensor_tensor(coi[:], tmp[:], den[:], ALU.mult), ["tmp", "den"], ["coi"])
            OP("dve", lambda h: h.tensor_scalar_mul(lm[:], ang[:], 128.0), ["ang"], ["lm"])
            sin_of(rhoi[:], lm[:], 0.0, tmp[:], "lm", "tmp", "rhoi", i32t[:], t2s[:])
            sin_of(rhor[:], lm[:], PI / 2, tmp[:], "lm", "tmp", "rhor", i32t[:], t2s[:])
            p.emit()

        def run_pass(seq):
            sample = seq is None
            T = NS if sample else S
            tiles = [(0, NS)] if sample else [(i * 512, 512) for i in range(NT)]
            col = NSEQ if sample else seq

            def fm_names(ai, ti):
                return [("A", ai, c, ti) for c in range(8)]

            with ExitStack() as st:
                p.stack = st
                if sample:
                    xs_ = p.sb("xs_", [128, 8, NS], F32)
                    p.dma(lambda h: h.dma_start(out=xs_[:], in_=xsT), "xst0", writes=["xs_"])
                    OP("dve", lambda h: h.tensor_tensor(xs_[:], xs_[:], modT[1][:, :, NSEQ:NCOL], ALU.mult), ["xs_"], ["xs_"])
                    OP("dve", lambda h: h.tensor_tensor(A[0][:, :, 0:NS], xs_[:], modT[0][:, :, NSEQ:NCOL], ALU.add), ["xs_"], fm_names(0, 0))
                else:
                    xst = [p.sb("xst%d" % i, [128, S], F32) for i in range(2)]
                    for c in range(8):
                        sl = c % 2
                        p.dma(lambda h, c=c, sl=sl: h.dma_start(out=xst[sl][:], in_=xT[seq, :, c, :]), "xst%d" % sl, writes=[("xst", sl)])
                        OP("dve", lambda h, c=c, sl=sl: h.tensor_scalar(A[0][:, c, :], xst[sl][:], modT[1][:, c, col:col + 1], modT[0][:, c, col:col + 1], ALU.mult, ALU.add),
                           [("xst", sl)], [("A", 0, c, ti) for ti in range(NT)])
                p.emit()

            with ExitStack() as st:
                p.stack = st
                load_w(w_in[:, 4096:5120], 0)

                def epi(c, ti, t0, tw, bs):
                    OP("dve", lambda h: h.tensor_copy(A[1][:, c, t0:t0 + tw], ps[bs[0]][:, 0:tw]), [("ps", bs[0])], [("A", 1, c, ti)])
                proj_fm([0], [(0, A[0])], tiles, epi)
                p.emit()

            with ExitStack() as st:
                p.stack = st
                load_w(w_glu, 0)
                W = 4 * 128
                ctr = p.sb("ctr", [128, 4, 128], F32); cti = p.sb("cti", [128, 4, 128], F32)
                btr = p.sb("btr", [128, 4, 128], F32); bti = p.sb("bti", [128, 4, 128], F32)
                Bre = p.sb("Bre", [128, 4, 128], BF16); Bim = p.sb("Bim", [128, 4, 128], BF16)
                Cr = p.sb("Cr", [128, 4, 128], BF16); nCr = p.sb("nCr", [128, 4, 128], BF16); nCi = p.sb("nCi", [128, 4, 128], BF16)
                tA = p.sb("tA", [128, 4, 128], F32); tB = p.sb("tB", [128, 4, 128], F32)
                tI = p.sb("tI", [128, 4, 128], I32); tC = p.sb("tC", [128, 4, 128], F32)
                ybuf = p.sb("ybuf", [128, 512], F32); y2b = p.sb("y2b", [128, 512], F32); y3b = p.sb("y3b", [128, 512], F32)
                if not sample:
                    cosT = p.sb("cosT", [128, 4, 128], F32); sinT = p.sb("sinT", [128, 4, 128], F32)
                    Str = p.sb("Str", [128, 4, 128], F32); Sti = p.sb("Sti", [128, 4, 128], F32)
                    rbr = [p.sb("rbr%d" % i, [128, 4, 128], F32) for i in range(2)]; rbi = [p.sb("rbi%d" % i, [128, 4, 128], F32) for i in range(2)]
                    tD = p.sb("tD", [128, 4, 128], F32)
                    P = [p.sb("P%d" % i, [128, 4, 128], BF16) for i in range(4)]
                    inr = p.sb("inr", [128, 4], F32); ini = p.sb("ini", [128, 4], F32)
                    sm = [p.sb("sm%d" % i, [128, 4], F32) for i in range(4)]
                else:
                    xr = p.sb("xr", [128, 32, NS], F32); xi = p.sb("xi", [128, 32, NS], F32)
                    xnr = p.sb("xnr", [128, 32, NS], F32); xni = p.sb("xni", [128, 32, NS], F32)
                    bur = p.sb("bur", [128, 4, NS], F32); bui = p.sb("bui", [128, 4, NS], F32)
                    Xrb = p.sb("Xrb", [128, 4, NS], BF16); Xib = p.sb("Xib", [128, 4, NS], BF16)
                    t4a = p.sb("t4a", [128, 4, NS], F32); t4b = p.sb("t4b", [128, 4, NS], F32)
                    p.dma(lambda h: h.dma_start(out=xr[:], in_=sstr_d), "sst0", writes=["xr"])
                    p.dma(lambda h: h.dma_start(out=xi[:], in_=ssti_d), "sst1", writes=["xi"])

                def gelu_store(F, t0, tw):
                    OP("dve", lambda h: h.tensor_tensor(y2b[:, 0:tw], ybuf[:, 0:tw], ybuf[:, 0:tw], ALU.mult), ["ybuf"], ["y2b"])
                    OP("dve", lambda h: h.tensor_scalar(y2b[:, 0:tw], y2b[:, 0:tw], 0.044715, 1.0, ALU.mult, ALU.add), ["y2b"], ["y2b"])
                    OP("dve", lambda h: h.tensor_tensor(y2b[:, 0:tw], y2b[:, 0:tw], ybuf[:, 0:tw], ALU.mult), ["y2b", "ybuf"], ["y2b"])
                    OP("act", lambda h: h.activation(y3b[:, 0:tw], y2b[:, 0:tw], AF.Sigmoid, scale=GK), ["y2b"], ["y3b"])
                    OP("dve", lambda h: h.tensor_tensor(A[2][:, F, t0:t0 + tw], ybuf[:, 0:tw], y3b[:, 0:tw], ALU.mult), ["y3b", "ybuf"],
                       [("A", 2, F, t0 // 512)])

                for F in range(8):
                    G0 = 4 * F
                    cob_r = cor[:, G0:G0 + 4].unsqueeze(2).broadcast_to([128, 4, 128])
                    cob_i = coi[:, G0:G0 + 4].unsqueeze(2).broadcast_to([128, 4, 128])
                    p.dma_multi([lambda h: h.dma_start(out=ctr[:], in_=CTr_d[:, G0:G0 + 4, :]),
                                 lambda h: h.dma_start(out=cti[:], in_=CTi_d[:, G0:G0 + 4, :]),
                                 lambda h: h.dma_start(out=btr[:], in_=BTr_d[:, G0:G0 + 4, :]),
                                 lambda h: h.dma_start(out=bti[:], in_=BTi_d[:, G0:G0 + 4, :])], "ssmtab", writes=["tabs"])
                    OP("pool", lambda h: h.tensor_copy(Bre[:], btr[:]), ["tabs"], ["Bre"])
                    OP("pool", lambda h: h.tensor_copy(Bim[:], bti[:]), ["tabs"], ["Bim"])
                    if not sample:
                        OP("dve", lambda h: h.tensor_tensor(tA[:], ctr[:], cob_r, ALU.mult), ["tabs"], ["tA"])
                        OP("dve", lambda h: h.tensor_tensor(tB[:], cti[:], cob_i, ALU.mult), ["tabs"], ["tB"])
                        OP("dve", lambda h: h.tensor_tensor(Cr[:], tA[:], tB[:], ALU.subtract), ["tA", "tB"], ["Cr"])
                        OP("dve", lambda h: h.tensor_tensor(nCr[:], tB[:], tA[:], ALU.subtract), ["tA", "tB"], ["nCr"])
                        OP("dve", lambda h: h.tensor_tensor(tA[:], ctr[:], cob_i, ALU.mult), ["tabs", "Cr", "nCr"], ["tA"])
                        OP("dve", lambda h: h.tensor_tensor(tB[:], cti[:], cob_r, ALU.mult), ["tabs", "Cr", "nCr"], ["tB"])
                        OP("dve", lambda h: h.tensor_tensor(tA[:], tA[:], tB[:], ALU.add), ["tA", "tB"], ["tA"])
                        OP("dve", lambda h: h.tensor_scalar_mul(nCi[:], tA[:], -1.0), ["tA"], ["nCi"])
                        OP("dve", lambda h: h.tensor_tensor(tA[:], ang[:, G0:G0 + 4].unsqueeze(2).broadcast_to([128, 4, 128]),
                                                            tvec[:, :].unsqueeze(1).broadcast_to([128, 4, 128]), ALU.mult), ["nCi"], ["tA"])
                        sin_of(sinT[:], tA[:], 0.0, tB[:], "tA", "tB", "sinT", tI[:], tC[:], "tC")
                        sin_of(cosT[:], tA[:], PI / 2, tB[:], "tA", "tB", "cosT", tI[:], tC[:], "tC")
                        OP("dve", lambda h: h.memset(inr[:], 0.0), [], ["inr"])
                        OP("dve", lambda h: h.memset(ini[:], 0.0), [], ["ini"])
                        c2 = cosT[:].rearrange("p a b -> p (a b)"); s2 = sinT[:].rearrange("p a b -> p (a b)")
                        scr = [t_[:].rearrange("p a b -> p (a b)") for t_ in (tA, tB, tC, tD)]
                        bR, bI = 6, 7

                        def stA(n):
                            t0 = n * 128
                            ti = t0 // 512
                            k = n % 2
                            for gl in range(4):
                                OP("pe", lambda h, gl=gl: h.matmul(ps[bR][:, gl * 128:(gl + 1) * 128], lhsT=Bre[:, gl, :], rhs=A[1][:, F, t0:t0 + 128], start=True, stop=True),
                                   ["Bre", ("A", 1, F, ti)], [("ps", bR)])
                                OP("pe", lambda h, gl=gl: h.matmul(ps[bI][:, gl * 128:(gl + 1) * 128], lhsT=Bim[:, gl, :], rhs=A[1][:, F, t0:t0 + 128], start=True, stop=True),
                                   ["Bim", ("A", 1, F, ti)], [("ps", bI)])
                            OP("dve", lambda h: h.tensor_tensor(scr[0], ps[bR][:, :], c2, ALU.mult), [("ps", bR), "cosT", "sinT"], ["tA"])
                            OP("dve", lambda h: h.tensor_tensor(scr[1], ps[bI][:, :], s2, ALU.mult), [("ps", bI), "cosT", "sinT"], ["tB"])
                            OP("dve", lambda h: h.tensor_tensor(scr[2], ps[bI][:, :], c2, ALU.mult), [("ps", bI), "cosT", "sinT"], ["tC"])
                            OP("dve", lambda h: h.tensor_tensor(scr[3], ps[bR][:, :], s2, ALU.mult), [("ps", bR), "cosT", "sinT"], ["tD"])
                            OP("pool", lambda h: h.tensor_tensor(rbr[k][:].rearrange("p a b -> p (a b)"), scr[0], scr[1], ALU.add), ["tA", "tB"], [("rbr", k)])
                            OP("dve", lambda h: h.tensor_tensor(rbi[k][:].rearrange("p a b -> p (a b)"), scr[2], scr[3], ALU.subtract), ["tC", "tD"], [("rbi", k)])

                        def stB(n):
                            t0 = n * 128
                            ti = t0 // 512
                            k = n % 2
                            bY = 4 + n % 2
                            if n > 0:
                                lr_ = Str[:, :, 127]; li_ = Sti[:, :, 127]
                                rr = rhor[:, G0:G0 + 4]; ri = rhoi[:, G0:G0 + 4]
                                OP("dve", lambda h: h.tensor_tensor(sm[0][:], rr, lr_, ALU.mult), ["Str"], ["sm0"])
                                OP("dve", lambda h: h.tensor_tensor(sm[1][:], ri, li_, ALU.mult), ["Sti"], ["sm1"])
                                OP("dve", lambda h: h.tensor_tensor(sm[2][:], rr, li_, ALU.mult), ["Sti"], ["sm2"])
                                OP("dve", lambda h: h.tensor_tensor(sm[3][:], ri, lr_, ALU.mult), ["Str"], ["sm3"])
                                OP("dve", lambda h: h.tensor_tensor(inr[:], sm[0][:], sm[1][:], ALU.subtract), ["sm0", "sm1"], ["inr"])
                                OP("dve", lambda h: h.tensor_tensor(ini[:], sm[2][:], sm[3][:], ALU.add), ["sm2", "sm3"], ["ini"])
                            for gl in range(4):
                                G = G0 + gl
                                OP("dve", lambda h, gl=gl, G=G: h.tensor_tensor_scan(Str[:, gl, :], mag[:, G:G + 1].to_broadcast([128, 128]), rbr[k][:, gl, :],
                                                                                      inr[:, gl:gl + 1], ALU.mult, ALU.add), [("rbr", k), "inr"], ["Str"])
                                OP("dve", lambda h, gl=gl, G=G: h.tensor_tensor_scan(Sti[:, gl, :], mag[:, G:G + 1].to_broadcast([128, 128]), rbi[k][:, gl, :],
                                                                                      ini[:, gl:gl + 1], ALU.mult, ALU.add), [("rbi", k), "ini"], ["Sti"])
                            OP("pool", lambda h: h.tensor_tensor(P[0][:], cosT[:], Str[:], ALU.mult), ["Str", "cosT", "sinT"], ["P"])
                            OP("pool", lambda h: h.tensor_tensor(P[1][:], sinT[:], Sti[:], ALU.mult), ["Sti", "cosT", "sinT"], ["P"])
                            OP("pool", lambda h: h.tensor_tensor(P[2][:], cosT[:], Sti[:], ALU.mult), ["Sti", "cosT", "sinT"], ["P"])
                            OP("pool", lambda h: h.tensor_tensor(P[3][:], sinT[:], Str[:], ALU.mult), ["Str", "cosT", "sinT"], ["P"])
                            kq = 0
                            for gl in range(4):
                                for (Wt, Pi) in ((Cr, 0), (nCr, 1), (nCi, 2), (nCi, 3)):
                                    OP("pe", lambda h, gl=gl, Wt=Wt, Pi=Pi, kq=kq: h.matmul(ps[bY][:, 0:128], lhsT=Wt[:, gl, :], rhs=P[Pi][:, gl, :], start=(kq == 0), stop=(kq == 15)),
                                       ["P", "Cr", "nCr", "nCi"], [("ps", bY)])
                                    kq += 1

                        def stEpi(n):
                            t0 = n * 128
                            ti = t0 // 512
                            bY = 4 + n % 2
                            o0 = (n % 4) * 128
                            OP("dve", lambda h: h.scalar_tensor_tensor(ybuf[:, o0:o0 + 128], A[1][:, F, t0:t0 + 128], dTs[:, F:F + 1], ps[bY][:, 0:128], ALU.mult, ALU.add),
                               [("ps", bY), ("A", 1, F, ti)], ["ybuf"])
                            if n % 4 == 3:
                                gelu_store(F, ti * 512, 512)

                        stA(0)
                        for n in range(NB):
                            if n + 1 < NB:
                                stA(n + 1)
                            stB(n)
                            if n > 0:
                                stEpi(n - 1)
                        stEpi(NB - 1)
                        lr_ = Str[:, :, 127]; li_ = Sti[:, :, 127]
                        c1 = cosT[:, :, 127]; s1 = sinT[:, :, 127]
                        OP("dve", lambda h: h.tensor_tensor(sm[0][:], c1, lr_, ALU.mult), ["Str", "cosT", "sinT"], ["sm0"])
                        OP("dve", lambda h: h.tensor_tensor(sm[1][:], s1, li_, ALU.mult), ["Sti", "cosT", "sinT"], ["sm1"])
                        OP("dve", lambda h: h.tensor_tensor(sm[0][:], sm[0][:], sm[1][:], ALU.subtract), ["sm0", "sm1"], ["sm0"])
                        OP("dve", lambda h: h.tensor_tensor(sm[2][:], c1, li_, ALU.mult), ["Sti", "cosT", "sinT"], ["sm2"])
                        OP("dve", lambda h: h.tensor_tensor(sm[3][:], s1, lr_, ALU.mult), ["Str", "cosT", "sinT"], ["sm3"])
                        OP("dve", lambda h: h.tensor_tensor(sm[2][:], sm[2][:], sm[3][:], ALU.add), ["sm2", "sm3"], ["sm2"])
                        cr4 = cor[:, G0:G0 + 4]; ci4 = coi[:, G0:G0 + 4]
                        OP("dve", lambda h: h.tensor_tensor(sm[1][:], cr4, sm[0][:], ALU.mult), ["sm0"], ["sm1"])
                        OP("dve", lambda h: h.tensor_tensor(sm[3][:], ci4, sm[2][:], ALU.mult), ["sm2"], ["sm3"])
                        OP("dve", lambda h: h.tensor_tensor(xendr[:, G0:G0 + 4], sm[1][:], sm[3][:], ALU.subtract), ["sm1", "sm3"], ["xendr"])
                        OP("dve", lambda h: h.tensor_tensor(sm[1][:], cr4, sm[2][:], ALU.mult), ["sm2", "xendr"], ["sm1"])
                        OP("dve", lambda h: h.tensor_tensor(sm[3][:], ci4, sm[0][:], ALU.mult), ["sm0", "xendr"], ["sm3"])
                        OP("dve", lambda h: h.tensor_tensor(xendi[:, G0:G0 + 4], sm[1][:], sm[3][:], ALU.add), ["sm1", "sm3"], ["xendi"])
                    else:
                        OP("pool", lambda h: h.tensor_copy(Cr[:], ctr[:]), ["tabs"], ["Cr"])
                        OP("dve", lambda h: h.tensor_scalar_mul(nCi[:], cti[:], -1.0), ["tabs"], ["nCi"])
                        bR, bI, bY = 6, 7, bank(2, 4)
                        for gl in range(4):
                            OP("pe", lambda h, gl=gl: h.matmul(ps[bR][:, gl * NS:(gl + 1) * NS], lhsT=Bre[:, gl, :], rhs=A[1][:, F, 0:NS], start=True, stop=True),
                               ["Bre", ("A", 1, F, 0)], [("ps", bR)])
                            OP("pe", lambda h, gl=gl: h.matmul(ps[bI][:, gl * NS:(gl + 1) * NS], lhsT=Bim[:, gl, :], rhs=A[1][:, F, 0:NS], start=True, stop=True),
                               ["Bim", ("A", 1, F, 0)], [("ps", bI)])
                        pr = ps[bR][:, 0:4 * NS].rearrange("p (a b) -> p a b", a=4); pi_ = ps[bI][:, 0:4 * NS].rearrange("p (a b) -> p a b", a=4)
                        cbr = cor[:, G0:G0 + 4].unsqueeze(2).broadcast_to([128, 4, NS]); cbi = coi[:, G0:G0 + 4].unsqueeze(2).broadcast_to([128, 4, NS])
                        abr4 = abr[:, G0:G0 + 4].unsqueeze(2).broadcast_to([128, 4, NS]); abi4 = abi[:, G0:G0 + 4].unsqueeze(2).broadcast_to([128, 4, NS])
                        xor_ = xr[:, G0:G0 + 4, :]; xoi_ = xi[:, G0:G0 + 4, :]
                        xnr_ = xnr[:, G0:G0 + 4, :]; xni_ = xni[:, G0:G0 + 4, :]
                        OP("dve", lambda h: h.tensor_tensor(t4a[:], pr, cbr, ALU.mult), [("ps", bR)], ["t4a"])
                        OP("dve", lambda h: h.tensor_tensor(t4b[:], pi_, cbi, ALU.mult), [("ps", bI)], ["t4b"])
                        OP("dve", lambda h: h.tensor_tensor(bur[:], t4a[:], t4b[:], ALU.subtract), ["t4a", "t4b"], ["bur"])
                        OP("dve", lambda h: h.tensor_tensor(t4a[:], pi_, cbr, ALU.mult), [("ps", bI), "bur"], ["t4a"])
                        OP("dve", lambda h: h.tensor_tensor(t4b[:], pr, cbi, ALU.mult), [("ps", bR), "bur"], ["t4b"])
                        OP("dve", lambda h: h.tensor_tensor(bui[:], t4a[:], t4b[:], ALU.add), ["t4a", "t4b"], ["bui"])
                        OP("dve", lambda h: h.tensor_tensor(t4a[:], xor_, abr4, ALU.mult), ["xr", "bui"], ["t4a"])
                        OP("dve", lambda h: h.tensor_tensor(t4b[:], xoi_, abi4, ALU.mult), ["xi", "bui"], ["t4b"])
                        OP("dve", lambda h: h.tensor_tensor(t4a[:], t4a[:], t4b[:], ALU.subtract), ["t4a", "t4b"], ["t4a"])
                        OP("dve", lambda h: h.tensor_tensor(xnr_, t4a[:], bur[:], ALU.add), ["t4a", "bur"], ["xnr"])
                        OP("dve", lambda h: h.tensor_tensor(t4a[:], xoi_, abr4, ALU.mult), ["xi", "xnr"], ["t4a"])
                        OP("dve", lambda h: h.tensor_tensor(t4b[:], xor_, abi4, ALU.mult), ["xr", "xnr"], ["t4b"])
                        OP("dve", lambda h: h.tensor_tensor(t4a[:], t4a[:], t4b[:], ALU.add), ["t4a", "t4b"], ["t4a"])
                        OP("dve", lambda h: h.tensor_tensor(xni_, t4a[:], bui[:], ALU.add), ["t4a", "bui"], ["xni"])
                        OP("pool", lambda h: h.tensor_copy(Xrb[:], xnr_), ["xnr"], ["Xrb"])
                        OP("pool", lambda h: h.tensor_copy(Xib[:], xni_), ["xni"], ["Xib"])
                        for gl in range(4):
                            OP("pe", lambda h, gl=gl: h.matmul(ps[bY][:, 0:NS], lhsT=Cr[:, gl, :], rhs=Xrb[:, gl, :], start=(gl == 0), stop=False),
                               ["Cr", "Xrb"], [("ps", bY)])
                            OP("pe", lambda h, gl=gl: h.matmul(ps[bY][:, 0:NS], lhsT=nCi[:, gl, :], rhs=Xib[:, gl, :], start=False, stop=(gl == 3)),
                               ["nCi", "Xib"], [("ps", bY)])
                        OP("dve", lambda h: h.scalar_tensor_tensor(ybuf[:, 0:NS], A[1][:, F, 0:NS], dTs[:, F:F + 1], ps[bY][:, 0:NS], ALU.mult, ALU.add),
                           [("ps", bY), ("A", 1, F, 0)], ["ybuf"])
                        gelu_store(F, 0, NS)
                if sample:
                    p.dma(lambda h: h.dma_start(out=ssre_o, in_=xnr[:]), "o_ss0", reads=["xnr"], is_out=True)
                    p.dma(lambda h: h.dma_start(out=ssim_o, in_=xni[:]), "o_ss1", reads=["xni"], is_out=True)
                else:
                    p.dma(lambda h: h.dma_start(out=sre_o[seq], in_=xendr[:]), "o_s0", reads=["xendr"], is_out=True)
                    p.dma(lambda h: h.dma_start(out=sim_o[seq], in_=xendi[:]), "o_s1", reads=["xendi"], is_out=True)
                p.emit()

            with ExitStack() as st:
                p.stack = st
                tmp = [p.sb("tmpf%d" % i, [128, 512], F32) for i in range(2)]
                load_w(w_in[:, 5120:6144], 1)

                def epi(c, ti, t0, tw, bs):
                    k = c % 2
                    OP("act", lambda h: h.activation(tmp[k][:, 0:tw], ps[bs[0]][:, 0:tw], AF.Sigmoid, bias=bgluTs[:, c:c + 1]), [("ps", bs[0])], [("tmp", k)])
                    OP("dve", lambda h: h.tensor_tensor(A[1][:, c, t0:t0 + tw], A[2][:, c, t0:t0 + tw], tmp[k][:, 0:tw], ALU.mult),
                       [("tmp", k), ("A", 2, c, ti)], [("A", 1, c, ti)])
                proj_fm([0], [(2, A[2])], tiles, epi)
                p.emit()

            with ExitStack() as st:
                p.stack = st
                tmp = [p.sb("tmpf%d" % i, [128, 512], F32) for i in range(2)]
                load_w(w_so, 0)

                def epi(c, ti, t0, tw, bs):
                    k = c % 2
                    OP("act", lambda h: h.activation(tmp[k][:, 0:tw], ps[bs[0]][:, 0:tw], AF.Sigmoid), [("ps", bs[0])], [("tmp", k)])
                    OP("dve", lambda h: h.tensor_tensor(tmp[k][:, 0:tw], tmp[k][:, 0:tw], ps[bs[0]][:, 0:tw], ALU.mult), [("ps", bs[0]), ("tmp", k)], [("tmp", k)])
                    OP("dve", lambda h: h.tensor_tensor(A[1][:, c, t0:t0 + tw], A[1][:, c, t0:t0 + tw], tmp[k][:, 0:tw], ALU.mult),
                       [("tmp", k)], [("A", 1, c, ti)])
                proj_fm([1], [(0, A[0])], tiles, epi)
                p.emit()

            with ExitStack() as st:
                p.stack = st
                tmp = [p.sb("tmpf%d" % i, [128, 512], F32) for i in range(2)]
                load_w(w_in[:, 7168:8192], 1)

                def epi(c, ti, t0, tw, bs):
                    k = c % 2
                    OP("act", lambda h: h.activation(tmp[k][:, 0:tw], ps[bs[1]][:, 0:tw], AF.Sigmoid), [("ps", bs[1])], [("tmp", k)])
                    OP("dve", lambda h: h.tensor_tensor(A[2][:, c, t0:t0 + tw], ps[bs[0]][:, 0:tw], tmp[k][:, 0:tw], ALU.mult),
                       [("tmp", k), ("ps", bs[0])], [("A", 2, c, ti)])
                proj_fm([0, 1], [(1, A[1]), (0, A[0])], tiles, epi)
                p.emit()

            if sample:
                sample_attention()
            else:
                prompt_attention(seq)

            with ExitStack() as st:
                p.stack = st
                tmp = [p.sb("tmpf%d" % i, [128, 512], F32) for i in range(2)]

                def epi(c, ti, t0, tw, bs):
                    k = c % 2
                    OP("act", lambda h: h.activation(tmp[k][:, 0:tw], ps[bs[0]][:, 0:tw], AF.Sigmoid), [("ps", bs[0])], [("tmp", k)])
                    OP("dve", lambda h: h.tensor_tensor(tmp[k][:, 0:tw], tmp[k][:, 0:tw], ps[bs[0]][:, 0:tw], ALU.mult), [("ps", bs[0]), ("tmp", k)], [("tmp", k)])
                    OP("dve", lambda h: h.tensor_tensor(A[1][:, c, t0:t0 + tw], A[1][:, c, t0:t0 + tw], tmp[k][:, 0:tw], ALU.mult),
                       [("tmp", k)], [("A", 1, c, ti)])
                proj_fm([0], [(0, A[0])], tiles, epi)
                p.emit()

            with ExitStack() as st:
                p.stack = st
                tmp = [p.sb("tmpf%d" % i, [128, 512], F32) for i in range(2)]
                load_w(w_in[:, 6144:7168], 0)

                def epi(c, ti, t0, tw, bs):
                    k = c % 2
                    OP("act", lambda h: h.activation(tmp[k][:, 0:tw], ps[bs[1]][:, 0:tw], AF.Sigmoid), [("ps", bs[1])], [("tmp", k)])
                    OP("dve", lambda h: h.tensor_tensor(tmp[k][:, 0:tw], ps[bs[0]][:, 0:tw], tmp[k][:, 0:tw], ALU.mult),
                       [("tmp", k), ("ps", bs[0])], [("tmp", k)])
                    OP("dve", lambda h: h.tensor_tensor(A[2][:, c, t0:t0 + tw], A[2][:, c, t0:t0 + tw], tmp[k][:, 0:tw], ALU.add),
                       [("tmp", k)], [("A", 2, c, ti)])
                proj_fm([1, 0], [(1, A[1]), (0, A[0])], tiles, epi)
                p.emit()

            with ExitStack() as st:
                p.stack = st
                xt = [p.sb("xt%d" % i, [128, 1024], F32) for i in range(2)]
                rr = [p.sb("rr%d" % i, [128, 1024], F32) for i in range(2)]
                stats = p.sb("stats", [128, 2, 6], F32); mv = p.sb("mv", [128, 2], F32); rstd = p.sb("rstd", [128, 1], F32)
                lng = p.sb("lngs", [128, 1024], F32); lnb = p.sb("lnbs", [128, 1024], F32)
                p.dma_multi([lambda h: h.dma_start(out=lng[:], in_=lng_d), lambda h: h.dma_start(out=lnb[:], in_=lnb_d)], "lnld", writes=["ln"])
                load_w(w_out, 0)
                ntile = 1 if sample else NB
                for i in range(ntile):
                    n = NS if sample else 128
                    k = i % 2
                    ti = 0 if sample else (i * 128) // 512
                    src = xstok if sample else xtok[seq, i * 128:(i + 1) * 128, :]
                    p.dma(lambda h, k=k, src=src, n=n: h.dma_start(out=xt[k][0:n, :], in_=src), "xt%d" % k, writes=[("xt", k)])
                    for half in range(2):
                        b = bank(4)
                        for kc in range(8):
                            OP("pe", lambda h, b=b, kc=kc, half=half, i=i, n=n: h.matmul(ps[b][0:n, :], lhsT=A[2][:, kc, i * 128:i * 128 + n],
                                                                                          rhs=wbf[0][:, kc, half * 512:(half + 1) * 512], start=(kc == 0), stop=(kc == 7)),
                               [("A", 2, kc, ti)] + [("wbf", 0, q) for q in range(half * 4, half * 4 + 4)], [("ps", b)])
                        g_ap = gate_s[0:n, half * 512:(half + 1) * 512] if sample else grow[:, seq, half * 512:(half + 1) * 512]
                        OP("dve", lambda h, b=b, k=k, half=half, n=n, g_ap=g_ap: h.tensor_tensor(rr[k][0:n, half * 512:(half + 1) * 512], ps[b][0:n, :], g_ap, ALU.mult),
                           [("ps", b)], [("rr", k)])
                    OP("dve", lambda h, k=k, n=n: h.scalar_tensor_tensor(rr[k][0:n, :], xt[k][0:n, :], ALPHA, rr[k][0:n, :], ALU.mult, ALU.add), [("xt", k), ("rr", k)], [("rr", k)])
                    for half in range(2):
                        OP("dve", lambda h, k=k, n=n, half=half: h.bn_stats(stats[0:n, half, :], rr[k][0:n, half * 512:(half + 1) * 512]), [("rr", k)], ["stats"])
                    OP("dve", lambda h, n=n: h.bn_aggr(mv[0:n, :], stats[0:n, :, :].rearrange("p a b -> p (a b)")), ["stats"], ["mv"])
                    OP("dve", lambda h, n=n: h.tensor_scalar_add(rstd[0:n, :], mv[0:n, 1:2], LN_EPS), ["mv"], ["rstd"])
                    OP("act", lambda h, n=n: h.activation(rstd[0:n, :], rstd[0:n, :], AF.Ln), ["rstd"], ["rstd"])
                    OP("act", lambda h, n=n: h.activation(rstd[0:n, :], rstd[0:n, :], AF.Exp, scale=-0.5), ["rstd"], ["rstd"])
                    OP("dve", lambda h, k=k, n=n: h.tensor_scalar(rr[k][0:n, :], rr[k][0:n, :], mv[0:n, 0:1], rstd[0:n, 0:1], ALU.subtract, ALU.mult), ["mv", "rstd", ("rr", k)], [("rr", k)])
                    OP("dve", lambda h, k=k, n=n: h.tensor_tensor(rr[k][0:n, :], rr[k][0:n, :], lng[0:n, :], ALU.mult), [("rr", k), "ln"], [("rr", k)])
                    OP("pool", lambda h, k=k, n=n: h.tensor_tensor(rr[k][0:n, :], rr[k][0:n, :], lnb[0:n, :], ALU.add), [("rr", k), "ln"], [("rr", k)])
                    dst = ys_o if sample else y_o[seq, i * 128:(i + 1) * 128, :]
                    p.dma(lambda h, k=k, n=n, dst=dst: h.dma_start(out=dst, in_=rr[k][0:n, :]), "o_y%d" % k, reads=[("rr", k)], is_out=True)
                p.emit()

        def prompt_attention(seq):
            with ExitStack() as st:
                p.stack = st
                wq = [p.sb("wq%d" % i, [128, 8, 128], BF16) for i in range(3)]
                qT = p.sb("qTp", [128, S], BF16); kT = p.sb("kTp", [128, S], BF16)
                vb = p.sb("vbp", [128, NB, 128], BF16)
                kst = p.sb("kst", [128, 4, 128], F32); vst = p.sb("vst", [128, 4, 128], F32)
                E = [p.sb("E%d" % i, [128, 512], F32) for i in range(4)]
                SP = [p.sb("SP%d" % i, [128, 512], BF16) for i in range(4)]
                EC = [p.sb("EC%d" % i, [128, 512], F32) for i in range(2)]
                Wb = [p.sb("Wb%d" % i, [128, 512], BF16) for i in range(2)]
                for c in range(8):
                    if c == 1:
                        load_w(w_in[:, 3072:4096], 0)
                        load_w(w_ao, 1)
                    for j in range(3):
                        sl = j % 2
                        col0 = j * 1024 + c * 128
                        p.dma(lambda h, sl=sl, col0=col0: h.dma_start(out=stg[sl][:], in_=w_in[:, col0:col0 + 128].rearrange("(kc p) n -> p kc n", p=128)),
                              "stg%d" % sl, writes=[("stg", sl)])
                        OP("pool", lambda h, sl=sl, j=j: h.tensor_copy(wq[j][:], stg[sl][:]), [("stg", sl)], [("wq", j)])
                    for j, dst in ((0, qT), (1, kT)):
                        for ti in range(NT):
                            b = bank(4)
                            for kc in range(8):
                                OP("pe", lambda h, b=b, kc=kc, j=j, ti=ti: h.matmul(ps[b][:, :], lhsT=wq[j][:, kc, :], rhs=A[0][:, kc, ti * 512:(ti + 1) * 512],
                                                                                  start=(kc == 0), stop=(kc == 7)), [("wq", j), ("A", 0, kc, ti)], [("ps", b)])
                            OP("dve", lambda h, b=b, dst=dst, ti=ti: h.tensor_copy(dst[:, ti * 512:(ti + 1) * 512], ps[b][:, :]), [("ps", b)], [("qk", j)])
                    for j, stb, dro in ((1, kst, kp_o), (2, vst, vp_o)):
                        for i4 in range(NB // 4):
                            b = bank(4)
                            for bl in range(4):
                                i = i4 * 4 + bl
                                for kc in range(8):
                                    OP("pe", lambda h, b=b, kc=kc, j=j, i=i, bl=bl: h.matmul(ps[b][:, bl * 128:(bl + 1) * 128], lhsT=A[0][:, kc, i * 128:(i + 1) * 128],
                                                                                            rhs=wq[j][:, kc, :], start=(kc == 0), stop=(kc == 7)),
                                       [("wq", j), ("A", 0, kc, i // 4)], [("ps", b)])
                            OP("dve", lambda h, b=b, stb=stb: h.tensor_copy(stb[:].rearrange("p a b -> p (a b)"), ps[b][:, :]), [("ps", b)], [("st", j)])
                            if j == 2:
                                OP("dve", lambda h, b=b, i4=i4: h.tensor_copy(vb[:, i4 * 4:(i4 + 1) * 4, :].rearrange("p a b -> p (a b)"), ps[b][:, :]), [("ps", b)], ["vb"])
                            p.dma(lambda h, stb=stb, dro=dro, i4=i4: h.dma_start(
                                out=dro[seq, i4 * 512:(i4 + 1) * 512, c * 128:(c + 1) * 128].rearrange("(a p) n -> p a n", p=128), in_=stb[:]),
                                "o_kv%d" % j, reads=[("st", j)], is_out=True)
                    for Tq in range(NT):
                        psS = (0, 1); acc = (2, 3); psO = 4
                        for hd in range(2):
                            hp = hd * 64
                            OP("pe", lambda h, hd=hd: h.matmul(ps[acc[hd]][:, :], lhsT=zbf[:, :], rhs=qT[:, 0:512], start=True, stop=True), ["zbf", ("qk", 0)], [("ps", acc[hd])])
                            OP("pe", lambda h, hp=hp: h.matmul(ps[psO][hp:hp + 64, :], lhsT=zbf[:, 0:64], rhs=qT[:, 0:512], start=True, stop=True), ["zbf", ("qk", 0)], [("psO", hd)])
                        def geo(j):
                            dj = j - 4 * Tq
                            c0 = max(0, 128 * dj)
                            return dj, c0, 512 * Tq + c0, 512 - c0

                        def stS(j):
                            dj, c0, q0, cw = geo(j)
                            for hd in range(2):
                                hp = hd * 64
                                OP("pe", lambda h, hd=hd, hp=hp: h.matmul(ps[psS[hd]][:, c0:512], lhsT=kT[hp:hp + 64, j * 128:(j + 1) * 128], rhs=qT[hp:hp + 64, q0:q0 + cw],
                                                                       start=True, stop=True), [("qk", 0), ("qk", 1)], [("ps", psS[hd])])

                        def stE(j):
                            dj, c0, q0, cw = geo(j)
                            kb = j % 2
                            for hd in range(2):
                                Eb = E[2 * kb + hd]; SPb = SP[2 * kb + hd]
                                OP("act", lambda h, hd=hd, Eb=Eb: h.activation(Eb[:, c0:512], ps[psS[hd]][:, c0:512], AF.Exp, bias=sbb[:, 2 * c + hd:2 * c + hd + 1], scale=0.125),
                                   [("ps", psS[hd])], [("E", kb, hd)])
                                if dj >= 0:
                                    OP("dve", lambda h, Eb=Eb: h.tensor_tensor(Eb[:, c0:c0 + 128], Eb[:, c0:c0 + 128], mask01[:, :], ALU.mult), [("E", kb, hd)], [("E", kb, hd)])
                                OP("act", lambda h, Eb=Eb, SPb=SPb: h.activation(SPb[:, c0:512], Eb[:, c0:512], AF.Ln, bias=1.0), [("E", kb, hd)], [("SP", kb, hd)])

                        def stU(j):
                            dj, c0, q0, cw = geo(j)
                            kb = j % 2
                            for hd in range(2):
                                SPb = SP[2 * kb + hd]
                                OP("pe", lambda h, hd=hd, SPb=SPb: h.matmul(ps[acc[hd]][:, c0:512], lhsT=Ubf[:, :], rhs=SPb[:, c0:512], start=False, stop=True, skip_group_check=True),
                                   [("SP", kb, hd)], [("ps", acc[hd])])

                        def stC(j):
                            dj, c0, q0, cw = geo(j)
                            for hd in range(2):
                                OP("act", lambda h, hd=hd: h.activation(EC[hd][:, c0:512], ps[acc[hd]][:, c0:512], AF.Exp, scale=-1.0), [("ps", acc[hd])], [("EC", hd)])

                        def stW(j):
                            dj, c0, q0, cw = geo(j)
                            kb = j % 2
                            for hd in range(2):
                                hp = hd * 64
                                Eb = E[2 * kb + hd]; SPb = SP[2 * kb + hd]
                                OP("pe", lambda h, hd=hd, SPb=SPb: h.matmul(ps[acc[hd]][:, c0:512], lhsT=Lcbf[:, :], rhs=SPb[:, c0:512], start=False, stop=True, skip_group_check=True),
                                   [("SP", kb, hd)], [("ps", acc[hd])])
                                OP("dve", lambda h, hd=hd, Eb=Eb: h.tensor_tensor(Wb[hd][:, c0:512], Eb[:, c0:512], EC[hd][:, c0:512], ALU.mult), [("E", kb, hd), ("EC", hd)], [("Wb", hd)])
                                OP("pe", lambda h, hd=hd, hp=hp: h.matmul(ps[psO][hp:hp + 64, c0:512], lhsT=vb[:, j, hd * 64:(hd + 1) * 64], rhs=Wb[hd][:, c0:512],
                                                                       start=False, stop=True, skip_group_check=True), [("Wb", hd), "vb"], [("psO", hd)])

                        jtop = 4 * Tq + 3
                        stS(jtop)
                        stE(jtop)
                        for j in range(jtop, -1, -1):
                            stU(j)
                            if j > 0:
                                stS(j - 1)
                            stC(j)
                            if j > 0:
                                stE(j - 1)
                            stW(j)
                        OP("dve", lambda h, Tq=Tq: h.tensor_copy(A[1][:, c, Tq * 512:(Tq + 1) * 512], ps[psO][:, :]), [("psO", 0), ("psO", 1)], [("A", 1, c, Tq)])
                p.emit()

        def sample_attention():
            GS = 4 if NS >= 4 else NS
            with ExitStack() as st:
                p.stack = st
                qtok = p.sb("qtok", [128, 1024], F32)
                oh = p.sb("oh", [128, 128], F32)
                ptb = p.sb("ptb", [128, NS * NPG], I32); idx = p.sb("idx", [128, NS * NPG], I32)
                iop = p.sb("iop", [128, 1], F32)
                U32 = p.sb("U32s", [128, 128], F32); ones32 = p.sb("ones32s", [128, 128], F32)
                bmask = p.sb("bmasks", [16, 1024], F32)
                qbs = [p.sb("qb%d" % i, [128, 1024], F32) for i in range(2)]
                pg_ = [p.sb("pg%d" % i, [128, 1024], F32) for i in range(3)]
                prod = [p.sb("prod%d" % i, [128, 1024], F32) for i in range(1)]
                sc = p.sb("sc", [128, GS, NPG, 16], F32)
                E8 = sc; SP8 = p.sb("SP8", [128, GS, NPG, 16], F32)
                WL = p.sb("WL", [128, GS, NPG, 16], F32)
                car = p.sb("car", [128, NPG, 16], F32); cum = p.sb("cum", [128, NPG, 16], F32)
                om = p.sb("om", [16, 1024], F32)
                p.dma_multi([lambda h: h.dma_start(out=ptb[:], in_=pt_d.partition_broadcast(128)),
                             lambda h: h.dma_start(out=iop[:], in_=iotap_d), lambda h: h.dma_start(out=U32[:], in_=U32_d),
                             lambda h: h.dma_start(out=ones32[:], in_=ones32_d), lambda h: h.dma_start(out=bmask[:], in_=bmask_d)], "sset", writes=["sset"])
                OP("dve", lambda h: h.tensor_scalar(idx[:], ptb[:], 128.0, iop[:, 0:1], ALU.mult, ALU.add), ["sset"], ["idx"])
                for j, (dst, dro) in enumerate(((qtok, None), (prod[0], ks_o), (prod[0], vs_o))):
                    load_w(w_in[:, j * 1024:(j + 1) * 1024], j % 2)
                    for half in range(2):
                        b = bank(4)
                        for kc in range(8):
                            OP("pe", lambda h, b=b, kc=kc, j=j, half=half: h.matmul(ps[b][0:NS, :], lhsT=A[0][:, kc, 0:NS], rhs=wbf[j % 2][:, kc, half * 512:(half + 1) * 512],
                                                                                  start=(kc == 0), stop=(kc == 7)),
                               [("A", 0, kc, 0)] + [("wbf", j % 2, q) for q in range(half * 4, half * 4 + 4)], [("ps", b)])
                        OP("dve", lambda h, b=b, dst=dst, half=half: h.tensor_copy(dst[0:NS, half * 512:(half + 1) * 512], ps[b][0:NS, :]), [("ps", b)], [("tok", False) if j == 0 else ("prod", 0)])
                    if dro is not None:
                        p.dma(lambda h, dst=dst, dro=dro: h.dma_start(out=dro, in_=dst[0:NS, :]), "o_kvs", reads=[("prod", 0)], writes=[], is_out=True)
                for g0 in range(0, NS, GS):
                    def qb_for(bl):
                        b_ = g0 + bl
                        k2 = 0
                        OP("dve", lambda h, b_=b_: h.tensor_scalar(oh[0:NS, :], iop[0:NS, 0:1].to_broadcast([NS, 128]), float(b_), None, ALU.is_equal), ["sset"], ["oh"])
                        for half in range(2):
                            bq = 6 + half
                            OP("pe", lambda h, bq=bq, b_=b_, half=half: h.matmul(ps[bq][:, :], lhsT=oh[0:NS, :], rhs=qtok[0:NS, half * 512:(half + 1) * 512], start=True, stop=True),
                               ["oh", ("tok", False)], [("ps", bq)])
                            OP("dve", lambda h, bq=bq, bl=bl, half=half: h.tensor_copy(qbs[bl % 2][:, half * 512:(half + 1) * 512], ps[bq][:, :]), [("ps", bq)], [("qbs", bl % 2)])
                        return None
                    NBUF, PF = 3, 2
                    pages = [(bl, pgi) for bl in range(GS) for pgi in range(NPG)]

                    def issue(src, ix):
                        bl, pgi = pages[ix]
                        e = (g0 + bl) * NPG + pgi
                        k3 = ix % NBUF
                        p.dma(lambda h: h.indirect_dma_start(out=pg_[k3][:], out_offset=None, in_=src,
                                                              in_offset=bass.IndirectOffsetOnAxis(ap=idx[:, e:e + 1], axis=0)),
                              "pg%d" % k3, reads=["idx"], writes=[("pg", k3)], q="pool")

                    for ix in range(min(PF, len(pages))):
                        issue(ck, ix)
                    for ix, (bl, pgi) in enumerate(pages):
                        k3 = ix % NBUF
                        if pgi == 0:
                            qb_for(bl)
                        if ix + PF < len(pages):
                            issue(ck, ix + PF)
                        OP("dve", lambda h: h.tensor_tensor(prod[0][:], pg_[k3][:], qbs[bl % 2][:], ALU.mult), [("pg", k3), ("qbs", bl % 2)], [("prod", 0)])
                        OP("dve", lambda h: h.tensor_reduce(sc[:, bl, pgi, :], prod[0][:].rearrange("p (a b) -> p a b", a=16), AX.X, ALU.add),
                           [("prod", 0)], ["sc"])
                    sc3 = sc[:].rearrange("p a b c -> p (a b) c")
                    OP("dve", lambda h: h.scalar_tensor_tensor(E8[:].rearrange("p a b c -> p (a b) c"), sc3, 0.125, sbb[:, :].unsqueeze(1).broadcast_to([128, GS * NPG, 16]), ALU.mult, ALU.add),
                       ["sc"], ["sc"])
                    E8f = E8[:].rearrange("p a b c -> p (a b c)"); SP8f = SP8[:].rearrange("p a b c -> p (a b c)")
                    OP("act", lambda h: h.activation(E8f, E8f, AF.Exp), ["sc"], ["sc"])
                    OP("act", lambda h: h.activation(SP8f, E8f, AF.Ln, bias=1.0), ["sc"], ["SP8"])
                    for bl in range(GS):
                        bC, bT = 4, 5
                        spb = SP8[:, bl, :, :].rearrange("p b c -> p (b c)")
                        OP("pe", lambda h, spb=spb: h.matmul(ps[bC][:, 0:NPG * 16], lhsT=U32[:, :], rhs=spb, start=True, stop=True), ["SP8", "sset"], [("ps", bC)])
                        OP("pe", lambda h, spb=spb: h.matmul(ps[bT][:, 0:NPG * 16], lhsT=ones32[:, :], rhs=spb, start=True, stop=True), ["SP8", "sset"], [("ps", bT)])
                        tot = ps[bT][:, 0:NPG * 16].rearrange("p (b c) -> p b c", c=16)
                        OP("dve", lambda h: h.memset(car[:, NPG - 1, :], 0.0), [], ["car"])
                        for pgi in range(NPG - 2, -1, -1):
                            OP("dve", lambda h, pgi=pgi, tot=tot: h.tensor_tensor(car[:, pgi, :], car[:, pgi + 1, :], tot[:, pgi + 1, :], ALU.add), ["car", ("ps", bT)], ["car"])
                        OP("dve", lambda h: h.tensor_tensor(cum[:].rearrange("p b c -> p (b c)"), ps[bC][:, 0:NPG * 16], car[:].rearrange("p b c -> p (b c)"), ALU.add),
                           [("ps", bC), "car"], ["cum"])
                        OP("act", lambda h: h.activation(cum[:].rearrange("p b c -> p (b c)"), cum[:].rearrange("p b c -> p (b c)"), AF.Exp, scale=-1.0), ["cum"], ["cum"])
                        OP("dve", lambda h, bl=bl: h.tensor_tensor(WL[:, bl, :, :], E8[:, bl, :, :], cum[:], ALU.mult), ["cum", "sc"], ["WL"])
                    for ix in range(min(PF, len(pages))):
                        issue(cv, ix)
                    for bl in range(GS):
                        b_ = g0 + bl
                        for pgi in range(NPG):
                            ix = bl * NPG + pgi
                            k3 = ix % NBUF
                            if ix + PF < len(pages):
                                issue(cv, ix + PF)
                            for half in range(2):
                                OP("pe", lambda h, half=half: h.matmul(ps[4 + half][0:16, :], lhsT=WL[:, bl, pgi, :], rhs=pg_[k3][:, half * 512:(half + 1) * 512],
                                                                     start=(pgi == 0), stop=(pgi == NPG - 1)), [("pg", k3), "WL"], [("ps", 4 + half)])
                        for half in range(2):
                            OP("dve", lambda h, half=half: h.tensor_tensor(om[:, half * 512:(half + 1) * 512], ps[4 + half][0:16, :], bmask[:, half * 512:(half + 1) * 512], ALU.mult),
                               [("ps", 4 + half), "sset"], ["om"])
                        for c8 in range(8):
                            OP("pe", lambda h, c8=c8, b_=b_: h.matmul(ps[3][:, c8 * NS + b_:c8 * NS + b_ + 1], lhsT=om[:, c8 * 128:(c8 + 1) * 128], rhs=ones32[0:16, 0:1], start=True, stop=True),
                               ["om", "sset"], [("ps", 3)])
                OP("dve", lambda h: h.tensor_copy(A[1][:, :, 0:NS], ps[3][:, 0:8 * NS].rearrange("p (a b) -> p a b", a=8)), [("ps", 3)], [("A", 1, c, 0) for c in range(8)])
                load_w(w_in[:, 3072:4096], 0)
                load_w(w_ao, 1)
                p.emit()

        for seq in range(NSEQ):
            run_pass(seq)
        run_pass(None)
        p.emit(final=True)
    return nc


NCORES = 8
_CACHE = {}


def _fm(v):
    return np.ascontiguousarray(v.reshape(8, 128).T)


def kernel(x_prompt, x_sample, c_prompt, c_sample, cache_k, cache_v, state_ssm_re, state_ssm_im, page_table,
           w_cond, b_cond, w_in, sb_bias, ssm_a_re, ssm_a_im, ssm_log_dt, ssm_b_re, ssm_b_im, ssm_c_re, ssm_c_im,
           ssm_d, w_glu, b_glu, w_att_out, w_ssm_out, w_out, ln_g, ln_b):
    f = np.float32
    A_ = lambda a: np.ascontiguousarray(np.asarray(a))
    x_prompt = A_(x_prompt); x_sample = A_(x_sample); c_prompt = A_(c_prompt); c_sample = A_(c_sample)
    B, S, D = x_prompt.shape
    DB = x_sample.shape[0]
    NPG = page_table.shape[1]
    NPHYS = cache_k.shape[1]
    ncores = min(NCORES, B)
    NSEQ = B // ncores
    NS = DB // ncores
    key = (S, NSEQ, NS, NPG, NPHYS)
    if key not in _CACHE:
        _CACHE[key] = build(*key)
    nc = _CACHE[key]
    bf = ml_dtypes.bfloat16
    kk = np.arange(128)
    Ubf = (kk[:, None] >= kk[None, :]).astype(f)
    consts = {
        "Ubf": Ubf.astype(bf), "Lcbf": (kk[:, None] < kk[None, :]).astype(f).astype(bf),
        "mask01": (kk[:, None] < kk[None, :]).astype(f), "tvec": np.tile(np.arange(128, dtype=f)[None, :], (128, 1)),
        "iotap": np.arange(128, dtype=f).reshape(128, 1), "U32": Ubf, "ones32": np.ones((128, 128), f),
        "bmask": np.kron(np.eye(16, dtype=f), np.ones((1, 64), f)),
    }
    bc = A_(b_cond)[0]
    are = A_(ssm_a_re)[0].reshape(32, 128).T
    aim = A_(ssm_a_im)[0].reshape(32, 128).T
    ldt = np.repeat(A_(ssm_log_dt)[0].reshape(32, 2, 1), 64, axis=2).reshape(32, 128).T
    bre, bim, cre, cim = A_(ssm_b_re)[0], A_(ssm_b_im)[0], A_(ssm_c_re)[0], A_(ssm_c_im)[0]
    BTr = np.zeros((128, 32, 128), f); BTi = np.zeros((128, 32, 128), f)
    CTr = np.zeros((128, 32, 128), f); CTi = np.zeros((128, 32, 128), f)
    for G in range(32):
        for g2 in range(2):
            g = 2 * G + g2
            r0 = 32 * (G % 4) + 16 * g2
            BTr[r0:r0 + 16, G, g2 * 64:(g2 + 1) * 64] = bre[g].T
            BTi[r0:r0 + 16, G, g2 * 64:(g2 + 1) * 64] = bim[g].T
            CTr[g2 * 64:(g2 + 1) * 64, G, r0:r0 + 16] = cre[g].T
            CTi[g2 * 64:(g2 + 1) * 64, G, r0:r0 + 16] = cim[g].T
    shared = dict(consts)
    shared.update({
        "w_cond": A_(w_cond)[0], "bcT": np.ascontiguousarray(bc.reshape(24, 128).T), "bgrow": np.tile(bc[None, 2048:3072], (128, 1)),
        "w_in": A_(w_in)[0], "w_glu": A_(w_glu)[0], "w_ao": A_(w_att_out)[0], "w_so": A_(w_ssm_out)[0], "w_out": A_(w_out)[0],
        "bgluT": _fm(A_(b_glu)[0]), "dT": _fm(A_(ssm_d)[0]), "sbb": np.tile(A_(sb_bias)[0][None, :], (128, 1)),
        "a_re": np.ascontiguousarray(are), "a_im": np.ascontiguousarray(aim), "ldt": np.ascontiguousarray(ldt),
        "BTr": BTr, "BTi": BTi, "CTr": CTr, "CTi": CTi,
        "lng": np.tile(A_(ln_g)[0][None, :], (128, 1)), "lnb": np.tile(A_(ln_b)[0][None, :], (128, 1)),
        "cache_k": A_(cache_k)[0].reshape(NPHYS * 128, 1024), "cache_v": A_(cache_v)[0].reshape(NPHYS * 128, 1024),
    })
    shared = {k: np.ascontiguousarray(v) for k, v in shared.items()}
    pt = A_(page_table).astype(np.int32)
    sre_in, sim_in = A_(state_ssm_re)[0], A_(state_ssm_im)[0]
    in_maps = []
    for i in range(ncores):
        seqs = list(range(i * NSEQ, (i + 1) * NSEQ))
        sm = slice(i * NS, (i + 1) * NS)
        m = dict(shared)
        m["xT"] = np.ascontiguousarray(np.stack([x_prompt[s].T.reshape(8, 128, S).transpose(1, 0, 2) for s in seqs]))
        m["xtok"] = np.ascontiguousarray(x_prompt[seqs])
        cols = [c_prompt[s] for s in seqs] + [c_sample[b] for b in range(i * NS, (i + 1) * NS)]
        m["cT"] = np.ascontiguousarray(np.stack([_fm(v) for v in cols], axis=2))
        m["crep"] = np.ascontiguousarray(np.stack([np.repeat(_fm(c_prompt[s])[:, :, None], 128, axis=2) for s in seqs]))
        xs = x_sample[sm, 0, :]
        m["xsT"] = np.ascontiguousarray(np.stack([_fm(v) for v in xs], axis=2))
        m["xstok"] = np.ascontiguousarray(xs)
        m["pt"] = np.ascontiguousarray(pt[sm].reshape(1, -1))
        m["sst_re"] = np.ascontiguousarray(sre_in[sm].reshape(NS, 32, 128).transpose(2, 1, 0))
        m["sst_im"] = np.ascontiguousarray(sim_in[sm].reshape(NS, 32, 128).transpose(2, 1, 0))
        in_maps.append(m)
    res = run_bass_kernel_spmd(nc, in_maps, core_ids=list(range(ncores)))
    R = res.results
    H, Dh = 16, 64
    y = np.concatenate([r["y"] for r in R], 0)
    kp = np.concatenate([r["kp"] for r in R], 0).reshape(1, B, S, H, Dh)
    vp = np.concatenate([r["vp"] for r in R], 0).reshape(1, B, S, H, Dh)
    sre = np.concatenate([r["sre"].transpose(0, 2, 1).reshape(NSEQ, 64, 64) for r in R], 0)[None]
    sim = np.concatenate([r["sim"].transpose(0, 2, 1).reshape(NSEQ, 64, 64) for r in R], 0)[None]
    ys = np.concatenate([r["ys"] for r in R], 0).reshape(DB, 1, D)
    ks = np.concatenate([r["ks"] for r in R], 0).reshape(1, DB, 1, H, Dh)
    vs = np.concatenate([r["vs"] for r in R], 0).reshape(1, DB, 1, H, Dh)
    ssre = np.concatenate([r["ssre"].transpose(2, 1, 0).reshape(NS, 64, 64) for r in R], 0)[None]
    ssim = np.concatenate([r["ssim"].transpose(2, 1, 0).reshape(NS, 64, 64) for r in R], 0)[None]
    outs = (y, ys, kp, vp, sre, sim, ks, vs, ssre, ssim)
    return tuple(np.ascontiguousarray(o.astype(np.float32)) for o in outs)
```

```python
import math
import ml_dtypes
from concourse.bass_utils import run_bass_kernel_spmd
import numpy as np
from contextlib import ExitStack
import concourse.bass as bass
import concourse.mybir as mybir

F32 = mybir.dt.float32
BF16 = mybir.dt.bfloat16
I32 = mybir.dt.int32
AF = mybir.ActivationFunctionType
ALU = mybir.AluOpType
AX = mybir.AxisListType


import types


def _snap(fn):
    if fn.__closure__ is None:
        return fn
    cells = []
    for c in fn.__closure__:
        try:
            cells.append(types.CellType(c.cell_contents))
        except ValueError:
            cells.append(c)
    return types.FunctionType(fn.__code__, fn.__globals__, fn.__name__, fn.__defaults__, tuple(cells))


class _Op:
    __slots__ = ("fn", "waits", "dwaits", "idx", "dma", "milestone")

    def __init__(self, fn, idx, dma=None):
        self.fn = _snap(fn)
        self.waits = []
        self.dwaits = []
        self.idx = idx
        self.dma = dma
        self.milestone = False


class Prog:
    ENGS = ("pe", "act", "dve", "pool", "sp")

    def __init__(self, nc, stack):
        self.nc = nc
        self.stack = stack
        self.gstack = stack
        self.h = {"pe": nc.tensor, "act": nc.scalar, "dve": nc.vector,
                  "pool": nc.gpsimd, "sp": nc.sync}
        self.ops = {e: [] for e in self.ENGS}
        self.seen = {e: {} for e in self.ENGS}
        self.last_w = {}
        self.readers = {}
        self.dsem_cnt = {}
        self.out_dma_keys = set()
        self.same_engine_sync = {"act", "dve", "pool"}

    def sb(self, name, shape, dt):
        self._uid = getattr(self, "_uid", 0) + 1
        return self.stack.enter_context(self.nc.sbuf_tensor("s%d_%s" % (self._uid, name), list(shape), dt))

    def ps(self, name, shape, dt=F32):
        return self.stack.enter_context(self.nc.psum_tensor(name, list(shape), dt))

    def _deps(self, reads, writes):
        deps = []
        for r in reads:
            t = self.last_w.get(r)
            if t is not None:
                deps.append(t)
        for w in writes:
            t = self.last_w.get(w)
            if t is not None:
                deps.append(t)
            deps.extend(self.readers.get(w, ()))
        return deps

    def _add_waits(self, eng, op, deps):
        seen = self.seen[eng]
        for t in deps:
            kind, key, val = t
            if kind == "e":
                if key == eng and eng not in self.same_engine_sync:
                    continue
                if key == eng and val >= op.idx:
                    continue
                if seen.get(("e", key), -1) >= val:
                    continue
                seen[("e", key)] = val
                op.waits.append((key, val))
                self.ops[key][val].milestone = True
            else:
                if seen.get(("d", key), -1) >= val:
                    continue
                seen[("d", key)] = val
                op.dwaits.append((key, val))

    def _commit(self, tok, reads, writes):
        for r in reads:
            self.readers.setdefault(r, []).append(tok)
        for w in writes:
            self.last_w[w] = tok
            self.readers[w] = []

    def op(self, eng, fn, reads=(), writes=()):
        lst = self.ops[eng]
        o = _Op(fn, len(lst))
        self._add_waits(eng, o, self._deps(reads, writes))
        lst.append(o)
        self._commit(("e", eng, o.idx), reads, writes)
        return o

    def dma(self, fn, semkey, reads=(), writes=(), q="sp", is_out=False):
        lst = self.ops[q]
        o = _Op(fn, len(lst), dma=semkey)
        self._add_waits(q, o, self._deps(reads, writes))
        lst.append(o)
        c = self.dsem_cnt.get(semkey, 0) + 16
        self.dsem_cnt[semkey] = c
        self._commit(("d", semkey, c), reads, writes)
        if is_out:
            self.out_dma_keys.add(semkey)
        return o

    def dma_multi(self, fns, semkey, reads=(), writes=(), q="sp", is_out=False):
        lst = self.ops[q]
        deps = self._deps(reads, writes)
        first = True
        for fn in fns:
            o = _Op(fn, len(lst), dma=semkey)
            if first:
                self._add_waits(q, o, deps)
                first = False
            lst.append(o)
            self.dsem_cnt[semkey] = self.dsem_cnt.get(semkey, 0) + 16
        self._commit(("d", semkey, self.dsem_cnt[semkey]), reads, writes)
        if is_out:
            self.out_dma_keys.add(semkey)

    def barrier(self):
        last = {}
        for e in self.ENGS:
            for o in reversed(self.ops[e]):
                if o.dma is None:
                    last[e] = o.idx
                    break
        dtoks = [("d", k, v) for k, v in self.dsem_cnt.items()]
        for e in self.ENGS:
            o = _Op(lambda h: h.nop(), len(self.ops[e]))
            deps = [("e", k, v) for k, v in last.items() if k != e] + dtoks
            self._add_waits(e, o, deps)
            self.ops[e].append(o)

    def emit(self, final=False):
        nc = self.nc
        self.barrier()
        if not hasattr(self, "esem"):
            self.esem = {e: self.gstack.enter_context(nc.semaphore("es_" + e)) for e in self.ENGS}
            self.dsem = {}
            self.mbase = {e: 0 for e in self.ENGS}
        for k in self.dsem_cnt:
            if k not in self.dsem:
                self.dsem[k] = self.gstack.enter_context(nc.semaphore("ds_%d" % len(self.dsem)))
        esem, dsem = self.esem, self.dsem
        fin = [(k, self.dsem_cnt[k]) for k in self.out_dma_keys] if final else []
        mcount = {}
        for e in self.ENGS:
            n = self.mbase[e]
            m = {}
            for o in self.ops[e]:
                if o.milestone:
                    assert o.dma is None
                    n += 1
                    m[o.idx] = n
            mcount[e] = m
            self.mbase[e] = n
        ops = self.ops

        def run(e, h):
            for o in ops[e]:
                for (k, v) in o.waits:
                    h.wait_ge(esem[k], mcount[k][v])
                for (k, v) in o.dwaits:
                    h.wait_ge(dsem[k], v)
                ins = o.fn(h)
                if o.dma is not None:
                    ins.then_inc(dsem[o.dma], 16)
                elif o.milestone:
                    ins.then_inc(esem[e], 1)
            if e == "sp":
                for (k, v) in fin:
                    h.wait_ge(dsem[k], v)

        with nc.Block() as block:
            @block.tensor
            def _(h):
                run("pe", h)

            @block.scalar
            def _(h):
                run("act", h)

            @block.vector
            def _(h):
                run("dve", h)

            @block.gpsimd
            def _(h):
                run("pool", h)

            @block.sync
            def _(h):
                run("sp", h)
        self.ops = {e: [] for e in self.ENGS}
        self.seen = {e: {} for e in self.ENGS}
        self.last_w = {}
        self.readers = {}
        self.nstage = getattr(self, "nstage", 0) + 1

PI = math.pi
ALPHA = 2.0 ** 0.25
LN_EPS = 1e-5
GK = 2.0 * math.sqrt(2.0 / PI)


def build(S, NSEQ, NS, NPG, NPHYS):
    nc = bass.Bass("TRN2", target_bir_lowering=False)
    NT = S // 512
    NB = S // 128
    NCOL = NSEQ + NS

    def din(name, shape, dt=F32):
        return nc.dram_tensor(name, list(shape), dt, kind="ExternalInput").ap()

    def dout(name, shape, dt=F32):
        return nc.dram_tensor(name, list(shape), dt, kind="ExternalOutput").ap()

    xT = din("xT", [NSEQ, 128, 8, S]); xtok = din("xtok", [NSEQ, S, 1024])
    cT = din("cT", [128, 8, NCOL]); crep = din("crep", [NSEQ, 128, 8, 128])
    xsT = din("xsT", [128, 8, NS]); xstok = din("xstok", [NS, 1024])
    w_cond = din("w_cond", [1024, 3072]); bcT = din("bcT", [128, 24]); bgrow = din("bgrow", [128, 1024])
    w_in = din("w_in", [1024, 8192]); w_glu = din("w_glu", [1024, 1024]); w_ao = din("w_ao", [1024, 1024])
    w_so = din("w_so", [1024, 1024]); w_out = din("w_out", [1024, 1024])
    bgluT = din("bgluT", [128, 8]); dT_d = din("dT", [128, 8]); sbb_d = din("sbb", [128, 16])
    are_d = din("a_re", [128, 32]); aim_d = din("a_im", [128, 32]); ldt_d = din("ldt", [128, 32])
    BTr_d = din("BTr", [128, 32, 128]); BTi_d = din("BTi", [128, 32, 128])
    CTr_d = din("CTr", [128, 32, 128]); CTi_d = din("CTi", [128, 32, 128])
    lng_d = din("lng", [128, 1024]); lnb_d = din("lnb", [128, 1024])
    Ubf_d = din("Ubf", [128, 128], BF16); Lcbf_d = din("Lcbf", [128, 128], BF16)
    mask01_d = din("mask01", [128, 128]); tvec_d = din("tvec", [128, 128]); iotap_d = din("iotap", [128, 1])
    U32_d = din("U32", [128, 128]); ones32_d = din("ones32", [128, 128])
    bmask_d = din("bmask", [16, 1024])
    ck = din("cache_k", [NPHYS * 128, 1024]); cv = din("cache_v", [NPHYS * 128, 1024])
    pt_d = din("pt", [1, NS * NPG], I32)
    sstr_d = din("sst_re", [128, 32, NS]); ssti_d = din("sst_im", [128, 32, NS])

    y_o = dout("y", [NSEQ, S, 1024]); kp_o = dout("kp", [NSEQ, S, 1024]); vp_o = dout("vp", [NSEQ, S, 1024])
    sre_o = dout("sre", [NSEQ, 128, 32]); sim_o = dout("sim", [NSEQ, 128, 32])
    ys_o = dout("ys", [NS, 1024]); ks_o = dout("ks", [NS, 1024]); vs_o = dout("vs", [NS, 1024])
    ssre_o = dout("ssre", [128, 32, NS]); ssim_o = dout("ssim", [128, 32, NS])

    with ExitStack() as gst:
        p = Prog(nc, gst)
        OP = lambda eng, fn, r=(), w=(): p.op(eng, fn, reads=r, writes=w)
        ps = [p.ps("psb%d" % i, [128, 512]) for i in range(8)]
        A = [p.sb("A%d" % i, [128, 8, S], BF16) for i in range(3)]
        stg = [p.sb("stg%d" % i, [128, 8, 128], F32) for i in range(2)]
        wbf = [p.sb("wbf%d" % i, [128, 8, 1024], BF16) for i in range(2)]
        Ubf = p.sb("Ubf", [128, 128], BF16); Lcbf = p.sb("Lcbf", [128, 128], BF16)
        zbf = p.sb("zbf", [128, 128], BF16)
        mask01 = p.sb("mask01", [128, 128], F32); tvec = p.sb("tvec", [128, 128], F32)
        modT = [p.sb("modT%d" % i, [128, 8, NCOL], F32) for i in range(2)]
        grow = p.sb("grow", [128, NSEQ, 1024], F32)
        gate_s = p.sb("gate_s", [128, 1024], F32)
        bcTs = p.sb("bcTs", [128, 24], F32); bgluTs = p.sb("bgluTs", [128, 8], F32)
        dTs = p.sb("dTs", [128, 8], F32); sbb = p.sb("sbbs", [128, 16], F32)
        mag = p.sb("mag", [128, 32], F32); ang = p.sb("ang", [128, 32], F32)
        abr = p.sb("abr", [128, 32], F32); abi = p.sb("abi", [128, 32], F32)
        cor = p.sb("cor", [128, 32], F32); coi = p.sb("coi", [128, 32], F32)
        rhor = p.sb("rhor", [128, 32], F32); rhoi = p.sb("rhoi", [128, 32], F32)
        xendr = p.sb("xendr", [128, 32], F32); xendi = p.sb("xendi", [128, 32], F32)
        psrr = [0]

        def bank(n=4, base=0):
            b = base + psrr[0] % n
            psrr[0] += 1
            return b

        def sin_of(out, arg, shift, tmp, ra, rt, ro, itile=None, t2=None, r2=None):
            r2 = r2 or (rt + "_2")
            OP("dve", lambda h: h.tensor_scalar(tmp, arg, shift, 1.0 / (2.0 * PI), ALU.add, ALU.mult), [ra], [rt])
            OP("dve", lambda h: h.tensor_copy(itile, tmp), [rt], [rt + "_i"])
            OP("dve", lambda h: h.tensor_copy(tmp, itile), [rt + "_i"], [rt])
            OP("dve", lambda h: h.tensor_scalar(t2, arg, shift, None, ALU.add), [ra], [r2])
            OP("dve", lambda h: h.scalar_tensor_tensor(tmp, tmp, -2.0 * PI, t2, ALU.mult, ALU.add), [rt, r2], [rt])
            OP("dve", lambda h: h.tensor_scalar(t2, tmp, PI, -2.0 * PI, ALU.is_gt, ALU.mult), [rt], [r2])
            OP("dve", lambda h: h.tensor_tensor(tmp, tmp, t2, ALU.add), [rt, r2], [rt])
            OP("act", lambda h: h.activation(out, tmp, AF.Sin), [rt], [ro])

        def load_w(src, slot):
            for q in range(8):
                sl = q % 2
                p.dma(lambda h, q=q, sl=sl: h.dma_start(out=stg[sl][:], in_=src[:, q * 128:(q + 1) * 128].rearrange("(kc p) n -> p kc n", p=128)),
                      "stg%d" % sl, writes=[("stg", sl)])
                OP("pool", lambda h, q=q, sl=sl: h.tensor_copy(wbf[slot][:, :, q * 128:(q + 1) * 128], stg[sl][:]),
                   [("stg", sl)], [("wbf", slot, q)])

        def proj_fm(slots, ins, tiles, epi):
            for ti, (t0, tw) in enumerate(tiles):
                for c in range(8):
                    bs = []
                    for j, (slot, (ai, ab)) in enumerate(zip(slots, ins)):
                        b = bank(6)
                        bs.append(b)
                        for kc in range(8):
                            OP("pe", lambda h, b=b, slot=slot, ab=ab, kc=kc, c=c, t0=t0, tw=tw: h.matmul(
                                ps[b][:, 0:tw], lhsT=wbf[slot][:, kc, c * 128:(c + 1) * 128], rhs=ab[:, kc, t0:t0 + tw],
                                start=(kc == 0), stop=(kc == 7)),
                               [("wbf", slot, c), ("A", ai, kc, ti)], [("ps", b)])
                    epi(c, ti, t0, tw, bs)

        with ExitStack() as st:
            p.stack = st
            cTa = p.sb("cTa", [128, 8, NCOL], F32)
            creps = p.sb("creps", [128, NSEQ, 8, 128], F32)
            are = p.sb("are", [128, 32], F32); aim = p.sb("aim", [128, 32], F32); ldt = p.sb("ldt", [128, 32], F32)
            wst = [p.sb("wst%d" % i, [128, 8, 512], F32) for i in range(2)]
            t32 = [p.sb("t32_%d" % i, [128, 32], F32) for i in range(6)]
            i32t = p.sb("i32t", [128, 32], I32); t2s = p.sb("t2s", [128, 32], F32)
            bgr = p.sb("bgr", [128, 1024], F32)
            loads = [(Ubf[:], Ubf_d), (Lcbf[:], Lcbf_d), (mask01[:], mask01_d), (tvec[:], tvec_d),
                     (bcTs[:], bcT), (bgluTs[:], bgluT), (dTs[:], dT_d), (sbb[:], sbb_d),
                     (cTa[:], cT), (are[:], are_d), (aim[:], aim_d),
                     (ldt[:], ldt_d), (bgr[:], bgrow)]
            for s_ in range(NSEQ):
                loads.append((creps[:, s_, :, :], crep[s_]))
            p.dma_multi([(lambda h, o=o, i=i: h.dma_start(out=o, in_=i)) for o, i in loads], "setup", writes=["setup"])
            OP("dve", lambda h: h.memset(zbf[:], 0.0), [], ["zbf"])
            for part in range(3):
                for half in range(2):
                    sl = (part * 2 + half) % 2
                    col0 = part * 1024 + half * 512
                    p.dma(lambda h, sl=sl, col0=col0: h.dma_start(out=wst[sl][:], in_=w_cond[:, col0:col0 + 512].rearrange("(kc p) n -> p kc n", p=128)),
                          "wst%d" % sl, writes=[("wst", sl)])
                    if part < 2:
                        for fcl in range(4):
                            fc = half * 4 + fcl
                            b = bank()
                            for kc in range(8):
                                OP("pe", lambda h, b=b, sl=sl, kc=kc, fcl=fcl: h.matmul(ps[b][:, 0:NCOL], lhsT=wst[sl][:, kc, fcl * 128:(fcl + 1) * 128],
                                                                                      rhs=cTa[:, kc, :], start=(kc == 0), stop=(kc == 7)),
                                   [("wst", sl), "setup"], [("ps", b)])
                            OP("dve", lambda h, b=b, part=part, fc=fc: h.tensor_scalar(modT[part][:, fc, :], ps[b][:, 0:NCOL], bcTs[:, part * 8 + fc:part * 8 + fc + 1],
                                                                                  1.0 if part == 1 else 0.0, ALU.add, ALU.add),
                               [("ps", b), "setup"], [("modT", part)])
                    else:
                        for s_ in range(NSEQ):
                            b = bank()
                            for kc in range(8):
                                OP("pe", lambda h, b=b, sl=sl, kc=kc, s_=s_: h.matmul(ps[b][:, :], lhsT=creps[:, s_, kc, :], rhs=wst[sl][:, kc, :],
                                                                                    start=(kc == 0), stop=(kc == 7)),
                                   [("wst", sl), "setup"], [("ps", b)])
                            OP("dve", lambda h, b=b, s_=s_, half=half: h.tensor_tensor(grow[:, s_, half * 512:(half + 1) * 512], ps[b][:, :], bgr[:, half * 512:(half + 1) * 512], ALU.add),
                               [("ps", b), "setup"], [("grow", s_, half)])
                        b = bank()
                        for kc in range(8):
                            OP("pe", lambda h, b=b, sl=sl, kc=kc: h.matmul(ps[b][0:NS, :], lhsT=cTa[:, kc, NSEQ:NCOL], rhs=wst[sl][:, kc, :],
                                                                         start=(kc == 0), stop=(kc == 7)),
                               [("wst", sl), "setup"], [("ps", b)])
                        OP("dve", lambda h, b=b, half=half: h.tensor_tensor(gate_s[0:NS, half * 512:(half + 1) * 512], ps[b][0:NS, :], bgr[0:NS, half * 512:(half + 1) * 512], ALU.add),
                           [("ps", b), "setup"], [("gate_s", half)])
            dt_, lm, c_, s_t, tmp, den = t32
            OP("act", lambda h: h.activation(dt_[:], ldt[:], AF.Exp), ["setup"], ["dt"])
            OP("dve", lambda h: h.tensor_tensor(lm[:], are[:], dt_[:], ALU.mult), ["dt", "setup"], ["lm"])
            OP("act", lambda h: h.activation(mag[:], lm[:], AF.Exp), ["lm"], ["mag"])
            OP("dve", lambda h: h.tensor_tensor(ang[:], aim[:], dt_[:], ALU.mult), ["dt", "setup"], ["ang"])
            sin_of(s_t[:], ang[:], 0.0, tmp[:], "ang", "tmp", "s_t", i32t[:], t2s[:])
            OP("dve", lambda h: h.tensor_tensor(abi[:], mag[:], s_t[:], ALU.mult), ["mag", "s_t"], ["abi"])
            sin_of(c_[:], ang[:], PI / 2, tmp[:], "ang", "tmp", "c_", i32t[:], t2s[:])
            OP("dve", lambda h: h.tensor_tensor(abr[:], mag[:], c_[:], ALU.mult), ["mag", "c_"], ["abr"])
            OP("dve", lambda h: h.tensor_tensor(den[:], are[:], are[:], ALU.mult), ["setup"], ["den"])
            OP("dve", lambda h: h.tensor_tensor(tmp[:], aim[:], aim[:], ALU.mult), ["setup"], ["tmp"])
            OP("dve", lambda h: h.tensor_tensor(den[:], den[:], tmp[:], ALU.add), ["den", "tmp"], ["den"])
            OP("dve", lambda h: h.reciprocal(den[:], den[:]), ["den"], ["den"])
            OP("dve", lambda h: h.tensor_scalar_add(c_[:], abr[:], -1.0), ["abr"], ["c_"])
            OP("dve", lambda h: h.tensor_tensor(tmp[:], c_[:], are[:], ALU.mult), ["c_"], ["tmp"])
            OP("dve", lambda h: h.tensor_tensor(s_t[:], abi[:], aim[:], ALU.mult), ["abi"], ["s_t"])
            OP("dve", lambda h: h.tensor_tensor(tmp[:], tmp[:], s_t[:], ALU.add), ["tmp", "s_t"], ["tmp"])
            OP("dve", lambda h: h.tensor_tensor(cor[:], tmp[:], den[:], ALU.mult), ["tmp", "den"], ["cor"])
            OP("dve", lambda h: h.tensor_tensor(tmp[:], abi[:], are[:], ALU.mult), ["abi"], ["tmp"])
            OP("dve", lambda h: h.tensor_tensor(s_t[:], c_[:], aim[:], ALU.mult), ["c_"], ["s_t"])
            OP("dve", lambda h: h.tensor_tensor(tmp[:], tmp[:], s_t[:], ALU.subtract), ["tmp", "s_t"], ["tmp"])
            OP("dve", lambda h: h.tensor_tensor(coi[:], tmp[:], den[:], ALU.mult), ["tmp", "den"], ["coi"])
            OP("dve", lambda h: h.tensor_scalar_mul(lm[:], ang[:], 128.0), ["ang"], ["lm"])
            sin_of(rhoi[:], lm[:], 0.0, tmp[:], "lm", "tmp", "rhoi", i32t[:], t2s[:])
            sin_of(rhor[:], lm[:], PI / 2, tmp[:], "lm", "tmp", "rhor", i32t[:], t2s[:])
            p.emit()

        def run_pass(seq):
            sample = seq is None
            T = NS if sample else S
            tiles = [(0, NS)] if sample else [(i * 512, 512) for i in range(NT)]
            col = NSEQ if sample else seq

            def fm_names(ai, ti):
                return [("A", ai, c, ti) for c in range(8)]

            with ExitStack() as st:
                p.stack = st
                if sample:
                    xs_ = p.sb("xs_", [128, 8, NS], F32)
                    p.dma(lambda h: h.dma_start(out=xs_[:], in_=xsT), "xst0", writes=["xs_"])
                    OP("dve", lambda h: h.tensor_tensor(xs_[:], xs_[:], modT[1][:, :, NSEQ:NCOL], ALU.mult), ["xs_"], ["xs_"])
                    OP("dve", lambda h: h.tensor_tensor(A[0][:, :, 0:NS], xs_[:], modT[0][:, :, NSEQ:NCOL], ALU.add), ["xs_"], fm_names(0, 0))
                else:
                    xst = [p.sb("xst%d" % i, [128, S], F32) for i in range(2)]
                    for c in range(8):
                        sl = c % 2
                        p.dma(lambda h, c=c, sl=sl: h.dma_start(out=xst[sl][:], in_=xT[seq, :, c, :]), "xst%d" % sl, writes=[("xst", sl)])
                        OP("dve", lambda h, c=c, sl=sl: h.tensor_scalar(A[0][:, c, :], xst[sl][:], modT[1][:, c, col:col + 1], modT[0][:, c, col:col + 1], ALU.mult, ALU.add),
                           [("xst", sl)], [("A", 0, c, ti) for ti in range(NT)])
                p.emit()

            with ExitStack() as st:
                p.stack = st
                load_w(w_in[:, 4096:5120], 0)

                def epi(c, ti, t0, tw, bs):
                    OP("dve", lambda h: h.tensor_copy(A[1][:, c, t0:t0 + tw], ps[bs[0]][:, 0:tw]), [("ps", bs[0])], [("A", 1, c, ti)])
                proj_fm([0], [(0, A[0])], tiles, epi)
                p.emit()

            with ExitStack() as st:
                p.stack = st
                load_w(w_glu, 0)
                W = 4 * 128
                ctr = p.sb("ctr", [128, 4, 128], F32); cti = p.sb("cti", [128, 4, 128], F32)
                btr = p.sb("btr", [128, 4, 128], F32); bti = p.sb("bti", [128, 4, 128], F32)
                Bre = p.sb("Bre", [128, 4, 128], BF16); Bim = p.sb("Bim", [128, 4, 128], BF16)
                Cr = p.sb("Cr", [128, 4, 128], BF16); nCr = p.sb("nCr", [128, 4, 128], BF16); nCi = p.sb("nCi", [128, 4, 128], BF16)
                tA = p.sb("tA", [128, 4, 128], F32); tB = p.sb("tB", [128, 4, 128], F32)
                tI = p.sb("tI", [128, 4, 128], I32); tC = p.sb("tC", [128, 4, 128], F32)
                ybuf = p.sb("ybuf", [128, 512], F32); y2b = p.sb("y2b", [128, 512], F32); y3b = p.sb("y3b", [128, 512], F32)
                if not sample:
                    cosT = p.sb("cosT", [128, 4, 128], F32); sinT = p.sb("sinT", [128, 4, 128], F32)
                    Str = p.sb("Str", [128, 4, 128], F32); Sti = p.sb("Sti", [128, 4, 128], F32)
                    rbr = [p.sb("rbr%d" % i, [128, 4, 128], F32) for i in range(2)]; rbi = [p.sb("rbi%d" % i, [128, 4, 128], F32) for i in range(2)]
                    tD = p.sb("tD", [128, 4, 128], F32)
                    P = [p.sb("P%d" % i, [128, 4, 128], BF16) for i in range(4)]
                    inr = p.sb("inr", [128, 4], F32); ini = p.sb("ini", [128, 4], F32)
                    sm = [p.sb("sm%d" % i, [128, 4], F32) for i in range(4)]
                else:
                    xr = p.sb("xr", [128, 32, NS], F32); xi = p.sb("xi", [128, 32, NS], F32)
                    xnr = p.sb("xnr", [128, 32, NS], F32); xni = p.sb("xni", [128, 32, NS], F32)
                    bur = p.sb("bur", [128, 4, NS], F32); bui = p.sb("bui", [128, 4, NS], F32)
                    Xrb = p.sb("Xrb", [128, 4, NS], BF16); Xib = p.sb("Xib", [128, 4, NS], BF16)
                    t4a = p.sb("t4a", [128, 4, NS], F32); t4b = p.sb("t4b", [128, 4, NS], F32)
                    p.dma(lambda h: h.dma_start(out=xr[:], in_=sstr_d), "sst0", writes=["xr"])
                    p.dma(lambda h: h.dma_start(out=xi[:], in_=ssti_d), "sst1", writes=["xi"])

                def gelu_store(F, t0, tw):
                    OP("dve", lambda h: h.tensor_tensor(y2b[:, 0:tw], ybuf[:, 0:tw], ybuf[:, 0:tw], ALU.mult), ["ybuf"], ["y2b"])
                    OP("dve", lambda h: h.tensor_scalar(y2b[:, 0:tw], y2b[:, 0:tw], 0.044715, 1.0, ALU.mult, ALU.add), ["y2b"], ["y2b"])
                    OP("dve", lambda h: h.tensor_tensor(y2b[:, 0:tw], y2b[:, 0:tw], ybuf[:, 0:tw], ALU.mult), ["y2b", "ybuf"], ["y2b"])
                    OP("act", lambda h: h.activation(y3b[:, 0:tw], y2b[:, 0:tw], AF.Sigmoid, scale=GK), ["y2b"], ["y3b"])
                    OP("dve", lambda h: h.tensor_tensor(A[2][:, F, t0:t0 + tw], ybuf[:, 0:tw], y3b[:, 0:tw], ALU.mult), ["y3b", "ybuf"],
                       [("A", 2, F, t0 // 512)])

                for F in range(8):
                    G0 = 4 * F
                    cob_r = cor[:, G0:G0 + 4].unsqueeze(2).broadcast_to([128, 4, 128])
                    cob_i = coi[:, G0:G0 + 4].unsqueeze(2).broadcast_to([128, 4, 128])
                    p.dma_multi([lambda h: h.dma_start(out=ctr[:], in_=CTr_d[:, G0:G0 + 4, :]),
                                 lambda h: h.dma_start(out=cti[:], in_=CTi_d[:, G0:G0 + 4, :]),
                                 lambda h: h.dma_start(out=btr[:], in_=BTr_d[:, G0:G0 + 4, :]),
                                 lambda h: h.dma_start(out=bti[:], in_=BTi_d[:, G0:G0 + 4, :])], "ssmtab", writes=["tabs"])
                    OP("pool", lambda h: h.tensor_copy(Bre[:], btr[:]), ["tabs"], ["Bre"])
                    OP("pool", lambda h: h.tensor_copy(Bim[:], bti[:]), ["tabs"], ["Bim"])
                    if not sample:
                        OP("dve", lambda h: h.tensor_tensor(tA[:], ctr[:], cob_r, ALU.mult), ["tabs"], ["tA"])
                        OP("dve", lambda h: h.tensor_tensor(tB[:], cti[:], cob_i, ALU.mult), ["tabs"], ["tB"])
                        OP("dve", lambda h: h.tensor_tensor(Cr[:], tA[:], tB[:], ALU.subtract), ["tA", "tB"], ["Cr"])
                        OP("dve", lambda h: h.tensor_tensor(nCr[:], tB[:], tA[:], ALU.subtract), ["tA", "tB"], ["nCr"])
                        OP("dve", lambda h: h.tensor_tensor(tA[:], ctr[:], cob_i, ALU.mult), ["tabs", "Cr", "nCr"], ["tA"])
                        OP("dve", lambda h: h.tensor_tensor(tB[:], cti[:], cob_r, ALU.mult), ["tabs", "Cr", "nCr"], ["tB"])
                        OP("dve", lambda h: h.tensor_tensor(tA[:], tA[:], tB[:], ALU.add), ["tA", "tB"], ["tA"])
                        OP("dve", lambda h: h.tensor_scalar_mul(nCi[:], tA[:], -1.0), ["tA"], ["nCi"])
                        OP("dve", lambda h: h.tensor_tensor(tA[:], ang[:, G0:G0 + 4].unsqueeze(2).broadcast_to([128, 4, 128]),
                                                            tvec[:, :].unsqueeze(1).broadcast_to([128, 4, 128]), ALU.mult), ["nCi"], ["tA"])
                        sin_of(sinT[:], tA[:], 0.0, tB[:], "tA", "tB", "sinT", tI[:], tC[:], "tC")
                        sin_of(cosT[:], tA[:], PI / 2, tB[:], "tA", "tB", "cosT", tI[:], tC[:], "tC")
                        OP("dve", lambda h: h.memset(inr[:], 0.0), [], ["inr"])
                        OP("dve", lambda h: h.memset(ini[:], 0.0), [], ["ini"])
                        c2 = cosT[:].rearrange("p a b -> p (a b)"); s2 = sinT[:].rearrange("p a b -> p (a b)")
                        scr = [t_[:].rearrange("p a b -> p (a b)") for t_ in (tA, tB, tC, tD)]
                        bR, bI = 6, 7

                        def stA(n):
                            t0 = n * 128
                            ti = t0 // 512
                            k = n % 2
                            for gl in range(4):
                                OP("pe", lambda h, gl=gl: h.matmul(ps[bR][:, gl * 128:(gl + 1) * 128], lhsT=Bre[:, gl, :], rhs=A[1][:, F, t0:t0 + 128], start=True, stop=True),
                                   ["Bre", ("A", 1, F, ti)], [("ps", bR)])
                                OP("pe", lambda h, gl=gl: h.matmul(ps[bI][:, gl * 128:(gl + 1) * 128], lhsT=Bim[:, gl, :], rhs=A[1][:, F, t0:t0 + 128], start=True, stop=True),
                                   ["Bim", ("A", 1, F, ti)], [("ps", bI)])
                            OP("dve", lambda h: h.tensor_tensor(scr[0], ps[bR][:, :], c2, ALU.mult), [("ps", bR), "cosT", "sinT"], ["tA"])
                            OP("dve", lambda h: h.tensor_tensor(scr[1], ps[bI][:, :], s2, ALU.mult), [("ps", bI), "cosT", "sinT"], ["tB"])
                            OP("dve", lambda h: h.tensor_tensor(scr[2], ps[bI][:, :], c2, ALU.mult), [("ps", bI), "cosT", "sinT"], ["tC"])
                            OP("dve", lambda h: h.tensor_tensor(scr[3], ps[bR][:, :], s2, ALU.mult), [("ps", bR), "cosT", "sinT"], ["tD"])
                            OP("pool", lambda h: h.tensor_tensor(rbr[k][:].rearrange("p a b -> p (a b)"), scr[0], scr[1], ALU.add), ["tA", "tB"], [("rbr", k)])
                            OP("dve", lambda h: h.tensor_tensor(rbi[k][:].rearrange("p a b -> p (a b)"), scr[2], scr[3], ALU.subtract), ["tC", "tD"], [("rbi", k)])

                        def stB(n):
                            t0 = n * 128
                            ti = t0 // 512
                            k = n % 2
                            bY = 4 + n % 2
                            if n > 0:
                                lr_ = Str[:, :, 127]; li_ = Sti[:, :, 127]
                                rr = rhor[:, G0:G0 + 4]; ri = rhoi[:, G0:G0 + 4]
                                OP("dve", lambda h: h.tensor_tensor(sm[0][:], rr, lr_, ALU.mult), [("Str", 0), ("Str", 1), ("Str", 2), ("Str", 3)], ["sm0"])
                                OP("dve", lambda h: h.tensor_tensor(sm[1][:], ri, li_, ALU.mult), [("Sti", 0), ("Sti", 1), ("Sti", 2), ("Sti", 3)], ["sm1"])
                                OP("dve", lambda h: h.tensor_tensor(sm[2][:], rr, li_, ALU.mult), [("Sti", 0), ("Sti", 1), ("Sti", 2), ("Sti", 3)], ["sm2"])
                                OP("dve", lambda h: h.tensor_tensor(sm[3][:], ri, lr_, ALU.mult), [("Str", 0), ("Str", 1), ("Str", 2), ("Str", 3)], ["sm3"])
                                OP("dve", lambda h: h.tensor_tensor(inr[:], sm[0][:], sm[1][:], ALU.subtract), ["sm0", "sm1"], ["inr"])
                                OP("dve", lambda h: h.tensor_tensor(ini[:], sm[2][:], sm[3][:], ALU.add), ["sm2", "sm3"], ["ini"])
                            for gl in range(4):
                                G = G0 + gl
                                OP("dve", lambda h, gl=gl, G=G: h.tensor_tensor_scan(Str[:, gl, :], mag[:, G:G + 1].to_broadcast([128, 128]), rbr[k][:, gl, :],
                                                                                      inr[:, gl:gl + 1], ALU.mult, ALU.add), [("rbr", k), "inr"], [("Str", gl)])
                                OP("dve", lambda h, gl=gl, G=G: h.tensor_tensor_scan(Sti[:, gl, :], mag[:, G:G + 1].to_broadcast([128, 128]), rbi[k][:, gl, :],
                                                                                      ini[:, gl:gl + 1], ALU.mult, ALU.add), [("rbi", k), "ini"], [("Sti", gl)])
                            OP("pool", lambda h: h.tensor_tensor(P[0][:], cosT[:], Str[:], ALU.mult), [("Str", 0), ("Str", 1), ("Str", 2), ("Str", 3)] + ["cosT", "sinT"], [("P", 0)])
                            OP("pool", lambda h: h.tensor_tensor(P[1][:], sinT[:], Sti[:], ALU.mult), [("Sti", 0), ("Sti", 1), ("Sti", 2), ("Sti", 3)] + ["cosT", "sinT"], [("P", 1)])
                            OP("pool", lambda h: h.tensor_tensor(P[2][:], cosT[:], Sti[:], ALU.mult), [("Sti", 0), ("Sti", 1), ("Sti", 2), ("Sti", 3)] + ["cosT", "sinT"], [("P", 2)])
                            OP("pool", lambda h: h.tensor_tensor(P[3][:], sinT[:], Str[:], ALU.mult), [("Str", 0), ("Str", 1), ("Str", 2), ("Str", 3)] + ["cosT", "sinT"], [("P", 3)])
                            kq = 0
                            for gl in range(4):
                                for (Wt, Pi) in ((Cr, 0), (nCr, 1), (nCi, 2), (nCi, 3)):
                                    OP("pe", lambda h, gl=gl, Wt=Wt, Pi=Pi, kq=kq: h.matmul(ps[bY][:, 0:128], lhsT=Wt[:, gl, :], rhs=P[Pi][:, gl, :], start=(kq == 0), stop=(kq == 15)),
                                       [("P", Pi), "Cr", "nCr", "nCi"], [("ps", bY)])
                                    kq += 1

                        def stEpi(n):
                            t0 = n * 128
                            ti = t0 // 512
                            bY = 4 + n % 2
                            o0 = (n % 4) * 128
                            OP("dve", lambda h: h.scalar_tensor_tensor(ybuf[:, o0:o0 + 128], A[1][:, F, t0:t0 + 128], dTs[:, F:F + 1], ps[bY][:, 0:128], ALU.mult, ALU.add),
                               [("ps", bY), ("A", 1, F, ti)], ["ybuf"])
                            if n % 4 == 3:
                                gelu_store(F, ti * 512, 512)

                        stA(0)
                        for n in range(NB):
                            if n + 1 < NB:
                                stA(n + 1)
                            stB(n)
                            if n > 0:
                                stEpi(n - 1)
                        stEpi(NB - 1)
                        lr_ = Str[:, :, 127]; li_ = Sti[:, :, 127]
                        c1 = cosT[:, :, 127]; s1 = sinT[:, :, 127]
                        OP("dve", lambda h: h.tensor_tensor(sm[0][:], c1, lr_, ALU.mult), [("Str", 0), ("Str", 1), ("Str", 2), ("Str", 3)] + ["cosT", "sinT"], ["sm0"])
                        OP("dve", lambda h: h.tensor_tensor(sm[1][:], s1, li_, ALU.mult), [("Sti", 0), ("Sti", 1), ("Sti", 2), ("Sti", 3)] + ["cosT", "sinT"], ["sm1"])
                        OP("dve", lambda h: h.tensor_tensor(sm[0][:], sm[0][:], sm[1][:], ALU.subtract), ["sm0", "sm1"], ["sm0"])
                        OP("dve", lambda h: h.tensor_tensor(sm[2][:], c1, li_, ALU.mult), [("Sti", 0), ("Sti", 1), ("Sti", 2), ("Sti", 3)] + ["cosT", "sinT"], ["sm2"])
                        OP("dve", lambda h: h.tensor_tensor(sm[3][:], s1, lr_, ALU.mult), [("Str", 0), ("Str", 1), ("Str", 2), ("Str", 3)] + ["cosT", "sinT"], ["sm3"])
                        OP("dve", lambda h: h.tensor_tensor(sm[2][:], sm[2][:], sm[3][:], ALU.add), ["sm2", "sm3"], ["sm2"])
                        cr4 = cor[:, G0:G0 + 4]; ci4 = coi[:, G0:G0 + 4]
                        OP("dve", lambda h: h.tensor_tensor(sm[1][:], cr4, sm[0][:], ALU.mult), ["sm0"], ["sm1"])
                        OP("dve", lambda h: h.tensor_tensor(sm[3][:], ci4, sm[2][:], ALU.mult), ["sm2"], ["sm3"])
                        OP("dve", lambda h: h.tensor_tensor(xendr[:, G0:G0 + 4], sm[1][:], sm[3][:], ALU.subtract), ["sm1", "sm3"], ["xendr"])
                        OP("dve", lambda h: h.tensor_tensor(sm[1][:], cr4, sm[2][:], ALU.mult), ["sm2", "xendr"], ["sm1"])
                        OP("dve", lambda h: h.tensor_tensor(sm[3][:], ci4, sm[0][:], ALU.mult), ["sm0", "xendr"], ["sm3"])
                        OP("dve", lambda h: h.tensor_tensor(xendi[:, G0:G0 + 4], sm[1][:], sm[3][:], ALU.add), ["sm1", "sm3"], ["xendi"])
                    else:
                        OP("pool", lambda h: h.tensor_copy(Cr[:], ctr[:]), ["tabs"], ["Cr"])
                        OP("dve", lambda h: h.tensor_scalar_mul(nCi[:], cti[:], -1.0), ["tabs"], ["nCi"])
                        bR, bI, bY = 6, 7, bank(2, 4)
                        for gl in range(4):
                            OP("pe", lambda h, gl=gl: h.matmul(ps[bR][:, gl * NS:(gl + 1) * NS], lhsT=Bre[:, gl, :], rhs=A[1][:, F, 0:NS], start=True, stop=True),
                               ["Bre", ("A", 1, F, 0)], [("ps", bR)])
                            OP("pe", lambda h, gl=gl: h.matmul(ps[bI][:, gl * NS:(gl + 1) * NS], lhsT=Bim[:, gl, :], rhs=A[1][:, F, 0:NS], start=True, stop=True),
                               ["Bim", ("A", 1, F, 0)], [("ps", bI)])
                        pr = ps[bR][:, 0:4 * NS].rearrange("p (a b) -> p a b", a=4); pi_ = ps[bI][:, 0:4 * NS].rearrange("p (a b) -> p a b", a=4)
                        cbr = cor[:, G0:G0 + 4].unsqueeze(2).broadcast_to([128, 4, NS]); cbi = coi[:, G0:G0 + 4].unsqueeze(2).broadcast_to([128, 4, NS])
                        abr4 = abr[:, G0:G0 + 4].unsqueeze(2).broadcast_to([128, 4, NS]); abi4 = abi[:, G0:G0 + 4].unsqueeze(2).broadcast_to([128, 4, NS])
                        xor_ = xr[:, G0:G0 + 4, :]; xoi_ = xi[:, G0:G0 + 4, :]
                        xnr_ = xnr[:, G0:G0 + 4, :]; xni_ = xni[:, G0:G0 + 4, :]
                        OP("dve", lambda h: h.tensor_tensor(t4a[:], pr, cbr, ALU.mult), [("ps", bR)], ["t4a"])
                        OP("dve", lambda h: h.tensor_tensor(t4b[:], pi_, cbi, ALU.mult), [("ps", bI)], ["t4b"])
                        OP("dve", lambda h: h.tensor_tensor(bur[:], t4a[:], t4b[:], ALU.subtract), ["t4a", "t4b"], ["bur"])
                        OP("dve", lambda h: h.tensor_tensor(t4a[:], pi_, cbr, ALU.mult), [("ps", bI), "bur"], ["t4a"])
                        OP("dve", lambda h: h.tensor_tensor(t4b[:], pr, cbi, ALU.mult), [("ps", bR), "bur"], ["t4b"])
                        OP("dve", lambda h: h.tensor_tensor(bui[:], t4a[:], t4b[:], ALU.add), ["t4a", "t4b"], ["bui"])
                        OP("dve", lambda h: h.tensor_tensor(t4a[:], xor_, abr4, ALU.mult), ["xr", "bui"], ["t4a"])
                        OP("dve", lambda h: h.tensor_tensor(t4b[:], xoi_, abi4, ALU.mult), ["xi", "bui"], ["t4b"])
                        OP("dve", lambda h: h.tensor_tensor(t4a[:], t4a[:], t4b[:], ALU.subtract), ["t4a", "t4b"], ["t4a"])
                        OP("dve", lambda h: h.tensor_tensor(xnr_, t4a[:], bur[:], ALU.add), ["t4a", "bur"], ["xnr"])
                        OP("dve", lambda h: h.tensor_tensor(t4a[:], xoi_, abr4, ALU.mult), ["xi", "xnr"], ["t4a"])
                        OP("dve", lambda h: h.tensor_tensor(t4b[:], xor_, abi4, ALU.mult), ["xr", "xnr"], ["t4b"])
                        OP("dve", lambda h: h.tensor_tensor(t4a[:], t4a[:], t4b[:], ALU.add), ["t4a", "t4b"], ["t4a"])
                        OP("dve", lambda h: h.tensor_tensor(xni_, t4a[:], bui[:], ALU.add), ["t4a", "bui"], ["xni"])
                        OP("pool", lambda h: h.tensor_copy(Xrb[:], xnr_), ["xnr"], ["Xrb"])
                        OP("pool", lambda h: h.tensor_copy(Xib[:], xni_), ["xni"], ["Xib"])
                        for gl in range(4):
                            OP("pe", lambda h, gl=gl: h.matmul(ps[bY][:, 0:NS], lhsT=Cr[:, gl, :], rhs=Xrb[:, gl, :], start=(gl == 0), stop=False),
                               ["Cr", "Xrb"], [("ps", bY)])
                            OP("pe", lambda h, gl=gl: h.matmul(ps[bY][:, 0:NS], lhsT=nCi[:, gl, :], rhs=Xib[:, gl, :], start=False, stop=(gl == 3)),
                               ["nCi", "Xib"], [("ps", bY)])
                        OP("dve", lambda h: h.scalar_tensor_tensor(ybuf[:, 0:NS], A[1][:, F, 0:NS], dTs[:, F:F + 1], ps[bY][:, 0:NS], ALU.mult, ALU.add),
                           [("ps", bY), ("A", 1, F, 0)], ["ybuf"])
                        gelu_store(F, 0, NS)
                if sample:
                    p.dma(lambda h: h.dma_start(out=ssre_o, in_=xnr[:]), "o_ss0", reads=["xnr"], is_out=True)
                    p.dma(lambda h: h.dma_start(out=ssim_o, in_=xni[:]), "o_ss1", reads=["xni"], is_out=True)
                else:
                    p.dma(lambda h: h.dma_start(out=sre_o[seq], in_=xendr[:]), "o_s0", reads=["xendr"], is_out=True)
                    p.dma(lambda h: h.dma_start(out=sim_o[seq], in_=xendi[:]), "o_s1", reads=["xendi"], is_out=True)
                p.emit()

            with ExitStack() as st:
                p.stack = st
                tmp = [p.sb("tmpf%d" % i, [128, 512], F32) for i in range(2)]
                load_w(w_in[:, 5120:6144], 1)

                def epi(c, ti, t0, tw, bs):
                    k = c % 2
                    OP("act", lambda h: h.activation(tmp[k][:, 0:tw], ps[bs[0]][:, 0:tw], AF.Sigmoid, bias=bgluTs[:, c:c + 1]), [("ps", bs[0])], [("tmp", k)])
                    OP("dve", lambda h: h.tensor_tensor(A[1][:, c, t0:t0 + tw], A[2][:, c, t0:t0 + tw], tmp[k][:, 0:tw], ALU.mult),
                       [("tmp", k), ("A", 2, c, ti)], [("A", 1, c, ti)])
                proj_fm([0], [(2, A[2])], tiles, epi)
                p.emit()

            with ExitStack() as st:
                p.stack = st
                tmp = [p.sb("tmpf%d" % i, [128, 512], F32) for i in range(2)]
                load_w(w_so, 0)

                def epi(c, ti, t0, tw, bs):
                    k = c % 2
                    OP("act", lambda h: h.activation(tmp[k][:, 0:tw], ps[bs[0]][:, 0:tw], AF.Sigmoid), [("ps", bs[0])], [("tmp", k)])
                    OP("dve", lambda h: h.tensor_tensor(tmp[k][:, 0:tw], tmp[k][:, 0:tw], ps[bs[0]][:, 0:tw], ALU.mult), [("ps", bs[0]), ("tmp", k)], [("tmp", k)])
                    OP("dve", lambda h: h.tensor_tensor(A[1][:, c, t0:t0 + tw], A[1][:, c, t0:t0 + tw], tmp[k][:, 0:tw], ALU.mult),
                       [("tmp", k)], [("A", 1, c, ti)])
                proj_fm([1], [(0, A[0])], tiles, epi)
                p.emit()

            with ExitStack() as st:
                p.stack = st
                tmp = [p.sb("tmpf%d" % i, [128, 512], F32) for i in range(2)]
                load_w(w_in[:, 7168:8192], 1)

                def epi(c, ti, t0, tw, bs):
                    k = c % 2
                    OP("act", lambda h: h.activation(tmp[k][:, 0:tw], ps[bs[1]][:, 0:tw], AF.Sigmoid), [("ps", bs[1])], [("tmp", k)])
                    OP("dve", lambda h: h.tensor_tensor(A[2][:, c, t0:t0 + tw], ps[bs[0]][:, 0:tw], tmp[k][:, 0:tw], ALU.mult),
                       [("tmp", k), ("ps", bs[0])], [("A", 2, c, ti)])
                proj_fm([0, 1], [(1, A[1]), (0, A[0])], tiles, epi)
                p.emit()

            if sample:
                sample_attention()
            else:
                prompt_attention(seq)

            with ExitStack() as st:
                p.stack = st
                tmp = [p.sb("tmpf%d" % i, [128, 512], F32) for i in range(2)]

                def epi(c, ti, t0, tw, bs):
                    k = c % 2
                    OP("act", lambda h: h.activation(tmp[k][:, 0:tw], ps[bs[0]][:, 0:tw], AF.Sigmoid), [("ps", bs[0])], [("tmp", k)])
                    OP("dve", lambda h: h.tensor_tensor(tmp[k][:, 0:tw], tmp[k][:, 0:tw], ps[bs[0]][:, 0:tw], ALU.mult), [("ps", bs[0]), ("tmp", k)], [("tmp", k)])
                    OP("dve", lambda h: h.tensor_tensor(A[1][:, c, t0:t0 + tw], A[1][:, c, t0:t0 + tw], tmp[k][:, 0:tw], ALU.mult),
                       [("tmp", k)], [("A", 1, c, ti)])
                proj_fm([0], [(0, A[0])], tiles, epi)
                p.emit()

            with ExitStack() as st:
                p.stack = st
                tmp = [p.sb("tmpf%d" % i, [128, 512], F32) for i in range(2)]
                load_w(w_in[:, 6144:7168], 0)

                def epi(c, ti, t0, tw, bs):
                    k = c % 2
                    OP("act", lambda h: h.activation(tmp[k][:, 0:tw], ps[bs[1]][:, 0:tw], AF.Sigmoid), [("ps", bs[1])], [("tmp", k)])
                    OP("dve", lambda h: h.tensor_tensor(tmp[k][:, 0:tw], ps[bs[0]][:, 0:tw], tmp[k][:, 0:tw], ALU.mult),
                       [("tmp", k), ("ps", bs[0])], [("tmp", k)])
                    OP("dve", lambda h: h.tensor_tensor(A[2][:, c, t0:t0 + tw], A[2][:, c, t0:t0 + tw], tmp[k][:, 0:tw], ALU.add),
                       [("tmp", k)], [("A", 2, c, ti)])
                proj_fm([1, 0], [(1, A[1]), (0, A[0])], tiles, epi)
                p.emit()

            with ExitStack() as st:
                p.stack = st
                xt = [p.sb("xt%d" % i, [128, 1024], F32) for i in range(2)]
                rr = [p.sb("rr%d" % i, [128, 1024], F32) for i in range(2)]
                stats = p.sb("stats", [128, 2, 6], F32); mv = p.sb("mv", [128, 2], F32); rstd = p.sb("rstd", [128, 1], F32)
                lng = p.sb("lngs", [128, 1024], F32); lnb = p.sb("lnbs", [128, 1024], F32)
                p.dma_multi([lambda h: h.dma_start(out=lng[:], in_=lng_d), lambda h: h.dma_start(out=lnb[:], in_=lnb_d)], "lnld", writes=["ln"])
                load_w(w_out, 0)
                ntile = 1 if sample else NB
                for i in range(ntile):
                    n = NS if sample else 128
                    k = i % 2
                    ti = 0 if sample else (i * 128) // 512
                    src = xstok if sample else xtok[seq, i * 128:(i + 1) * 128, :]
                    p.dma(lambda h, k=k, src=src, n=n: h.dma_start(out=xt[k][0:n, :], in_=src), "xt%d" % k, writes=[("xt", k)])
                    for half in range(2):
                        b = bank(4)
                        for kc in range(8):
                            OP("pe", lambda h, b=b, kc=kc, half=half, i=i, n=n: h.matmul(ps[b][0:n, :], lhsT=A[2][:, kc, i * 128:i * 128 + n],
                                                                                          rhs=wbf[0][:, kc, half * 512:(half + 1) * 512], start=(kc == 0), stop=(kc == 7)),
                               [("A", 2, kc, ti)] + [("wbf", 0, q) for q in range(half * 4, half * 4 + 4)], [("ps", b)])
                        g_ap = gate_s[0:n, half * 512:(half + 1) * 512] if sample else grow[:, seq, half * 512:(half + 1) * 512]
                        OP("dve", lambda h, b=b, k=k, half=half, n=n, g_ap=g_ap: h.tensor_tensor(rr[k][0:n, half * 512:(half + 1) * 512], ps[b][0:n, :], g_ap, ALU.mult),
                           [("ps", b)], [("rr", k)])
                    OP("dve", lambda h, k=k, n=n: h.scalar_tensor_tensor(rr[k][0:n, :], xt[k][0:n, :], ALPHA, rr[k][0:n, :], ALU.mult, ALU.add), [("xt", k), ("rr", k)], [("rr", k)])
                    for half in range(2):
                        OP("dve", lambda h, k=k, n=n, half=half: h.bn_stats(stats[0:n, half, :], rr[k][0:n, half * 512:(half + 1) * 512]), [("rr", k)], ["stats"])
                    OP("dve", lambda h, n=n: h.bn_aggr(mv[0:n, :], stats[0:n, :, :].rearrange("p a b -> p (a b)")), ["stats"], ["mv"])
                    OP("dve", lambda h, n=n: h.tensor_scalar_add(rstd[0:n, :], mv[0:n, 1:2], LN_EPS), ["mv"], ["rstd"])
                    OP("act", lambda h, n=n: h.activation(rstd[0:n, :], rstd[0:n, :], AF.Ln), ["rstd"], ["rstd"])
                    OP("act", lambda h, n=n: h.activation(rstd[0:n, :], rstd[0:n, :], AF.Exp, scale=-0.5), ["rstd"], ["rstd"])
                    OP("dve", lambda h, k=k, n=n: h.tensor_scalar(rr[k][0:n, :], rr[k][0:n, :], mv[0:n, 0:1], rstd[0:n, 0:1], ALU.subtract, ALU.mult), ["mv", "rstd", ("rr", k)], [("rr", k)])
                    OP("dve", lambda h, k=k, n=n: h.tensor_tensor(rr[k][0:n, :], rr[k][0:n, :], lng[0:n, :], ALU.mult), [("rr", k), "ln"], [("rr", k)])
                    OP("pool", lambda h, k=k, n=n: h.tensor_tensor(rr[k][0:n, :], rr[k][0:n, :], lnb[0:n, :], ALU.add), [("rr", k), "ln"], [("rr", k)])
                    dst = ys_o if sample else y_o[seq, i * 128:(i + 1) * 128, :]
                    p.dma(lambda h, k=k, n=n, dst=dst: h.dma_start(out=dst, in_=rr[k][0:n, :]), "o_y%d" % k, reads=[("rr", k)], is_out=True)
                p.emit()

        def prompt_attention(seq):
            with ExitStack() as st:
                p.stack = st
                wq = [p.sb("wq%d" % i, [128, 8, 128], BF16) for i in range(3)]
                qT = p.sb("qTp", [128, S], BF16); kT = p.sb("kTp", [128, S], BF16)
                vb = p.sb("vbp", [128, NB, 128], BF16)
                kst = p.sb("kst", [128, 4, 128], F32); vst = p.sb("vst", [128, 4, 128], F32)
                E = [p.sb("E%d" % i, [128, 512], F32) for i in range(4)]
                SP = [p.sb("SP%d" % i, [128, 512], BF16) for i in range(4)]
                EC = [p.sb("EC%d" % i, [128, 512], F32) for i in range(2)]
                Wb = [p.sb("Wb%d" % i, [128, 512], BF16) for i in range(2)]
                for c in range(8):
                    if c == 1:
                        load_w(w_in[:, 3072:4096], 0)
                        load_w(w_ao, 1)
                    for j in range(3):
                        sl = j % 2
                        col0 = j * 1024 + c * 128
                        p.dma(lambda h, sl=sl, col0=col0: h.dma_start(out=stg[sl][:], in_=w_in[:, col0:col0 + 128].rearrange("(kc p) n -> p kc n", p=128)),
                              "stg%d" % sl, writes=[("stg", sl)])
                        OP("pool", lambda h, sl=sl, j=j: h.tensor_copy(wq[j][:], stg[sl][:]), [("stg", sl)], [("wq", j)])
                    for j, dst in ((0, qT), (1, kT)):
                        for ti in range(NT):
                            b = bank(4)
                            for kc in range(8):
                                OP("pe", lambda h, b=b, kc=kc, j=j, ti=ti: h.matmul(ps[b][:, :], lhsT=wq[j][:, kc, :], rhs=A[0][:, kc, ti * 512:(ti + 1) * 512],
                                                                                  start=(kc == 0), stop=(kc == 7)), [("wq", j), ("A", 0, kc, ti)], [("ps", b)])
                            OP("dve", lambda h, b=b, dst=dst, ti=ti: h.tensor_copy(dst[:, ti * 512:(ti + 1) * 512], ps[b][:, :]), [("ps", b)], [("qk", j)])
                    for j, stb, dro in ((1, kst, kp_o), (2, vst, vp_o)):
                        for i4 in range(NB // 4):
                            b = bank(4)
                            for bl in range(4):
                                i = i4 * 4 + bl
                                for kc in range(8):
                                    OP("pe", lambda h, b=b, kc=kc, j=j, i=i, bl=bl: h.matmul(ps[b][:, bl * 128:(bl + 1) * 128], lhsT=A[0][:, kc, i * 128:(i + 1) * 128],
                                                                                            rhs=wq[j][:, kc, :], start=(kc == 0), stop=(kc == 7)),
                                       [("wq", j), ("A", 0, kc, i // 4)], [("ps", b)])
                            OP("dve", lambda h, b=b, stb=stb: h.tensor_copy(stb[:].rearrange("p a b -> p (a b)"), ps[b][:, :]), [("ps", b)], [("st", j)])
                            if j == 2:
                                OP("dve", lambda h, b=b, i4=i4: h.tensor_copy(vb[:, i4 * 4:(i4 + 1) * 4, :].rearrange("p a b -> p (a b)"), ps[b][:, :]), [("ps", b)], ["vb"])
                            p.dma(lambda h, stb=stb, dro=dro, i4=i4: h.dma_start(
                                out=dro[seq, i4 * 512:(i4 + 1) * 512, c * 128:(c + 1) * 128].rearrange("(a p) n -> p a n", p=128), in_=stb[:]),
                                "o_kv%d" % j, reads=[("st", j)], is_out=True)
                    for Tq in range(NT):
                        psS = (0, 1); acc = (2, 3); psO = 4
                        for hd in range(2):
                            hp = hd * 64
                            OP("pe", lambda h, hd=hd: h.matmul(ps[acc[hd]][:, :], lhsT=zbf[:, :], rhs=qT[:, 0:512], start=True, stop=True), ["zbf", ("qk", 0)], [("ps", acc[hd])])
                            OP("pe", lambda h, hp=hp: h.matmul(ps[psO][hp:hp + 64, :], lhsT=zbf[:, 0:64], rhs=qT[:, 0:512], start=True, stop=True), ["zbf", ("qk", 0)], [("psO", hd)])
                        def geo(j):
                            dj = j - 4 * Tq
                            c0 = max(0, 128 * dj)
                            return dj, c0, 512 * Tq + c0, 512 - c0

                        def stS(j):
                            dj, c0, q0, cw = geo(j)
                            for hd in range(2):
                                hp = hd * 64
                                OP("pe", lambda h, hd=hd, hp=hp: h.matmul(ps[psS[hd]][:, c0:512], lhsT=kT[hp:hp + 64, j * 128:(j + 1) * 128], rhs=qT[hp:hp + 64, q0:q0 + cw],
                                                                       start=True, stop=True), [("qk", 0), ("qk", 1)], [("ps", psS[hd])])

                        def stE(j):
                            dj, c0, q0, cw = geo(j)
                            kb = j % 2
                            for hd in range(2):
                                Eb = E[2 * kb + hd]; SPb = SP[2 * kb + hd]
                                OP("act", lambda h, hd=hd, Eb=Eb: h.activation(Eb[:, c0:512], ps[psS[hd]][:, c0:512], AF.Exp, bias=sbb[:, 2 * c + hd:2 * c + hd + 1], scale=0.125),
                                   [("ps", psS[hd])], [("E", kb, hd)])
                                if dj >= 0:
                                    OP("dve", lambda h, Eb=Eb: h.tensor_tensor(Eb[:, c0:c0 + 128], Eb[:, c0:c0 + 128], mask01[:, :], ALU.mult), [("E", kb, hd)], [("E", kb, hd)])
                                OP("act", lambda h, Eb=Eb, SPb=SPb: h.activation(SPb[:, c0:512], Eb[:, c0:512], AF.Ln, bias=1.0), [("E", kb, hd)], [("SP", kb, hd)])

                        def stU(j):
                            dj, c0, q0, cw = geo(j)
                            kb = j % 2
                            for hd in range(2):
                                SPb = SP[2 * kb + hd]
                                OP("pe", lambda h, hd=hd, SPb=SPb: h.matmul(ps[acc[hd]][:, c0:512], lhsT=Ubf[:, :], rhs=SPb[:, c0:512], start=False, stop=True, skip_group_check=True),
                                   [("SP", kb, hd)], [("ps", acc[hd])])

                        def stC(j):
                            dj, c0, q0, cw = geo(j)
                            for hd in range(2):
                                OP("act", lambda h, hd=hd: h.activation(EC[hd][:, c0:512], ps[acc[hd]][:, c0:512], AF.Exp, scale=-1.0), [("ps", acc[hd])], [("EC", hd)])

                        def stW(j):
                            dj, c0, q0, cw = geo(j)
                            kb = j % 2
                            for hd in range(2):
                                hp = hd * 64
                                Eb = E[2 * kb + hd]; SPb = SP[2 * kb + hd]
                                OP("pe", lambda h, hd=hd, SPb=SPb: h.matmul(ps[acc[hd]][:, c0:512], lhsT=Lcbf[:, :], rhs=SPb[:, c0:512], start=False, stop=True, skip_group_check=True),
                                   [("SP", kb, hd)], [("ps", acc[hd])])
                                OP("dve", lambda h, hd=hd, Eb=Eb: h.tensor_tensor(Wb[hd][:, c0:512], Eb[:, c0:512], EC[hd][:, c0:512], ALU.mult), [("E", kb, hd), ("EC", hd)], [("Wb", hd)])
                                OP("pe", lambda h, hd=hd, hp=hp: h.matmul(ps[psO][hp:hp + 64, c0:512], lhsT=vb[:, j, hd * 64:(hd + 1) * 64], rhs=Wb[hd][:, c0:512],
                                                                       start=False, stop=True, skip_group_check=True), [("Wb", hd), "vb"], [("psO", hd)])

                        jtop = 4 * Tq + 3
                        stS(jtop)
                        stE(jtop)
                        for j in range(jtop, -1, -1):
                            stU(j)
                            if j > 0:
                                stS(j - 1)
                            stC(j)
                            if j > 0:
                                stE(j - 1)
                            stW(j)
                        OP("dve", lambda h, Tq=Tq: h.tensor_copy(A[1][:, c, Tq * 512:(Tq + 1) * 512], ps[psO][:, :]), [("psO", 0), ("psO", 1)], [("A", 1, c, Tq)])
                p.emit()

        def sample_attention():
            GS = 4 if NS >= 4 else NS
            with ExitStack() as st:
                p.stack = st
                qtok = p.sb("qtok", [128, 1024], F32)
                oh = p.sb("oh", [128, 128], F32)
                ptb = p.sb("ptb", [128, NS * NPG], I32); idx = p.sb("idx", [128, NS * NPG], I32)
                iop = p.sb("iop", [128, 1], F32)
                U32 = p.sb("U32s", [128, 128], F32); ones32 = p.sb("ones32s", [128, 128], F32)
                bmask = p.sb("bmasks", [16, 1024], F32)
                qbs = [p.sb("qb%d" % i, [128, 1024], F32) for i in range(2)]
                pg_ = [p.sb("pg%d" % i, [128, 1024], F32) for i in range(3)]
                prod = [p.sb("prod%d" % i, [128, 1024], F32) for i in range(1)]
                sc = p.sb("sc", [128, GS, NPG, 16], F32)
                E8 = sc; SP8 = p.sb("SP8", [128, GS, NPG, 16], F32)
                WL = p.sb("WL", [128, GS, NPG, 16], F32)
                car = p.sb("car", [128, NPG, 16], F32); cum = p.sb("cum", [128, NPG, 16], F32)
                om = p.sb("om", [16, 1024], F32)
                p.dma_multi([lambda h: h.dma_start(out=ptb[:], in_=pt_d.partition_broadcast(128)),
                             lambda h: h.dma_start(out=iop[:], in_=iotap_d), lambda h: h.dma_start(out=U32[:], in_=U32_d),
                             lambda h: h.dma_start(out=ones32[:], in_=ones32_d), lambda h: h.dma_start(out=bmask[:], in_=bmask_d)], "sset", writes=["sset"])
                OP("dve", lambda h: h.tensor_scalar(idx[:], ptb[:], 128.0, iop[:, 0:1], ALU.mult, ALU.add), ["sset"], ["idx"])
                for j, (dst, dro) in enumerate(((qtok, None), (prod[0], ks_o), (prod[0], vs_o))):
                    load_w(w_in[:, j * 1024:(j + 1) * 1024], j % 2)
                    for half in range(2):
                        b = bank(4)
                        for kc in range(8):
                            OP("pe", lambda h, b=b, kc=kc, j=j, half=half: h.matmul(ps[b][0:NS, :], lhsT=A[0][:, kc, 0:NS], rhs=wbf[j % 2][:, kc, half * 512:(half + 1) * 512],
                                                                                  start=(kc == 0), stop=(kc == 7)),
                               [("A", 0, kc, 0)] + [("wbf", j % 2, q) for q in range(half * 4, half * 4 + 4)], [("ps", b)])
                        OP("dve", lambda h, b=b, dst=dst, half=half: h.tensor_copy(dst[0:NS, half * 512:(half + 1) * 512], ps[b][0:NS, :]), [("ps", b)], [("tok", False) if j == 0 else ("prod", 0)])
                    if dro is not None:
                        p.dma(lambda h, dst=dst, dro=dro: h.dma_start(out=dro, in_=dst[0:NS, :]), "o_kvs", reads=[("prod", 0)], writes=[], is_out=True)
                for g0 in range(0, NS, GS):
                    def qb_for(bl):
                        b_ = g0 + bl
                        k2 = 0
                        OP("dve", lambda h, b_=b_: h.tensor_scalar(oh[0:NS, :], iop[0:NS, 0:1].to_broadcast([NS, 128]), float(b_), None, ALU.is_equal), ["sset"], ["oh"])
                        for half in range(2):
                            bq = 6 + half
                            OP("pe", lambda h, bq=bq, b_=b_, half=half: h.matmul(ps[bq][:, :], lhsT=oh[0:NS, :], rhs=qtok[0:NS, half * 512:(half + 1) * 512], start=True, stop=True),
                               ["oh", ("tok", False)], [("ps", bq)])
                            OP("dve", lambda h, bq=bq, bl=bl, half=half: h.tensor_copy(qbs[bl % 2][:, half * 512:(half + 1) * 512], ps[bq][:, :]), [("ps", bq)], [("qbs", bl % 2)])
                        return None
                    NBUF, PF = 3, 2
                    pages = [(bl, pgi) for bl in range(GS) for pgi in range(NPG)]

                    def issue(src, ix):
                        bl, pgi = pages[ix]
                        e = (g0 + bl) * NPG + pgi
                        k3 = ix % NBUF
                        p.dma(lambda h: h.indirect_dma_start(out=pg_[k3][:], out_offset=None, in_=src,
                                                              in_offset=bass.IndirectOffsetOnAxis(ap=idx[:, e:e + 1], axis=0)),
                              "pg%d" % k3, reads=["idx"], writes=[("pg", k3)], q="pool")

                    for ix in range(min(PF, len(pages))):
                        issue(ck, ix)
                    for ix, (bl, pgi) in enumerate(pages):
                        k3 = ix % NBUF
                        if pgi == 0:
                            qb_for(bl)
                        if ix + PF < len(pages):
                            issue(ck, ix + PF)
                        OP("dve", lambda h: h.tensor_tensor(prod[0][:], pg_[k3][:], qbs[bl % 2][:], ALU.mult), [("pg", k3), ("qbs", bl % 2)], [("prod", 0)])
                        OP("dve", lambda h: h.tensor_reduce(sc[:, bl, pgi, :], prod[0][:].rearrange("p (a b) -> p a b", a=16), AX.X, ALU.add),
                           [("prod", 0)], ["sc"])
                    sc3 = sc[:].rearrange("p a b c -> p (a b) c")
                    OP("dve", lambda h: h.scalar_tensor_tensor(E8[:].rearrange("p a b c -> p (a b) c"), sc3, 0.125, sbb[:, :].unsqueeze(1).broadcast_to([128, GS * NPG, 16]), ALU.mult, ALU.add),
                       ["sc"], ["sc"])
                    E8f = E8[:].rearrange("p a b c -> p (a b c)"); SP8f = SP8[:].rearrange("p a b c -> p (a b c)")
                    OP("act", lambda h: h.activation(E8f, E8f, AF.Exp), ["sc"], ["sc"])
                    OP("act", lambda h: h.activation(SP8f, E8f, AF.Ln, bias=1.0), ["sc"], ["SP8"])
                    for bl in range(GS):
                        bC, bT = 4, 5
                        spb = SP8[:, bl, :, :].rearrange("p b c -> p (b c)")
                        OP("pe", lambda h, spb=spb: h.matmul(ps[bC][:, 0:NPG * 16], lhsT=U32[:, :], rhs=spb, start=True, stop=True), ["SP8", "sset"], [("ps", bC)])
                        OP("pe", lambda h, spb=spb: h.matmul(ps[bT][:, 0:NPG * 16], lhsT=ones32[:, :], rhs=spb, start=True, stop=True), ["SP8", "sset"], [("ps", bT)])
                        tot = ps[bT][:, 0:NPG * 16].rearrange("p (b c) -> p b c", c=16)
                        OP("dve", lambda h: h.memset(car[:, NPG - 1, :], 0.0), [], ["car"])
                        for pgi in range(NPG - 2, -1, -1):
                            OP("dve", lambda h, pgi=pgi, tot=tot: h.tensor_tensor(car[:, pgi, :], car[:, pgi + 1, :], tot[:, pgi + 1, :], ALU.add), ["car", ("ps", bT)], ["car"])
                        OP("dve", lambda h: h.tensor_tensor(cum[:].rearrange("p b c -> p (b c)"), ps[bC][:, 0:NPG * 16], car[:].rearrange("p b c -> p (b c)"), ALU.add),
                           [("ps", bC), "car"], ["cum"])
                        OP("act", lambda h: h.activation(cum[:].rearrange("p b c -> p (b c)"), cum[:].rearrange("p b c -> p (b c)"), AF.Exp, scale=-1.0), ["cum"], ["cum"])
                        OP("dve", lambda h, bl=bl: h.tensor_tensor(WL[:, bl, :, :], E8[:, bl, :, :], cum[:], ALU.mult), ["cum", "sc"], ["WL"])
                    for ix in range(min(PF, len(pages))):
                        issue(cv, ix)
                    for bl in range(GS):
                        b_ = g0 + bl
                        for pgi in range(NPG):
                            ix = bl * NPG + pgi
                            k3 = ix % NBUF
                            if ix + PF < len(pages):
                                issue(cv, ix + PF)
                            for half in range(2):
                                OP("pe", lambda h, half=half: h.matmul(ps[4 + half][0:16, :], lhsT=WL[:, bl, pgi, :], rhs=pg_[k3][:, half * 512:(half + 1) * 512],
                                                                     start=(pgi == 0), stop=(pgi == NPG - 1)), [("pg", k3), "WL"], [("ps", 4 + half)])
                        for half in range(2):
                            OP("dve", lambda h, half=half: h.tensor_tensor(om[:, half * 512:(half + 1) * 512], ps[4 + half][0:16, :], bmask[:, half * 512:(half + 1) * 512], ALU.mult),
                               [("ps", 4 + half), "sset"], ["om"])
                        for c8 in range(8):
                            OP("pe", lambda h, c8=c8, b_=b_: h.matmul(ps[3][:, c8 * NS + b_:c8 * NS + b_ + 1], lhsT=om[:, c8 * 128:(c8 + 1) * 128], rhs=ones32[0:16, 0:1], start=True, stop=True),
                               ["om", "sset"], [("ps", 3)])
                OP("dve", lambda h: h.tensor_copy(A[1][:, :, 0:NS], ps[3][:, 0:8 * NS].rearrange("p (a b) -> p a b", a=8)), [("ps", 3)], [("A", 1, c, 0) for c in range(8)])
                load_w(w_in[:, 3072:4096], 0)
                load_w(w_ao, 1)
                p.emit()

        for seq in range(NSEQ):
            run_pass(seq)
        run_pass(None)
        p.emit(final=True)
    return nc


NCORES = 8
_CACHE = {}


def _fm(v):
    return np.ascontiguousarray(v.reshape(8, 128).T)


def kernel(x_prompt, x_sample, c_prompt, c_sample, cache_k, cache_v, state_ssm_re, state_ssm_im, page_table,
           w_cond, b_cond, w_in, sb_bias, ssm_a_re, ssm_a_im, ssm_log_dt, ssm_b_re, ssm_b_im, ssm_c_re, ssm_c_im,
           ssm_d, w_glu, b_glu, w_att_out, w_ssm_out, w_out, ln_g, ln_b):
    f = np.float32
    A_ = lambda a: np.ascontiguousarray(np.asarray(a))
    x_prompt = A_(x_prompt); x_sample = A_(x_sample); c_prompt = A_(c_prompt); c_sample = A_(c_sample)
    B, S, D = x_prompt.shape
    DB = x_sample.shape[0]
    NPG = page_table.shape[1]
    NPHYS = cache_k.shape[1]
    ncores = min(NCORES, B)
    NSEQ = B // ncores
    NS = DB // ncores
    key = (S, NSEQ, NS, NPG, NPHYS)
    if key not in _CACHE:
        _CACHE[key] = build(*key)
    nc = _CACHE[key]
    bf = ml_dtypes.bfloat16
    kk = np.arange(128)
    Ubf = (kk[:, None] >= kk[None, :]).astype(f)
    consts = {
        "Ubf": Ubf.astype(bf), "Lcbf": (kk[:, None] < kk[None, :]).astype(f).astype(bf),
        "mask01": (kk[:, None] < kk[None, :]).astype(f), "tvec": np.tile(np.arange(128, dtype=f)[None, :], (128, 1)),
        "iotap": np.arange(128, dtype=f).reshape(128, 1), "U32": Ubf, "ones32": np.ones((128, 128), f),
        "bmask": np.kron(np.eye(16, dtype=f), np.ones((1, 64), f)),
    }
    bc = A_(b_cond)[0]
    are = A_(ssm_a_re)[0].reshape(32, 128).T
    aim = A_(ssm_a_im)[0].reshape(32, 128).T
    ldt = np.repeat(A_(ssm_log_dt)[0].reshape(32, 2, 1), 64, axis=2).reshape(32, 128).T
    bre, bim, cre, cim = A_(ssm_b_re)[0], A_(ssm_b_im)[0], A_(ssm_c_re)[0], A_(ssm_c_im)[0]
    BTr = np.zeros((128, 32, 128), f); BTi = np.zeros((128, 32, 128), f)
    CTr = np.zeros((128, 32, 128), f); CTi = np.zeros((128, 32, 128), f)
    for G in range(32):
        for g2 in range(2):
            g = 2 * G + g2
            r0 = 32 * (G % 4) + 16 * g2
            BTr[r0:r0 + 16, G, g2 * 64:(g2 + 1) * 64] = bre[g].T
            BTi[r0:r0 + 16, G, g2 * 64:(g2 + 1) * 64] = bim[g].T
            CTr[g2 * 64:(g2 + 1) * 64, G, r0:r0 + 16] = cre[g].T
            CTi[g2 * 64:(g2 + 1) * 64, G, r0:r0 + 16] = cim[g].T
    shared = dict(consts)
    shared.update({
        "w_cond": A_(w_cond)[0], "bcT": np.ascontiguousarray(bc.reshape(24, 128).T), "bgrow": np.tile(bc[None, 2048:3072], (128, 1)),
        "w_in": A_(w_in)[0], "w_glu": A_(w_glu)[0], "w_ao": A_(w_att_out)[0], "w_so": A_(w_ssm_out)[0], "w_out": A_(w_out)[0],
        "bgluT": _fm(A_(b_glu)[0]), "dT": _fm(A_(ssm_d)[0]), "sbb": np.tile(A_(sb_bias)[0][None, :], (128, 1)),
        "a_re": np.ascontiguousarray(are), "a_im": np.ascontiguousarray(aim), "ldt": np.ascontiguousarray(ldt),
        "BTr": BTr, "BTi": BTi, "CTr": CTr, "CTi": CTi,
        "lng": np.tile(A_(ln_g)[0][None, :], (128, 1)), "lnb": np.tile(A_(ln_b)[0][None, :], (128, 1)),
        "cache_k": A_(cache_k)[0].reshape(NPHYS * 128, 1024), "cache_v": A_(cache_v)[0].reshape(NPHYS * 128, 1024),
    })
    shared = {k: np.ascontiguousarray(v) for k, v in shared.items()}
    pt = A_(page_table).astype(np.int32)
    sre_in, sim_in = A_(state_ssm_re)[0], A_(state_ssm_im)[0]
    in_maps = []
    for i in range(ncores):
        seqs = list(range(i * NSEQ, (i + 1) * NSEQ))
        sm = slice(i * NS, (i + 1) * NS)
        m = dict(shared)
        m["xT"] = np.ascontiguousarray(np.stack([x_prompt[s].T.reshape(8, 128, S).transpose(1, 0, 2) for s in seqs]))
        m["xtok"] = np.ascontiguousarray(x_prompt[seqs])
        cols = [c_prompt[s] for s in seqs] + [c_sample[b] for b in range(i * NS, (i + 1) * NS)]
        m["cT"] = np.ascontiguousarray(np.stack([_fm(v) for v in cols], axis=2))
        m["crep"] = np.ascontiguousarray(np.stack([np.repeat(_fm(c_prompt[s])[:, :, None], 128, axis=2) for s in seqs]))
        xs = x_sample[sm, 0, :]
        m["xsT"] = np.ascontiguousarray(np.stack([_fm(v) for v in xs], axis=2))
        m["xstok"] = np.ascontiguousarray(xs)
        m["pt"] = np.ascontiguousarray(pt[sm].reshape(1, -1))
        m["sst_re"] = np.ascontiguousarray(sre_in[sm].reshape(NS, 32, 128).transpose(2, 1, 0))
        m["sst_im"] = np.ascontiguousarray(sim_in[sm].reshape(NS, 32, 128).transpose(2, 1, 0))
        in_maps.append(m)
    res = run_bass_kernel_spmd(nc, in_maps, core_ids=list(range(ncores)))
    R = res.results
    H, Dh = 16, 64
    y = np.concatenate([r["y"] for r in R], 0)
    kp = np.concatenate([r["kp"] for r in R], 0).reshape(1, B, S, H, Dh)
    vp = np.concatenate([r["vp"] for r in R], 0).reshape(1, B, S, H, Dh)
    sre = np.concatenate([r["sre"].transpose(0, 2, 1).reshape(NSEQ, 64, 64) for r in R], 0)[None]
    sim = np.concatenate([r["sim"].transpose(0, 2, 1).reshape(NSEQ, 64, 64) for r in R], 0)[None]
    ys = np.concatenate([r["ys"] for r in R], 0).reshape(DB, 1, D)
    ks = np.concatenate([r["ks"] for r in R], 0).reshape(1, DB, 1, H, Dh)
    vs = np.concatenate([r["vs"] for r in R], 0).reshape(1, DB, 1, H, Dh)
    ssre = np.concatenate([r["ssre"].transpose(2, 1, 0).reshape(NS, 64, 64) for r in R], 0)[None]
    ssim = np.concatenate([r["ssim"].transpose(2, 1, 0).reshape(NS, 64, 64) for r in R], 0)[None]
    outs = (y, ys, kp, vp, sre, sim, ks, vs, ssre, ssim)
    return tuple(np.ascontiguousarray(o.astype(np.float32)) for o in outs)
```

```python
import math
import ml_dtypes
from concourse.bass_utils import run_bass_kernel_spmd
import numpy as np
from contextlib import ExitStack
import concourse.bass as bass
import concourse.mybir as mybir

F32 = mybir.dt.float32
BF16 = mybir.dt.bfloat16
I32 = mybir.dt.int32
AF = mybir.ActivationFunctionType
ALU = mybir.AluOpType
AX = mybir.AxisListType


import types


def _snap(fn):
    if fn.__closure__ is None:
        return fn
    cells = []
    for c in fn.__closure__:
        try:
            cells.append(types.CellType(c.cell_contents))
        except ValueError:
            cells.append(c)
    return types.FunctionType(fn.__code__, fn.__globals__, fn.__name__, fn.__defaults__, tuple(cells))


class _Op:
    __slots__ = ("fn", "waits", "dwaits", "idx", "dma", "milestone")

    def __init__(self, fn, idx, dma=None):
        self.fn = _snap(fn)
        self.waits = []
        self.dwaits = []
        self.idx = idx
        self.dma = dma
        self.milestone = False


class Prog:
    ENGS = ("pe", "act", "dve", "pool", "sp")

    def __init__(self, nc, stack):
        self.nc = nc
        self.stack = stack
        self.gstack = stack
        self.h = {"pe": nc.tensor, "act": nc.scalar, "dve": nc.vector,
                  "pool": nc.gpsimd, "sp": nc.sync}
        self.ops = {e: [] for e in self.ENGS}
        self.seen = {e: {} for e in self.ENGS}
        self.last_w = {}
        self.readers = {}
        self.dsem_cnt = {}
        self.out_dma_keys = set()
        self.same_engine_sync = {"act", "dve", "pool"}

    def sb(self, name, shape, dt):
        self._uid = getattr(self, "_uid", 0) + 1
        return self.stack.enter_context(self.nc.sbuf_tensor("s%d_%s" % (self._uid, name), list(shape), dt))

    def ps(self, name, shape, dt=F32):
        return self.stack.enter_context(self.nc.psum_tensor(name, list(shape), dt))

    def _deps(self, reads, writes):
        deps = []
        for r in reads:
            t = self.last_w.get(r)
            if t is not None:
                deps.append(t)
        for w in writes:
            t = self.last_w.get(w)
            if t is not None:
                deps.append(t)
            deps.extend(self.readers.get(w, ()))
        return deps

    def _add_waits(self, eng, op, deps):
        seen = self.seen[eng]
        for t in deps:
            kind, key, val = t
            if kind == "e":
                if key == eng and eng not in self.same_engine_sync:
                    continue
                if key == eng and val >= op.idx:
                    continue
                if seen.get(("e", key), -1) >= val:
                    continue
                seen[("e", key)] = val
                op.waits.append((key, val))
                self.ops[key][val].milestone = True
            else:
                if seen.get(("d", key), -1) >= val:
                    continue
                seen[("d", key)] = val
                op.dwaits.append((key, val))

    def _commit(self, tok, reads, writes):
        for r in reads:
            self.readers.setdefault(r, []).append(tok)
        for w in writes:
            self.last_w[w] = tok
            self.readers[w] = []

    def op(self, eng, fn, reads=(), writes=()):
        lst = self.ops[eng]
        o = _Op(fn, len(lst))
        self._add_waits(eng, o, self._deps(reads, writes))
        lst.append(o)
        self._commit(("e", eng, o.idx), reads, writes)
        return o

    def dma(self, fn, semkey, reads=(), writes=(), q="sp", is_out=False):
        lst = self.ops[q]
        o = _Op(fn, len(lst), dma=semkey)
        self._add_waits(q, o, self._deps(reads, writes))
        lst.append(o)
        c = self.dsem_cnt.get(semkey, 0) + 16
        self.dsem_cnt[semkey] = c
        self._commit(("d", semkey, c), reads, writes)
        if is_out:
            self.out_dma_keys.add(semkey)
        return o

    def dma_multi(self, fns, semkey, reads=(), writes=(), q="sp", is_out=False):
        lst = self.ops[q]
        deps = self._deps(reads, writes)
        first = True
        for fn in fns:
            o = _Op(fn, len(lst), dma=semkey)
            if first:
                self._add_waits(q, o, deps)
                first = False
            lst.append(o)
            self.dsem_cnt[semkey] = self.dsem_cnt.get(semkey, 0) + 16
        self._commit(("d", semkey, self.dsem_cnt[semkey]), reads, writes)
        if is_out:
            self.out_dma_keys.add(semkey)

    def barrier(self):
        last = {}
        for e in self.ENGS:
            for o in reversed(self.ops[e]):
                if o.dma is None:
                    last[e] = o.idx
                    break
        dtoks = [("d", k, v) for k, v in self.dsem_cnt.items()]
        for e in self.ENGS:
            o = _Op(lambda h: h.nop(), len(self.ops[e]))
            deps = [("e", k, v) for k, v in last.items() if k != e] + dtoks
            self._add_waits(e, o, deps)
            self.ops[e].append(o)

    def emit(self, final=False):
        nc = self.nc
        self.barrier()
        if not hasattr(self, "esem"):
            self.esem = {e: self.gstack.enter_context(nc.semaphore("es_" + e)) for e in self.ENGS}
            self.dsem = {}
            self.mbase = {e: 0 for e in self.ENGS}
        for k in self.dsem_cnt:
            if k not in self.dsem:
                self.dsem[k] = self.gstack.enter_context(nc.semaphore("ds_%d" % len(self.dsem)))
        esem, dsem = self.esem, self.dsem
        fin = [(k, self.dsem_cnt[k]) for k in self.out_dma_keys] if final else []
        mcount = {}
        for e in self.ENGS:
            n = self.mbase[e]
            m = {}
            for o in self.ops[e]:
                if o.milestone:
                    assert o.dma is None
                    n += 1
                    m[o.idx] = n
            mcount[e] = m
            self.mbase[e] = n
        ops = self.ops

        def run(e, h):
            for o in ops[e]:
                for (k, v) in o.waits:
                    h.wait_ge(esem[k], mcount[k][v])
                for (k, v) in o.dwaits:
                    h.wait_ge(dsem[k], v)
                ins = o.fn(h)
                if o.dma is not None:
                    ins.then_inc(dsem[o.dma], 16)
                elif o.milestone:
                    ins.then_inc(esem[e], 1)
            if e == "sp":
                for (k, v) in fin:
                    h.wait_ge(dsem[k], v)

        with nc.Block() as block:
            @block.tensor
            def _(h):
                run("pe", h)

            @block.scalar
            def _(h):
                run("act", h)

            @block.vector
            def _(h):
                run("dve", h)

            @block.gpsimd
            def _(h):
                run("pool", h)

            @block.sync
            def _(h):
                run("sp", h)
        self.ops = {e: [] for e in self.ENGS}
        self.seen = {e: {} for e in self.ENGS}
        self.last_w = {}
        self.readers = {}
        self.nstage = getattr(self, "nstage", 0) + 1

PI = math.pi
ALPHA = 2.0 ** 0.25
LN_EPS = 1e-5
GK = 2.0 * math.sqrt(2.0 / PI)


def build(S, NSEQ, NS, NPG, NPHYS):
    nc = bass.Bass("TRN2", target_bir_lowering=False)
    NT = S // 512
    NB = S // 128
    NCOL = NSEQ + NS

    def din(name, shape, dt=F32):
        return nc.dram_tensor(name, list(shape), dt, kind="ExternalInput").ap()

    def dout(name, shape, dt=F32):
        return nc.dram_tensor(name, list(shape), dt, kind="ExternalOutput").ap()

    xT = din("xT", [NSEQ, 128, 8, S]); xtok = din("xtok", [NSEQ, S, 1024])
    cT = din("cT", [128, 8, NCOL]); crep = din("crep", [NSEQ, 128, 8, 128])
    xsT = din("xsT", [128, 8, NS]); xstok = din("xstok", [NS, 1024])
    w_cond = din("w_cond", [1024, 3072]); bcT = din("bcT", [128, 24]); bgrow = din("bgrow", [128, 1024])
    w_in = din("w_in", [1024, 8192]); w_glu = din("w_glu", [1024, 1024]); w_ao = din("w_ao", [1024, 1024])
    w_so = din("w_so", [1024, 1024]); w_out = din("w_out", [1024, 1024])
    bgluT = din("bgluT", [128, 8]); dT_d = din("dT", [128, 8]); sbb_d = din("sbb", [128, 16])
    are_d = din("a_re", [128, 32]); aim_d = din("a_im", [128, 32]); ldt_d = din("ldt", [128, 32])
    BTr_d = din("BTr", [128, 32, 128]); BTi_d = din("BTi", [128, 32, 128])
    CTr_d = din("CTr", [128, 32, 128]); CTi_d = din("CTi", [128, 32, 128])
    lng_d = din("lng", [128, 1024]); lnb_d = din("lnb", [128, 1024])
    Ubf_d = din("Ubf", [128, 128], BF16); Lcbf_d = din("Lcbf", [128, 128], BF16)
    mask01_d = din("mask01", [128, 128]); tvec_d = din("tvec", [128, 128]); iotap_d = din("iotap", [128, 1])
    U32_d = din("U32", [128, 128]); ones32_d = din("ones32", [128, 128])
    bmask_d = din("bmask", [16, 1024])
    ck = din("cache_k", [NPHYS * 128, 1024]); cv = din("cache_v", [NPHYS * 128, 1024])
    pt_d = din("pt", [1, NS * NPG], I32)
    sstr_d = din("sst_re", [128, 32, NS]); ssti_d = din("sst_im", [128, 32, NS])

    y_o = dout("y", [NSEQ, S, 1024]); kp_o = dout("kp", [NSEQ, S, 1024]); vp_o = dout("vp", [NSEQ, S, 1024])
    sre_o = dout("sre", [NSEQ, 128, 32]); sim_o = dout("sim", [NSEQ, 128, 32])
    ys_o = dout("ys", [NS, 1024]); ks_o = dout("ks", [NS, 1024]); vs_o = dout("vs", [NS, 1024])
    ssre_o = dout("ssre", [128, 32, NS]); ssim_o = dout("ssim", [128, 32, NS])

    with ExitStack() as gst:
        p = Prog(nc, gst)
        OP = lambda eng, fn, r=(), w=(): p.op(eng, fn, reads=r, writes=w)
        ps = [p.ps("psb%d" % i, [128, 512]) for i in range(8)]
        A = [p.sb("A%d" % i, [128, 8, S], BF16) for i in range(3)]
        stg = [p.sb("stg%d" % i, [128, 8, 128], F32) for i in range(2)]
        wbf = [p.sb("wbf%d" % i, [128, 8, 1024], BF16) for i in range(2)]
        Ubf = p.sb("Ubf", [128, 128], BF16); Lcbf = p.sb("Lcbf", [128, 128], BF16)
        zbf = p.sb("zbf", [128, 128], BF16)
        mask01 = p.sb("mask01", [128, 128], F32); tvec = p.sb("tvec", [128, 128], F32)
        modT = [p.sb("modT%d" % i, [128, 8, NCOL], F32) for i in range(2)]
        grow = p.sb("grow", [128, NSEQ, 1024], F32)
        gate_s = p.sb("gate_s", [128, 1024], F32)
        bcTs = p.sb("bcTs", [128, 24], F32); bgluTs = p.sb("bgluTs", [128, 8], F32)
        dTs = p.sb("dTs", [128, 8], F32); sbb = p.sb("sbbs", [128, 16], F32)
        mag = p.sb("mag", [128, 32], F32); ang = p.sb("ang", [128, 32], F32)
        abr = p.sb("abr", [128, 32], F32); abi = p.sb("abi", [128, 32], F32)
        cor = p.sb("cor", [128, 32], F32); coi = p.sb("coi", [128, 32], F32)
        rhor = p.sb("rhor", [128, 32], F32); rhoi = p.sb("rhoi", [128, 32], F32)
        xendr = p.sb("xendr", [128, 32], F32); xendi = p.sb("xendi", [128, 32], F32)
        psrr = [0]

        def bank(n=4, base=0):
            b = base + psrr[0] % n
            psrr[0] += 1
            return b

        def sin_of(out, arg, shift, tmp, ra, rt, ro, itile=None, t2=None, r2=None):
            r2 = r2 or (rt + "_2")
            OP("dve", lambda h: h.tensor_scalar(tmp, arg, shift, 1.0 / (2.0 * PI), ALU.add, ALU.mult), [ra], [rt])
            OP("dve", lambda h: h.tensor_copy(itile, tmp), [rt], [rt + "_i"])
            OP("dve", lambda h: h.tensor_copy(tmp, itile), [rt + "_i"], [rt])
            OP("dve", lambda h: h.tensor_scalar(t2, arg, shift, None, ALU.add), [ra], [r2])
            OP("dve", lambda h: h.scalar_tensor_tensor(tmp, tmp, -2.0 * PI, t2, ALU.mult, ALU.add), [rt, r2], [rt])
            OP("dve", lambda h: h.tensor_scalar(t2, tmp, PI, -2.0 * PI, ALU.is_gt, ALU.mult), [rt], [r2])
            OP("dve", lambda h: h.tensor_tensor(tmp, tmp, t2, ALU.add), [rt, r2], [rt])
            OP("act", lambda h: h.activation(out, tmp, AF.Sin), [rt], [ro])

        def load_w(src, slot):
            for q in range(8):
                sl = q % 2
                p.dma(lambda h, q=q, sl=sl: h.dma_start(out=stg[sl][:], in_=src[:, q * 128:(q + 1) * 128].rearrange("(kc p) n -> p kc n", p=128)),
                      "stg%d" % sl, writes=[("stg", sl)])
                OP("pool", lambda h, q=q, sl=sl: h.tensor_copy(wbf[slot][:, :, q * 128:(q + 1) * 128], stg[sl][:]),
                   [("stg", sl)], [("wbf", slot, q)])

        def proj_fm(slots, ins, tiles, epi):
            for ti, (t0, tw) in enumerate(tiles):
                for c in range(8):
                    bs = []
                    for j, (slot, (ai, ab)) in enumerate(zip(slots, ins)):
                        b = bank(6)
                        bs.append(b)
                        for kc in range(8):
                            OP("pe", lambda h, b=b, slot=slot, ab=ab, kc=kc, c=c, t0=t0, tw=tw: h.matmul(
                                ps[b][:, 0:tw], lhsT=wbf[slot][:, kc, c * 128:(c + 1) * 128], rhs=ab[:, kc, t0:t0 + tw],
                                start=(kc == 0), stop=(kc == 7)),
                               [("wbf", slot, c), ("A", ai, kc, ti)], [("ps", b)])
                    epi(c, ti, t0, tw, bs)

        with ExitStack() as st:
            p.stack = st
            cTa = p.sb("cTa", [128, 8, NCOL], F32)
            creps = p.sb("creps", [128, NSEQ, 8, 128], F32)
            are = p.sb("are", [128, 32], F32); aim = p.sb("aim", [128, 32], F32); ldt = p.sb("ldt", [128, 32], F32)
            wst = [p.sb("wst%d" % i, [128, 8, 512], F32) for i in range(2)]
            t32 = [p.sb("t32_%d" % i, [128, 32], F32) for i in range(6)]
            i32t = p.sb("i32t", [128, 32], I32); t2s = p.sb("t2s", [128, 32], F32)
            bgr = p.sb("bgr", [128, 1024], F32)
            loads = [(Ubf[:], Ubf_d), (Lcbf[:], Lcbf_d), (mask01[:], mask01_d), (tvec[:], tvec_d),
                     (bcTs[:], bcT), (bgluTs[:], bgluT), (dTs[:], dT_d), (sbb[:], sbb_d),
                     (cTa[:], cT), (are[:], are_d), (aim[:], aim_d),
                     (ldt[:], ldt_d), (bgr[:], bgrow)]
            for s_ in range(NSEQ):
                loads.append((creps[:, s_, :, :], crep[s_]))
            p.dma_multi([(lambda h, o=o, i=i: h.dma_start(out=o, in_=i)) for o, i in loads], "setup", writes=["setup"])
            OP("dve", lambda h: h.memset(zbf[:], 0.0), [], ["zbf"])
            for part in range(3):
                for half in range(2):
                    sl = (part * 2 + half) % 2
                    col0 = part * 1024 + half * 512
                    p.dma(lambda h, sl=sl, col0=col0: h.dma_start(out=wst[sl][:], in_=w_cond[:, col0:col0 + 512].rearrange("(kc p) n -> p kc n", p=128)),
                          "wst%d" % sl, writes=[("wst", sl)])
                    if part < 2:
                        for fcl in range(4):
                            fc = half * 4 + fcl
                            b = bank()
                            for kc in range(8):
                                OP("pe", lambda h, b=b, sl=sl, kc=kc, fcl=fcl: h.matmul(ps[b][:, 0:NCOL], lhsT=wst[sl][:, kc, fcl * 128:(fcl + 1) * 128],
                                                                                      rhs=cTa[:, kc, :], start=(kc == 0), stop=(kc == 7)),
                                   [("wst", sl), "setup"], [("ps", b)])
                            OP("dve", lambda h, b=b, part=part, fc=fc: h.tensor_scalar(modT[part][:, fc, :], ps[b][:, 0:NCOL], bcTs[:, part * 8 + fc:part * 8 + fc + 1],
                                                                                  1.0 if part == 1 else 0.0, ALU.add, ALU.add),
                               [("ps", b), "setup"], [("modT", part)])
                    else:
                        for s_ in range(NSEQ):
                            b = bank()
                            for kc in range(8):
                                OP("pe", lambda h, b=b, sl=sl, kc=kc, s_=s_: h.matmul(ps[b][:, :], lhsT=creps[:, s_, kc, :], rhs=wst[sl][:, kc, :],
                                                                                    start=(kc == 0), stop=(kc == 7)),
                                   [("wst", sl), "setup"], [("ps", b)])
                            OP("dve", lambda h, b=b, s_=s_, half=half: h.tensor_tensor(grow[:, s_, half * 512:(half + 1) * 512], ps[b][:, :], bgr[:, half * 512:(half + 1) * 512], ALU.add),
                               [("ps", b), "setup"], [("grow", s_, half)])
                        b = bank()
                        for kc in range(8):
                            OP("pe", lambda h, b=b, sl=sl, kc=kc: h.matmul(ps[b][0:NS, :], lhsT=cTa[:, kc, NSEQ:NCOL], rhs=wst[sl][:, kc, :],
                                                                         start=(kc == 0), stop=(kc == 7)),
                               [("wst", sl), "setup"], [("ps", b)])
                        OP("dve", lambda h, b=b, half=half: h.tensor_tensor(gate_s[0:NS, half * 512:(half + 1) * 512], ps[b][0:NS, :], bgr[0:NS, half * 512:(half + 1) * 512], ALU.add),
                           [("ps", b), "setup"], [("gate_s", half)])
            dt_, lm, c_, s_t, tmp, den = t32
            OP("act", lambda h: h.activation(dt_[:], ldt[:], AF.Exp), ["setup"], ["dt"])
            OP("dve", lambda h: h.tensor_tensor(lm[:], are[:], dt_[:], ALU.mult), ["dt", "setup"], ["lm"])
            OP("act", lambda h: h.activation(mag[:], lm[:], AF.Exp), ["lm"], ["mag"])
            OP("dve", lambda h: h.tensor_tensor(ang[:], aim[:], dt_[:], ALU.mult), ["dt", "setup"], ["ang"])
            sin_of(s_t[:], ang[:], 0.0, tmp[:], "ang", "tmp", "s_t", i32t[:], t2s[:])
            OP("dve", lambda h: h.tensor_tensor(abi[:], mag[:], s_t[:], ALU.mult), ["mag", "s_t"], ["abi"])
            sin_of(c_[:], ang[:], PI / 2, tmp[:], "ang", "tmp", "c_", i32t[:], t2s[:])
            OP("dve", lambda h: h.tensor_tensor(abr[:], mag[:], c_[:], ALU.mult), ["mag", "c_"], ["abr"])
            OP("dve", lambda h: h.tensor_tensor(den[:], are[:], are[:], ALU.mult), ["setup"], ["den"])
            OP("dve", lambda h: h.tensor_tensor(tmp[:], aim[:], aim[:], ALU.mult), ["setup"], ["tmp"])
            OP("dve", lambda h: h.tensor_tensor(den[:], den[:], tmp[:], ALU.add), ["den", "tmp"], ["den"])
            OP("dve", lambda h: h.reciprocal(den[:], den[:]), ["den"], ["den"])
            OP("dve", lambda h: h.tensor_scalar_add(c_[:], abr[:], -1.0), ["abr"], ["c_"])
            OP("dve", lambda h: h.tensor_tensor(tmp[:], c_[:], are[:], ALU.mult), ["c_"], ["tmp"])
            OP("dve", lambda h: h.tensor_tensor(s_t[:], abi[:], aim[:], ALU.mult), ["abi"], ["s_t"])
            OP("dve", lambda h: h.tensor_tensor(tmp[:], tmp[:], s_t[:], ALU.add), ["tmp", "s_t"], ["tmp"])
            OP("dve", lambda h: h.tensor_tensor(cor[:], tmp[:], den[:], ALU.mult), ["tmp", "den"], ["cor"])
            OP("dve", lambda h: h.tensor_tensor(tmp[:], abi[:], are[:], ALU.mult), ["abi"], ["tmp"])
            OP("dve", lambda h: h.tensor_tensor(s_t[:], c_[:], aim[:], ALU.mult), ["c_"], ["s_t"])
            OP("dve", lambda h: h.tensor_tensor(tmp[:], tmp[:], s_t[:], ALU.subtract), ["tmp", "s_t"], ["tmp"])
            OP("dve", lambda h: h.tensor_tensor(coi[:], tmp[:], den[:], ALU.mult), ["tmp", "den"], ["coi"])
            OP("dve", lambda h: h.tensor_scalar_mul(lm[:], ang[:], 128.0), ["ang"], ["lm"])
            sin_of(rhoi[:], lm[:], 0.0, tmp[:], "lm", "tmp", "rhoi", i32t[:], t2s[:])
            sin_of(rhor[:], lm[:], PI / 2, tmp[:], "lm", "tmp", "rhor", i32t[:], t2s[:])
            p.emit()

        def run_pass(seq):
            sample = seq is None
            T = NS if sample else S
            tiles = [(0, NS)] if sample else [(i * 512, 512) for i in range(NT)]
            col = NSEQ if sample else seq

            def fm_names(ai, ti):
                return [("A", ai, c, ti) for c in range(8)]

            with ExitStack() as st:
                p.stack = st
                if sample:
                    xs_ = p.sb("xs_", [128, 8, NS], F32)
                    p.dma(lambda h: h.dma_start(out=xs_[:], in_=xsT), "xst0", writes=["xs_"])
                    OP("dve", lambda h: h.tensor_tensor(xs_[:], xs_[:], modT[1][:, :, NSEQ:NCOL], ALU.mult), ["xs_"], ["xs_"])
                    OP("dve", lambda h: h.tensor_tensor(A[0][:, :, 0:NS], xs_[:], modT[0][:, :, NSEQ:NCOL], ALU.add), ["xs_"], fm_names(0, 0))
                else:
                    xst = [p.sb("xst%d" % i, [128, S], F32) for i in range(2)]
                    for c in range(8):
                        sl = c % 2
                        p.dma(lambda h, c=c, sl=sl: h.dma_start(out=xst[sl][:], in_=xT[seq, :, c, :]), "xst%d" % sl, writes=[("xst", sl)])
                        OP("dve", lambda h, c=c, sl=sl: h.tensor_scalar(A[0][:, c, :], xst[sl][:], modT[1][:, c, col:col + 1], modT[0][:, c, col:col + 1], ALU.mult, ALU.add),
                           [("xst", sl)], [("A", 0, c, ti) for ti in range(NT)])
                p.emit()

            with ExitStack() as st:
                p.stack = st
                load_w(w_in[:, 4096:5120], 0)

                def epi(c, ti, t0, tw, bs):
                    OP("dve", lambda h: h.tensor_copy(A[1][:, c, t0:t0 + tw], ps[bs[0]][:, 0:tw]), [("ps", bs[0])], [("A", 1, c, ti)])
                proj_fm([0], [(0, A[0])], tiles, epi)
                p.emit()

            with ExitStack() as st:
                p.stack = st
                load_w(w_glu, 0)
                W = 4 * 128
                ctr = p.sb("ctr", [128, 4, 128], F32); cti = p.sb("cti", [128, 4, 128], F32)
                btr = p.sb("btr", [128, 4, 128], F32); bti = p.sb("bti", [128, 4, 128], F32)
                Bre = p.sb("Bre", [128, 4, 128], BF16); Bim = p.sb("Bim", [128, 4, 128], BF16)
                Cr = p.sb("Cr", [128, 4, 128], BF16); nCr = p.sb("nCr", [128, 4, 128], BF16); nCi = p.sb("nCi", [128, 4, 128], BF16)
                tA = p.sb("tA", [128, 4, 128], F32); tB = p.sb("tB", [128, 4, 128], F32)
                tI = p.sb("tI", [128, 4, 128], I32); tC = p.sb("tC", [128, 4, 128], F32)
                ybuf = p.sb("ybuf", [128, 512], F32); y2b = p.sb("y2b", [128, 512], F32); y3b = p.sb("y3b", [128, 512], F32)
                if not sample:
                    cosT = p.sb("cosT", [128, 4, 128], F32); sinT = p.sb("sinT", [128, 4, 128], F32)
                    Str = p.sb("Str", [128, 4, 128], F32); Sti = p.sb("Sti", [128, 4, 128], F32)
                    rbr = [p.sb("rbr%d" % i, [128, 4, 128], F32) for i in range(2)]; rbi = [p.sb("rbi%d" % i, [128, 4, 128], F32) for i in range(2)]
                    tD = p.sb("tD", [128, 4, 128], F32)
                    P = [p.sb("P%d" % i, [128, 4, 128], BF16) for i in range(4)]
                    inr = p.sb("inr", [128, 4], F32); ini = p.sb("ini", [128, 4], F32)
                    sm = [p.sb("sm%d" % i, [128, 4], F32) for i in range(4)]
                else:
                    xr = p.sb("xr", [128, 32, NS], F32); xi = p.sb("xi", [128, 32, NS], F32)
                    xnr = p.sb("xnr", [128, 32, NS], F32); xni = p.sb("xni", [128, 32, NS], F32)
                    bur = p.sb("bur", [128, 4, NS], F32); bui = p.sb("bui", [128, 4, NS], F32)
                    Xrb = p.sb("Xrb", [128, 4, NS], BF16); Xib = p.sb("Xib", [128, 4, NS], BF16)
                    t4a = p.sb("t4a", [128, 4, NS], F32); t4b = p.sb("t4b", [128, 4, NS], F32)
                    p.dma(lambda h: h.dma_start(out=xr[:], in_=sstr_d), "sst0", writes=["xr"])
                    p.dma(lambda h: h.dma_start(out=xi[:], in_=ssti_d), "sst1", writes=["xi"])

                def gelu_store(F, t0, tw):
                    OP("dve", lambda h: h.tensor_tensor(y2b[:, 0:tw], ybuf[:, 0:tw], ybuf[:, 0:tw], ALU.mult), ["ybuf"], ["y2b"])
                    OP("dve", lambda h: h.tensor_scalar(y2b[:, 0:tw], y2b[:, 0:tw], 0.044715, 1.0, ALU.mult, ALU.add), ["y2b"], ["y2b"])
                    OP("dve", lambda h: h.tensor_tensor(y2b[:, 0:tw], y2b[:, 0:tw], ybuf[:, 0:tw], ALU.mult), ["y2b", "ybuf"], ["y2b"])
                    OP("act", lambda h: h.activation(y3b[:, 0:tw], y2b[:, 0:tw], AF.Sigmoid, scale=GK), ["y2b"], ["y3b"])
                    OP("dve", lambda h: h.tensor_tensor(A[2][:, F, t0:t0 + tw], ybuf[:, 0:tw], y3b[:, 0:tw], ALU.mult), ["y3b", "ybuf"],
                       [("A", 2, F, t0 // 512)])

                for F in range(8):
                    G0 = 4 * F
                    cob_r = cor[:, G0:G0 + 4].unsqueeze(2).broadcast_to([128, 4, 128])
                    cob_i = coi[:, G0:G0 + 4].unsqueeze(2).broadcast_to([128, 4, 128])
                    p.dma_multi([lambda h: h.dma_start(out=ctr[:], in_=CTr_d[:, G0:G0 + 4, :]),
                                 lambda h: h.dma_start(out=cti[:], in_=CTi_d[:, G0:G0 + 4, :]),
                                 lambda h: h.dma_start(out=btr[:], in_=BTr_d[:, G0:G0 + 4, :]),
                                 lambda h: h.dma_start(out=bti[:], in_=BTi_d[:, G0:G0 + 4, :])], "ssmtab", writes=["tabs"])
                    OP("pool", lambda h: h.tensor_copy(Bre[:], btr[:]), ["tabs"], ["Bre"])
                    OP("pool", lambda h: h.tensor_copy(Bim[:], bti[:]), ["tabs"], ["Bim"])
                    if not sample:
                        OP("dve", lambda h: h.tensor_tensor(tA[:], ctr[:], cob_r, ALU.mult), ["tabs"], ["tA"])
                        OP("dve", lambda h: h.tensor_tensor(tB[:], cti[:], cob_i, ALU.mult), ["tabs"], ["tB"])
                        OP("dve", lambda h: h.tensor_tensor(Cr[:], tA[:], tB[:], ALU.subtract), ["tA", "tB"], ["Cr"])
                        OP("dve", lambda h: h.tensor_tensor(nCr[:], tB[:], tA[:], ALU.subtract), ["tA", "tB"], ["nCr"])
                        OP("dve", lambda h: h.tensor_tensor(tA[:], ctr[:], cob_i, ALU.mult), ["tabs", "Cr", "nCr"], ["tA"])
                        OP("dve", lambda h: h.tensor_tensor(tB[:], cti[:], cob_r, ALU.mult), ["tabs", "Cr", "nCr"], ["tB"])
                        OP("dve", lambda h: h.tensor_tensor(tA[:], tA[:], tB[:], ALU.add), ["tA", "tB"], ["tA"])
                        OP("dve", lambda h: h.tensor_scalar_mul(nCi[:], tA[:], -1.0), ["tA"], ["nCi"])
                        OP("dve", lambda h: h.tensor_tensor(tA[:], ang[:, G0:G0 + 4].unsqueeze(2).broadcast_to([128, 4, 128]),
                                                            tvec[:, :].unsqueeze(1).broadcast_to([128, 4, 128]), ALU.mult), ["nCi"], ["tA"])
                        sin_of(sinT[:], tA[:], 0.0, tB[:], "tA", "tB", "sinT", tI[:], tC[:], "tC")
                        sin_of(cosT[:], tA[:], PI / 2, tB[:], "tA", "tB", "cosT", tI[:], tC[:], "tC")
                        OP("dve", lambda h: h.memset(inr[:], 0.0), [], ["inr"])
                        OP("dve", lambda h: h.memset(ini[:], 0.0), [], ["ini"])
                        c2 = cosT[:].rearrange("p a b -> p (a b)"); s2 = sinT[:].rearrange("p a b -> p (a b)")
                        scr = [t_[:].rearrange("p a b -> p (a b)") for t_ in (tA, tB, tC, tD)]
                        def stA(n):
                            t0 = n * 128
                            ti = t0 // 512
                            k = n % 2
                            bR, bI = (6, 7) if k == 0 else (2, 3)
                            for gl in range(4):
                                OP("pe", lambda h, gl=gl: h.matmul(ps[bR][:, gl * 128:(gl + 1) * 128], lhsT=Bre[:, gl, :], rhs=A[1][:, F, t0:t0 + 128], start=True, stop=True),
                                   ["Bre", ("A", 1, F, ti)], [("ps", bR)])
                                OP("pe", lambda h, gl=gl: h.matmul(ps[bI][:, gl * 128:(gl + 1) * 128], lhsT=Bim[:, gl, :], rhs=A[1][:, F, t0:t0 + 128], start=True, stop=True),
                                   ["Bim", ("A", 1, F, ti)], [("ps", bI)])
                            OP("dve", lambda h: h.tensor_tensor(scr[0], ps[bR][:, :], c2, ALU.mult), [("ps", bR), "cosT", "sinT"], ["tA"])
                            OP("dve", lambda h: h.tensor_tensor(scr[1], ps[bI][:, :], s2, ALU.mult), [("ps", bI), "cosT", "sinT"], ["tB"])
                            OP("dve", lambda h: h.tensor_tensor(scr[2], ps[bI][:, :], c2, ALU.mult), [("ps", bI), "cosT", "sinT"], ["tC"])
                            OP("dve", lambda h: h.tensor_tensor(scr[3], ps[bR][:, :], s2, ALU.mult), [("ps", bR), "cosT", "sinT"], ["tD"])
                            OP("pool", lambda h: h.tensor_tensor(rbr[k][:].rearrange("p a b -> p (a b)"), scr[0], scr[1], ALU.add), ["tA", "tB"], [("rbr", k)])
                            OP("dve", lambda h: h.tensor_tensor(rbi[k][:].rearrange("p a b -> p (a b)"), scr[2], scr[3], ALU.subtract), ["tC", "tD"], [("rbi", k)])

                        def stB(n):
                            t0 = n * 128
                            ti = t0 // 512
                            k = n % 2
                            bY = 4 + n % 2
                            if n > 0:
                                lr_ = Str[:, :, 127]; li_ = Sti[:, :, 127]
                                rr = rhor[:, G0:G0 + 4]; ri = rhoi[:, G0:G0 + 4]
                                OP("dve", lambda h: h.tensor_tensor(sm[0][:], rr, lr_, ALU.mult), [("Str", 0), ("Str", 1), ("Str", 2), ("Str", 3)], ["sm0"])
                                OP("dve", lambda h: h.tensor_tensor(sm[1][:], ri, li_, ALU.mult), [("Sti", 0), ("Sti", 1), ("Sti", 2), ("Sti", 3)], ["sm1"])
                                OP("dve", lambda h: h.tensor_tensor(sm[2][:], rr, li_, ALU.mult), [("Sti", 0), ("Sti", 1), ("Sti", 2), ("Sti", 3)], ["sm2"])
                                OP("dve", lambda h: h.tensor_tensor(sm[3][:], ri, lr_, ALU.mult), [("Str", 0), ("Str", 1), ("Str", 2), ("Str", 3)], ["sm3"])
                                OP("dve", lambda h: h.tensor_tensor(inr[:], sm[0][:], sm[1][:], ALU.subtract), ["sm0", "sm1"], ["inr"])
                                OP("dve", lambda h: h.tensor_tensor(ini[:], sm[2][:], sm[3][:], ALU.add), ["sm2", "sm3"], ["ini"])
                            for gl in range(4):
                                G = G0 + gl
                                OP("dve", lambda h, gl=gl, G=G: h.tensor_tensor_scan(Str[:, gl, :], mag[:, G:G + 1].to_broadcast([128, 128]), rbr[k][:, gl, :],
                                                                                      inr[:, gl:gl + 1], ALU.mult, ALU.add), [("rbr", k), "inr"], [("Str", gl)])
                                OP("dve", lambda h, gl=gl, G=G: h.tensor_tensor_scan(Sti[:, gl, :], mag[:, G:G + 1].to_broadcast([128, 128]), rbi[k][:, gl, :],
                                                                                      ini[:, gl:gl + 1], ALU.mult, ALU.add), [("rbi", k), "ini"], [("Sti", gl)])
                            OP("pool", lambda h: h.tensor_tensor(P[0][:], cosT[:], Str[:], ALU.mult), [("Str", 0), ("Str", 1), ("Str", 2), ("Str", 3)] + ["cosT", "sinT"], [("P", 0)])
                            OP("pool", lambda h: h.tensor_tensor(P[1][:], sinT[:], Sti[:], ALU.mult), [("Sti", 0), ("Sti", 1), ("Sti", 2), ("Sti", 3)] + ["cosT", "sinT"], [("P", 1)])
                            OP("pool", lambda h: h.tensor_tensor(P[2][:], cosT[:], Sti[:], ALU.mult), [("Sti", 0), ("Sti", 1), ("Sti", 2), ("Sti", 3)] + ["cosT", "sinT"], [("P", 2)])
                            OP("pool", lambda h: h.tensor_tensor(P[3][:], sinT[:], Str[:], ALU.mult), [("Str", 0), ("Str", 1), ("Str", 2), ("Str", 3)] + ["cosT", "sinT"], [("P", 3)])
                            kq = 0
                            for gl in range(4):
                                for (Wt, Pi) in ((Cr, 0), (nCr, 1), (nCi, 2), (nCi, 3)):
                                    OP("pe", lambda h, gl=gl, Wt=Wt, Pi=Pi, kq=kq: h.matmul(ps[bY][:, 0:128], lhsT=Wt[:, gl, :], rhs=P[Pi][:, gl, :], start=(kq == 0), stop=(kq == 15)),
                                       [("P", Pi), "Cr", "nCr", "nCi"], [("ps", bY)])
                                    kq += 1

                        def stEpi(n):
                            t0 = n * 128
                            ti = t0 // 512
                            bY = 4 + n % 2
                            o0 = (n % 4) * 128
                            OP("dve", lambda h: h.scalar_tensor_tensor(ybuf[:, o0:o0 + 128], A[1][:, F, t0:t0 + 128], dTs[:, F:F + 1], ps[bY][:, 0:128], ALU.mult, ALU.add),
                               [("ps", bY), ("A", 1, F, ti)], ["ybuf"])
                            if n % 4 == 3:
                                gelu_store(F, ti * 512, 512)

                        stA(0)
                        for n in range(NB):
                            if n + 1 < NB:
                                stA(n + 1)
                            stB(n)
                            if n > 0:
                                stEpi(n - 1)
                        stEpi(NB - 1)
                        lr_ = Str[:, :, 127]; li_ = Sti[:, :, 127]
                        c1 = cosT[:, :, 127]; s1 = sinT[:, :, 127]
                        OP("dve", lambda h: h.tensor_tensor(sm[0][:], c1, lr_, ALU.mult), [("Str", 0), ("Str", 1), ("Str", 2), ("Str", 3)] + ["cosT", "sinT"], ["sm0"])
                        OP("dve", lambda h: h.tensor_tensor(sm[1][:], s1, li_, ALU.mult), [("Sti", 0), ("Sti", 1), ("Sti", 2), ("Sti", 3)] + ["cosT", "sinT"], ["sm1"])
                        OP("dve", lambda h: h.tensor_tensor(sm[0][:], sm[0][:], sm[1][:], ALU.subtract), ["sm0", "sm1"], ["sm0"])
                        OP("dve", lambda h: h.tensor_tensor(sm[2][:], c1, li_, ALU.mult), [("Sti", 0), ("Sti", 1), ("Sti", 2), ("Sti", 3)] + ["cosT", "sinT"], ["sm2"])
                        OP("dve", lambda h: h.tensor_tensor(sm[3][:], s1, lr_, ALU.mult), [("Str", 0), ("Str", 1), ("Str", 2), ("Str", 3)] + ["cosT", "sinT"], ["sm3"])
                        OP("dve", lambda h: h.tensor_tensor(sm[2][:], sm[2][:], sm[3][:], ALU.add), ["sm2", "sm3"], ["sm2"])
                        cr4 = cor[:, G0:G0 + 4]; ci4 = coi[:, G0:G0 + 4]
                        OP("dve", lambda h: h.tensor_tensor(sm[1][:], cr4, sm[0][:], ALU.mult), ["sm0"], ["sm1"])
                        OP("dve", lambda h: h.tensor_tensor(sm[3][:], ci4, sm[2][:], ALU.mult), ["sm2"], ["sm3"])
                        OP("dve", lambda h: h.tensor_tensor(xendr[:, G0:G0 + 4], sm[1][:], sm[3][:], ALU.subtract), ["sm1", "sm3"], ["xendr"])
                        OP("dve", lambda h: h.tensor_tensor(sm[1][:], cr4, sm[2][:], ALU.mult), ["sm2", "xendr"], ["sm1"])
                        OP("dve", lambda h: h.tensor_tensor(sm[3][:], ci4, sm[0][:], ALU.mult), ["sm0", "xendr"], ["sm3"])
                        OP("dve", lambda h: h.tensor_tensor(xendi[:, G0:G0 + 4], sm[1][:], sm[3][:], ALU.add), ["sm1", "sm3"], ["xendi"])
                    else:
                        OP("pool", lambda h: h.tensor_copy(Cr[:], ctr[:]), ["tabs"], ["Cr"])
                        OP("dve", lambda h: h.tensor_scalar_mul(nCi[:], cti[:], -1.0), ["tabs"], ["nCi"])
                        bR, bI, bY = 6, 7, bank(2, 4)
                        for gl in range(4):
                            OP("pe", lambda h, gl=gl: h.matmul(ps[bR][:, gl * NS:(gl + 1) * NS], lhsT=Bre[:, gl, :], rhs=A[1][:, F, 0:NS], start=True, stop=True),
                               ["Bre", ("A", 1, F, 0)], [("ps", bR)])
                            OP("pe", lambda h, gl=gl: h.matmul(ps[bI][:, gl * NS:(gl + 1) * NS], lhsT=Bim[:, gl, :], rhs=A[1][:, F, 0:NS], start=True, stop=True),
                               ["Bim", ("A", 1, F, 0)], [("ps", bI)])
                        pr = ps[bR][:, 0:4 * NS].rearrange("p (a b) -> p a b", a=4); pi_ = ps[bI][:, 0:4 * NS].rearrange("p (a b) -> p a b", a=4)
                        cbr = cor[:, G0:G0 + 4].unsqueeze(2).broadcast_to([128, 4, NS]); cbi = coi[:, G0:G0 + 4].unsqueeze(2).broadcast_to([128, 4, NS])
                        abr4 = abr[:, G0:G0 + 4].unsqueeze(2).broadcast_to([128, 4, NS]); abi4 = abi[:, G0:G0 + 4].unsqueeze(2).broadcast_to([128, 4, NS])
                        xor_ = xr[:, G0:G0 + 4, :]; xoi_ = xi[:, G0:G0 + 4, :]
                        xnr_ = xnr[:, G0:G0 + 4, :]; xni_ = xni[:, G0:G0 + 4, :]
                        OP("dve", lambda h: h.tensor_tensor(t4a[:], pr, cbr, ALU.mult), [("ps", bR)], ["t4a"])
                        OP("dve", lambda h: h.tensor_tensor(t4b[:], pi_, cbi, ALU.mult), [("ps", bI)], ["t4b"])
                        OP("dve", lambda h: h.tensor_tensor(bur[:], t4a[:], t4b[:], ALU.subtract), ["t4a", "t4b"], ["bur"])
                        OP("dve", lambda h: h.tensor_tensor(t4a[:], pi_, cbr, ALU.mult), [("ps", bI), "bur"], ["t4a"])
                        OP("dve", lambda h: h.tensor_tensor(t4b[:], pr, cbi, ALU.mult), [("ps", bR), "bur"], ["t4b"])
                        OP("dve", lambda h: h.tensor_tensor(bui[:], t4a[:], t4b[:], ALU.add), ["t4a", "t4b"], ["bui"])
                        OP("dve", lambda h: h.tensor_tensor(t4a[:], xor_, abr4, ALU.mult), ["xr", "bui"], ["t4a"])
                        OP("dve", lambda h: h.tensor_tensor(t4b[:], xoi_, abi4, ALU.mult), ["xi", "bui"], ["t4b"])
                        OP("dve", lambda h: h.tensor_tensor(t4a[:], t4a[:], t4b[:], ALU.subtract), ["t4a", "t4b"], ["t4a"])
                        OP("dve", lambda h: h.tensor_tensor(xnr_, t4a[:], bur[:], ALU.add), ["t4a", "bur"], ["xnr"])
                        OP("dve", lambda h: h.tensor_tensor(t4a[:], xoi_, abr4, ALU.mult), ["xi", "xnr"], ["t4a"])
                        OP("dve", lambda h: h.tensor_tensor(t4b[:], xor_, abi4, ALU.mult), ["xr", "xnr"], ["t4b"])
                        OP("dve", lambda h: h.tensor_tensor(t4a[:], t4a[:], t4b[:], ALU.add), ["t4a", "t4b"], ["t4a"])
                        OP("dve", lambda h: h.tensor_tensor(xni_, t4a[:], bui[:], ALU.add), ["t4a", "bui"], ["xni"])
                        OP("pool", lambda h: h.tensor_copy(Xrb[:], xnr_), ["xnr"], ["Xrb"])
                        OP("pool", lambda h: h.tensor_copy(Xib[:], xni_), ["xni"], ["Xib"])
                        for gl in range(4):
                            OP("pe", lambda h, gl=gl: h.matmul(ps[bY][:, 0:NS], lhsT=Cr[:, gl, :], rhs=Xrb[:, gl, :], start=(gl == 0), stop=False),
                               ["Cr", "Xrb"], [("ps", bY)])
                            OP("pe", lambda h, gl=gl: h.matmul(ps[bY][:, 0:NS], lhsT=nCi[:, gl, :], rhs=Xib[:, gl, :], start=False, stop=(gl == 3)),
                               ["nCi", "Xib"], [("ps", bY)])
                        OP("dve", lambda h: h.scalar_tensor_tensor(ybuf[:, 0:NS], A[1][:, F, 0:NS], dTs[:, F:F + 1], ps[bY][:, 0:NS], ALU.mult, ALU.add),
                           [("ps", bY), ("A", 1, F, 0)], ["ybuf"])
                        gelu_store(F, 0, NS)
                if sample:
                    p.dma(lambda h: h.dma_start(out=ssre_o, in_=xnr[:]), "o_ss0", reads=["xnr"], is_out=True)
                    p.dma(lambda h: h.dma_start(out=ssim_o, in_=xni[:]), "o_ss1", reads=["xni"], is_out=True)
                else:
                    p.dma(lambda h: h.dma_start(out=sre_o[seq], in_=xendr[:]), "o_s0", reads=["xendr"], is_out=True)
                    p.dma(lambda h: h.dma_start(out=sim_o[seq], in_=xendi[:]), "o_s1", reads=["xendi"], is_out=True)
                p.emit()

            with ExitStack() as st:
                p.stack = st
                tmp = [p.sb("tmpf%d" % i, [128, 512], F32) for i in range(2)]
                load_w(w_in[:, 5120:6144], 1)

                def epi(c, ti, t0, tw, bs):
                    k = c % 2
                    OP("act", lambda h: h.activation(tmp[k][:, 0:tw], ps[bs[0]][:, 0:tw], AF.Sigmoid, bias=bgluTs[:, c:c + 1]), [("ps", bs[0])], [("tmp", k)])
                    OP("dve", lambda h: h.tensor_tensor(A[1][:, c, t0:t0 + tw], A[2][:, c, t0:t0 + tw], tmp[k][:, 0:tw], ALU.mult),
                       [("tmp", k), ("A", 2, c, ti)], [("A", 1, c, ti)])
                proj_fm([0], [(2, A[2])], tiles, epi)
                p.emit()

            with ExitStack() as st:
                p.stack = st
                tmp = [p.sb("tmpf%d" % i, [128, 512], F32) for i in range(2)]
                load_w(w_so, 0)

                def epi(c, ti, t0, tw, bs):
                    k = c % 2
                    OP("act", lambda h: h.activation(tmp[k][:, 0:tw], ps[bs[0]][:, 0:tw], AF.Sigmoid), [("ps", bs[0])], [("tmp", k)])
                    OP("dve", lambda h: h.tensor_tensor(tmp[k][:, 0:tw], tmp[k][:, 0:tw], ps[bs[0]][:, 0:tw], ALU.mult), [("ps", bs[0]), ("tmp", k)], [("tmp", k)])
                    OP("dve", lambda h: h.tensor_tensor(A[1][:, c, t0:t0 + tw], A[1][:, c, t0:t0 + tw], tmp[k][:, 0:tw], ALU.mult),
                       [("tmp", k)], [("A", 1, c, ti)])
                proj_fm([1], [(0, A[0])], tiles, epi)
                p.emit()

            with ExitStack() as st:
                p.stack = st
                tmp = [p.sb("tmpf%d" % i, [128, 512], F32) for i in range(2)]
                load_w(w_in[:, 7168:8192], 1)

                def epi(c, ti, t0, tw, bs):
                    k = c % 2
                    OP("act", lambda h: h.activation(tmp[k][:, 0:tw], ps[bs[1]][:, 0:tw], AF.Sigmoid), [("ps", bs[1])], [("tmp", k)])
                    OP("dve", lambda h: h.tensor_tensor(A[2][:, c, t0:t0 + tw], ps[bs[0]][:, 0:tw], tmp[k][:, 0:tw], ALU.mult),
                       [("tmp", k), ("ps", bs[0])], [("A", 2, c, ti)])
                proj_fm([0, 1], [(1, A[1]), (0, A[0])], tiles, epi)
                p.emit()

            if sample:
                sample_attention()
            else:
                prompt_attention(seq)

            with ExitStack() as st:
                p.stack = st
                tmp = [p.sb("tmpf%d" % i, [128, 512], F32) for i in range(2)]

                def epi(c, ti, t0, tw, bs):
                    k = c % 2
                    OP("act", lambda h: h.activation(tmp[k][:, 0:tw], ps[bs[0]][:, 0:tw], AF.Sigmoid), [("ps", bs[0])], [("tmp", k)])
                    OP("dve", lambda h: h.tensor_tensor(tmp[k][:, 0:tw], tmp[k][:, 0:tw], ps[bs[0]][:, 0:tw], ALU.mult), [("ps", bs[0]), ("tmp", k)], [("tmp", k)])
                    OP("dve", lambda h: h.tensor_tensor(A[1][:, c, t0:t0 + tw], A[1][:, c, t0:t0 + tw], tmp[k][:, 0:tw], ALU.mult),
                       [("tmp", k)], [("A", 1, c, ti)])
                proj_fm([0], [(0, A[0])], tiles, epi)
                p.emit()

            with ExitStack() as st:
                p.stack = st
                tmp = [p.sb("tmpf%d" % i, [128, 512], F32) for i in range(2)]
                load_w(w_in[:, 6144:7168], 0)

                def epi(c, ti, t0, tw, bs):
                    k = c % 2
                    OP("act", lambda h: h.activation(tmp[k][:, 0:tw], ps[bs[1]][:, 0:tw], AF.Sigmoid), [("ps", bs[1])], [("tmp", k)])
                    OP("dve", lambda h: h.tensor_tensor(tmp[k][:, 0:tw], ps[bs[0]][:, 0:tw], tmp[k][:, 0:tw], ALU.mult),
                       [("tmp", k), ("ps", bs[0])], [("tmp", k)])
                    OP("dve", lambda h: h.tensor_tensor(A[2][:, c, t0:t0 + tw], A[2][:, c, t0:t0 + tw], tmp[k][:, 0:tw], ALU.add),
                       [("tmp", k)], [("A", 2, c, ti)])
                proj_fm([1, 0], [(1, A[1]), (0, A[0])], tiles, epi)
                p.emit()

            with ExitStack() as st:
                p.stack = st
                xt = [p.sb("xt%d" % i, [128, 1024], F32) for i in range(2)]
                rr = [p.sb("rr%d" % i, [128, 1024], F32) for i in range(2)]
                stats = p.sb("stats", [128, 2, 6], F32); mv = p.sb("mv", [128, 2], F32); rstd = p.sb("rstd", [128, 1], F32)
                lng = p.sb("lngs", [128, 1024], F32); lnb = p.sb("lnbs", [128, 1024], F32)
                p.dma_multi([lambda h: h.dma_start(out=lng[:], in_=lng_d), lambda h: h.dma_start(out=lnb[:], in_=lnb_d)], "lnld", writes=["ln"])
                load_w(w_out, 0)
                ntile = 1 if sample else NB
                for i in range(ntile):
                    n = NS if sample else 128
                    k = i % 2
                    ti = 0 if sample else (i * 128) // 512
                    src = xstok if sample else xtok[seq, i * 128:(i + 1) * 128, :]
                    p.dma(lambda h, k=k, src=src, n=n: h.dma_start(out=xt[k][0:n, :], in_=src), "xt%d" % k, writes=[("xt", k)])
                    for half in range(2):
                        b = bank(4)
                        for kc in range(8):
                            OP("pe", lambda h, b=b, kc=kc, half=half, i=i, n=n: h.matmul(ps[b][0:n, :], lhsT=A[2][:, kc, i * 128:i * 128 + n],
                                                                                          rhs=wbf[0][:, kc, half * 512:(half + 1) * 512], start=(kc == 0), stop=(kc == 7)),
                               [("A", 2, kc, ti)] + [("wbf", 0, q) for q in range(half * 4, half * 4 + 4)], [("ps", b)])
                        g_ap = gate_s[0:n, half * 512:(half + 1) * 512] if sample else grow[:, seq, half * 512:(half + 1) * 512]
                        OP("dve", lambda h, b=b, k=k, half=half, n=n, g_ap=g_ap: h.tensor_tensor(rr[k][0:n, half * 512:(half + 1) * 512], ps[b][0:n, :], g_ap, ALU.mult),
                           [("ps", b)], [("rr", k)])
                    OP("dve", lambda h, k=k, n=n: h.scalar_tensor_tensor(rr[k][0:n, :], xt[k][0:n, :], ALPHA, rr[k][0:n, :], ALU.mult, ALU.add), [("xt", k), ("rr", k)], [("rr", k)])
                    for half in range(2):
                        OP("dve", lambda h, k=k, n=n, half=half: h.bn_stats(stats[0:n, half, :], rr[k][0:n, half * 512:(half + 1) * 512]), [("rr", k)], ["stats"])
                    OP("dve", lambda h, n=n: h.bn_aggr(mv[0:n, :], stats[0:n, :, :].rearrange("p a b -> p (a b)")), ["stats"], ["mv"])
                    OP("dve", lambda h, n=n: h.tensor_scalar_add(rstd[0:n, :], mv[0:n, 1:2], LN_EPS), ["mv"], ["rstd"])
                    OP("act", lambda h, n=n: h.activation(rstd[0:n, :], rstd[0:n, :], AF.Ln), ["rstd"], ["rstd"])
                    OP("act", lambda h, n=n: h.activation(rstd[0:n, :], rstd[0:n, :], AF.Exp, scale=-0.5), ["rstd"], ["rstd"])
                    OP("dve", lambda h, k=k, n=n: h.tensor_scalar(rr[k][0:n, :], rr[k][0:n, :], mv[0:n, 0:1], rstd[0:n, 0:1], ALU.subtract, ALU.mult), ["mv", "rstd", ("rr", k)], [("rr", k)])
                    OP("dve", lambda h, k=k, n=n: h.tensor_tensor(rr[k][0:n, :], rr[k][0:n, :], lng[0:n, :], ALU.mult), [("rr", k), "ln"], [("rr", k)])
                    OP("pool", lambda h, k=k, n=n: h.tensor_tensor(rr[k][0:n, :], rr[k][0:n, :], lnb[0:n, :], ALU.add), [("rr", k), "ln"], [("rr", k)])
                    dst = ys_o if sample else y_o[seq, i * 128:(i + 1) * 128, :]
                    p.dma(lambda h, k=k, n=n, dst=dst: h.dma_start(out=dst, in_=rr[k][0:n, :]), "o_y%d" % k, reads=[("rr", k)], is_out=True)
                p.emit()

        def prompt_attention(seq):
            with ExitStack() as st:
                p.stack = st
                wq = [p.sb("wq%d" % i, [128, 8, 128], BF16) for i in range(3)]
                qT = p.sb("qTp", [128, S], BF16); kT = p.sb("kTp", [128, S], BF16)
                vb = p.sb("vbp", [128, NB, 128], BF16)
                kst = p.sb("kst", [128, 4, 128], F32); vst = p.sb("vst", [128, 4, 128], F32)
                E = [p.sb("E%d" % i, [128, 512], F32) for i in range(4)]
                SP = [p.sb("SP%d" % i, [128, 512], BF16) for i in range(4)]
                EC = [p.sb("EC%d" % i, [128, 512], F32) for i in range(2)]
                Wb = [p.sb("Wb%d" % i, [128, 512], BF16) for i in range(2)]
                for c in range(8):
                    if c == 1:
                        load_w(w_in[:, 3072:4096], 0)
                        load_w(w_ao, 1)
                    for j in range(3):
                        sl = j % 2
                        col0 = j * 1024 + c * 128
                        p.dma(lambda h, sl=sl, col0=col0: h.dma_start(out=stg[sl][:], in_=w_in[:, col0:col0 + 128].rearrange("(kc p) n -> p kc n", p=128)),
                              "stg%d" % sl, writes=[("stg", sl)])
                        OP("pool", lambda h, sl=sl, j=j: h.tensor_copy(wq[j][:], stg[sl][:]), [("stg", sl)], [("wq", j)])
                    for j, dst in ((0, qT), (1, kT)):
                        for ti in range(NT):
                            b = bank(4)
                            for kc in range(8):
                                OP("pe", lambda h, b=b, kc=kc, j=j, ti=ti: h.matmul(ps[b][:, :], lhsT=wq[j][:, kc, :], rhs=A[0][:, kc, ti * 512:(ti + 1) * 512],
                                                                                  start=(kc == 0), stop=(kc == 7)), [("wq", j), ("A", 0, kc, ti)], [("ps", b)])
                            OP("dve", lambda h, b=b, dst=dst, ti=ti: h.tensor_copy(dst[:, ti * 512:(ti + 1) * 512], ps[b][:, :]), [("ps", b)], [("qk", j)])
                    for j, stb, dro in ((1, kst, kp_o), (2, vst, vp_o)):
                        for i4 in range(NB // 4):
                            b = bank(4)
                            for bl in range(4):
                                i = i4 * 4 + bl
                                for kc in range(8):
                                    OP("pe", lambda h, b=b, kc=kc, j=j, i=i, bl=bl: h.matmul(ps[b][:, bl * 128:(bl + 1) * 128], lhsT=A[0][:, kc, i * 128:(i + 1) * 128],
                                                                                            rhs=wq[j][:, kc, :], start=(kc == 0), stop=(kc == 7)),
                                       [("wq", j), ("A", 0, kc, i // 4)], [("ps", b)])
                            OP("dve", lambda h, b=b, stb=stb: h.tensor_copy(stb[:].rearrange("p a b -> p (a b)"), ps[b][:, :]), [("ps", b)], [("st", j)])
                            if j == 2:
                                OP("dve", lambda h, b=b, i4=i4: h.tensor_copy(vb[:, i4 * 4:(i4 + 1) * 4, :].rearrange("p a b -> p (a b)"), ps[b][:, :]), [("ps", b)], ["vb"])
                            p.dma(lambda h, stb=stb, dro=dro, i4=i4: h.dma_start(
                                out=dro[seq, i4 * 512:(i4 + 1) * 512, c * 128:(c + 1) * 128].rearrange("(a p) n -> p a n", p=128), in_=stb[:]),
                                "o_kv%d" % j, reads=[("st", j)], is_out=True)
                    for Tq in range(NT):
                        psS = (0, 1); acc = (2, 3); psO = 4
                        for hd in range(2):
                            hp = hd * 64
                            OP("pe", lambda h, hd=hd: h.matmul(ps[acc[hd]][:, :], lhsT=zbf[:, :], rhs=qT[:, 0:512], start=True, stop=True), ["zbf", ("qk", 0)], [("ps", acc[hd])])
                            OP("pe", lambda h, hp=hp: h.matmul(ps[psO][hp:hp + 64, :], lhsT=zbf[:, 0:64], rhs=qT[:, 0:512], start=True, stop=True), ["zbf", ("qk", 0)], [("psO", hd)])
                        def geo(j):
                            dj = j - 4 * Tq
                            c0 = max(0, 128 * dj)
                            return dj, c0, 512 * Tq + c0, 512 - c0

                        def stS(j):
                            dj, c0, q0, cw = geo(j)
                            for hd in range(2):
                                hp = hd * 64
                                OP("pe", lambda h, hd=hd, hp=hp: h.matmul(ps[psS[hd]][:, c0:512], lhsT=kT[hp:hp + 64, j * 128:(j + 1) * 128], rhs=qT[hp:hp + 64, q0:q0 + cw],
                                                                       start=True, stop=True), [("qk", 0), ("qk", 1)], [("ps", psS[hd])])

                        def stE(j):
                            dj, c0, q0, cw = geo(j)
                            kb = j % 2
                            for hd in range(2):
                                Eb = E[2 * kb + hd]; SPb = SP[2 * kb + hd]
                                OP("act", lambda h, hd=hd, Eb=Eb: h.activation(Eb[:, c0:512], ps[psS[hd]][:, c0:512], AF.Exp, bias=sbb[:, 2 * c + hd:2 * c + hd + 1], scale=0.125),
                                   [("ps", psS[hd])], [("E", kb, hd)])
                                if dj >= 0:
                                    OP("dve", lambda h, Eb=Eb: h.tensor_tensor(Eb[:, c0:c0 + 128], Eb[:, c0:c0 + 128], mask01[:, :], ALU.mult), [("E", kb, hd)], [("E", kb, hd)])
                                OP("act", lambda h, Eb=Eb, SPb=SPb: h.activation(SPb[:, c0:512], Eb[:, c0:512], AF.Ln, bias=1.0), [("E", kb, hd)], [("SP", kb, hd)])

                        def stU(j):
                            dj, c0, q0, cw = geo(j)
                            kb = j % 2
                            for hd in range(2):
                                SPb = SP[2 * kb + hd]
                                OP("pe", lambda h, hd=hd, SPb=SPb: h.matmul(ps[acc[hd]][:, c0:512], lhsT=Ubf[:, :], rhs=SPb[:, c0:512], start=False, stop=True, skip_group_check=True),
                                   [("SP", kb, hd)], [("ps", acc[hd])])

                        def stC(j):
                            dj, c0, q0, cw = geo(j)
                            for hd in range(2):
                                OP("act", lambda h, hd=hd: h.activation(EC[hd][:, c0:512], ps[acc[hd]][:, c0:512], AF.Exp, scale=-1.0), [("ps", acc[hd])], [("EC", hd)])

                        def stW(j):
                            dj, c0, q0, cw = geo(j)
                            kb = j % 2
                            for hd in range(2):
                                hp = hd * 64
                                Eb = E[2 * kb + hd]; SPb = SP[2 * kb + hd]
                                OP("pe", lambda h, hd=hd, SPb=SPb: h.matmul(ps[acc[hd]][:, c0:512], lhsT=Lcbf[:, :], rhs=SPb[:, c0:512], start=False, stop=True, skip_group_check=True),
                                   [("SP", kb, hd)], [("ps", acc[hd])])
                                OP("dve", lambda h, hd=hd, Eb=Eb: h.tensor_tensor(Wb[hd][:, c0:512], Eb[:, c0:512], EC[hd][:, c0:512], ALU.mult), [("E", kb, hd), ("EC", hd)], [("Wb", hd)])
                                OP("pe", lambda h, hd=hd, hp=hp: h.matmul(ps[psO][hp:hp + 64, c0:512], lhsT=vb[:, j, hd * 64:(hd + 1) * 64], rhs=Wb[hd][:, c0:512],
                                                                       start=False, stop=True, skip_group_check=True), [("Wb", hd), "vb"], [("psO", hd)])

                        jtop = 4 * Tq + 3
                        stS(jtop)
                        stE(jtop)
                        for j in range(jtop, -1, -1):
                            stU(j)
                            if j > 0:
                                stS(j - 1)
                            stC(j)
                            if j > 0:
                                stE(j - 1)
                            stW(j)
                        OP("dve", lambda h, Tq=Tq: h.tensor_copy(A[1][:, c, Tq * 512:(Tq + 1) * 512], ps[psO][:, :]), [("psO", 0), ("psO", 1)], [("A", 1, c, Tq)])
                p.emit()

        def sample_attention():
            GS = 4 if NS >= 4 else NS
            with ExitStack() as st:
                p.stack = st
                qtok = p.sb("qtok", [128, 1024], F32)
                oh = p.sb("oh", [128, 128], F32)
                ptb = p.sb("ptb", [128, NS * NPG], I32); idx = p.sb("idx", [128, NS * NPG], I32)
                iop = p.sb("iop", [128, 1], F32)
                U32 = p.sb("U32s", [128, 128], F32); ones32 = p.sb("ones32s", [128, 128], F32)
                bmask = p.sb("bmasks", [16, 1024], F32)
                qbs = [p.sb("qb%d" % i, [128, 1024], F32) for i in range(2)]
                pg_ = [p.sb("pg%d" % i, [128, 1024], F32) for i in range(3)]
                prod = [p.sb("prod%d" % i, [128, 1024], F32) for i in range(1)]
                sc = p.sb("sc", [128, GS, NPG, 16], F32)
                E8 = sc; SP8 = p.sb("SP8", [128, GS, NPG, 16], F32)
                WL = p.sb("WL", [128, GS, NPG, 16], F32)
                car = p.sb("car", [128, NPG, 16], F32); cum = p.sb("cum", [128, NPG, 16], F32)
                om = p.sb("om", [16, 1024], F32)
                p.dma_multi([lambda h: h.dma_start(out=ptb[:], in_=pt_d.partition_broadcast(128)),
                             lambda h: h.dma_start(out=iop[:], in_=iotap_d), lambda h: h.dma_start(out=U32[:], in_=U32_d),
                             lambda h: h.dma_start(out=ones32[:], in_=ones32_d), lambda h: h.dma_start(out=bmask[:], in_=bmask_d)], "sset", writes=["sset"])
                OP("dve", lambda h: h.tensor_scalar(idx[:], ptb[:], 128.0, iop[:, 0:1], ALU.mult, ALU.add), ["sset"], ["idx"])
                for j, (dst, dro) in enumerate(((qtok, None), (prod[0], ks_o), (prod[0], vs_o))):
                    load_w(w_in[:, j * 1024:(j + 1) * 1024], j % 2)
                    for half in range(2):
                        b = bank(4)
                        for kc in range(8):
                            OP("pe", lambda h, b=b, kc=kc, j=j, half=half: h.matmul(ps[b][0:NS, :], lhsT=A[0][:, kc, 0:NS], rhs=wbf[j % 2][:, kc, half * 512:(half + 1) * 512],
                                                                                  start=(kc == 0), stop=(kc == 7)),
                               [("A", 0, kc, 0)] + [("wbf", j % 2, q) for q in range(half * 4, half * 4 + 4)], [("ps", b)])
                        OP("dve", lambda h, b=b, dst=dst, half=half: h.tensor_copy(dst[0:NS, half * 512:(half + 1) * 512], ps[b][0:NS, :]), [("ps", b)], [("tok", False) if j == 0 else ("prod", 0)])
                    if dro is not None:
                        p.dma(lambda h, dst=dst, dro=dro: h.dma_start(out=dro, in_=dst[0:NS, :]), "o_kvs", reads=[("prod", 0)], writes=[], is_out=True)
                for g0 in range(0, NS, GS):
                    def qb_for(bl):
                        b_ = g0 + bl
                        k2 = 0
                        OP("dve", lambda h, b_=b_: h.tensor_scalar(oh[0:NS, :], iop[0:NS, 0:1].to_broadcast([NS, 128]), float(b_), None, ALU.is_equal), ["sset"], ["oh"])
                        for half in range(2):
                            bq = 6 + half
                            OP("pe", lambda h, bq=bq, b_=b_, half=half: h.matmul(ps[bq][:, :], lhsT=oh[0:NS, :], rhs=qtok[0:NS, half * 512:(half + 1) * 512], start=True, stop=True),
                               ["oh", ("tok", False)], [("ps", bq)])
                            OP("dve", lambda h, bq=bq, bl=bl, half=half: h.tensor_copy(qbs[bl % 2][:, half * 512:(half + 1) * 512], ps[bq][:, :]), [("ps", bq)], [("qbs", bl % 2)])
                        return None
                    NBUF, PF = 3, 2
                    pages = [(bl, pgi) for bl in range(GS) for pgi in range(NPG)]

                    def issue(src, ix):
                        bl, pgi = pages[ix]
                        e = (g0 + bl) * NPG + pgi
                        k3 = ix % NBUF
                        p.dma(lambda h: h.indirect_dma_start(out=pg_[k3][:], out_offset=None, in_=src,
                                                              in_offset=bass.IndirectOffsetOnAxis(ap=idx[:, e:e + 1], axis=0)),
                              "pg%d" % k3, reads=["idx"], writes=[("pg", k3)], q="pool")

                    for ix in range(min(PF, len(pages))):
                        issue(ck, ix)
                    for ix, (bl, pgi) in enumerate(pages):
                        k3 = ix % NBUF
                        if pgi == 0:
                            qb_for(bl)
                        if ix + PF < len(pages):
                            issue(ck, ix + PF)
                        OP("dve", lambda h: h.tensor_tensor(prod[0][:], pg_[k3][:], qbs[bl % 2][:], ALU.mult), [("pg", k3), ("qbs", bl % 2)], [("prod", 0)])
                        OP("dve", lambda h: h.tensor_reduce(sc[:, bl, pgi, :], prod[0][:].rearrange("p (a b) -> p a b", a=16), AX.X, ALU.add),
                           [("prod", 0)], ["sc"])
                    sc3 = sc[:].rearrange("p a b c -> p (a b) c")
                    OP("dve", lambda h: h.scalar_tensor_tensor(E8[:].rearrange("p a b c -> p (a b) c"), sc3, 0.125, sbb[:, :].unsqueeze(1).broadcast_to([128, GS * NPG, 16]), ALU.mult, ALU.add),
                       ["sc"], ["sc"])
                    E8f = E8[:].rearrange("p a b c -> p (a b c)"); SP8f = SP8[:].rearrange("p a b c -> p (a b c)")
                    OP("act", lambda h: h.activation(E8f, E8f, AF.Exp), ["sc"], ["sc"])
                    OP("act", lambda h: h.activation(SP8f, E8f, AF.Ln, bias=1.0), ["sc"], ["SP8"])
                    for bl in range(GS):
                        bC, bT = 4, 5
                        spb = SP8[:, bl, :, :].rearrange("p b c -> p (b c)")
                        OP("pe", lambda h, spb=spb: h.matmul(ps[bC][:, 0:NPG * 16], lhsT=U32[:, :], rhs=spb, start=True, stop=True), ["SP8", "sset"], [("ps", bC)])
                        OP("pe", lambda h, spb=spb: h.matmul(ps[bT][:, 0:NPG * 16], lhsT=ones32[:, :], rhs=spb, start=True, stop=True), ["SP8", "sset"], [("ps", bT)])
                        tot = ps[bT][:, 0:NPG * 16].rearrange("p (b c) -> p b c", c=16)
                        OP("dve", lambda h: h.memset(car[:, NPG - 1, :], 0.0), [], ["car"])
                        for pgi in range(NPG - 2, -1, -1):
                            OP("dve", lambda h, pgi=pgi, tot=tot: h.tensor_tensor(car[:, pgi, :], car[:, pgi + 1, :], tot[:, pgi + 1, :], ALU.add), ["car", ("ps", bT)], ["car"])
                        OP("dve", lambda h: h.tensor_tensor(cum[:].rearrange("p b c -> p (b c)"), ps[bC][:, 0:NPG * 16], car[:].rearrange("p b c -> p (b c)"), ALU.add),
                           [("ps", bC), "car"], ["cum"])
                        OP("act", lambda h: h.activation(cum[:].rearrange("p b c -> p (b c)"), cum[:].rearrange("p b c -> p (b c)"), AF.Exp, scale=-1.0), ["cum"], ["cum"])
                        OP("dve", lambda h, bl=bl: h.tensor_tensor(WL[:, bl, :, :], E8[:, bl, :, :], cum[:], ALU.mult), ["cum", "sc"], ["WL"])
                    for ix in range(min(PF, len(pages))):
                        issue(cv, ix)
                    for bl in range(GS):
                        b_ = g0 + bl
                        for pgi in range(NPG):
                            ix = bl * NPG + pgi
                            k3 = ix % NBUF
                            if ix + PF < len(pages):
                                issue(cv, ix + PF)
                            for half in range(2):
                                OP("pe", lambda h, half=half: h.matmul(ps[4 + half][0:16, :], lhsT=WL[:, bl, pgi, :], rhs=pg_[k3][:, half * 512:(half + 1) * 512],
                                                                     start=(pgi == 0), stop=(pgi == NPG - 1)), [("pg", k3), "WL"], [("ps", 4 + half)])
                        for half in range(2):
                            OP("dve", lambda h, half=half: h.tensor_tensor(om[:, half * 512:(half + 1) * 512], ps[4 + half][0:16, :], bmask[:, half * 512:(half + 1) * 512], ALU.mult),
                               [("ps", 4 + half), "sset"], ["om"])
                        for c8 in range(8):
                            OP("pe", lambda h, c8=c8, b_=b_: h.matmul(ps[3][:, c8 * NS + b_:c8 * NS + b_ + 1], lhsT=om[:, c8 * 128:(c8 + 1) * 128], rhs=ones32[0:16, 0:1], start=True, stop=True),
                               ["om", "sset"], [("ps", 3)])
                OP("dve", lambda h: h.tensor_copy(A[1][:, :, 0:NS], ps[3][:, 0:8 * NS].rearrange("p (a b) -> p a b", a=8)), [("ps", 3)], [("A", 1, c, 0) for c in range(8)])
                load_w(w_in[:, 3072:4096], 0)
                load_w(w_ao, 1)
                p.emit()

        for seq in range(NSEQ):
            run_pass(seq)
        run_pass(None)
        p.emit(final=True)
    return nc


NCORES = 8
_CACHE = {}


def _fm(v):
    return np.ascontiguousarray(v.reshape(8, 128).T)


def kernel(x_prompt, x_sample, c_prompt, c_sample, cache_k, cache_v, state_ssm_re, state_ssm_im, page_table,
           w_cond, b_cond, w_in, sb_bias, ssm_a_re, ssm_a_im, ssm_log_dt, ssm_b_re, ssm_b_im, ssm_c_re, ssm_c_im,
           ssm_d, w_glu, b_glu, w_att_out, w_ssm_out, w_out, ln_g, ln_b):
    f = np.float32
    A_ = lambda a: np.ascontiguousarray(np.asarray(a))
    x_prompt = A_(x_prompt); x_sample = A_(x_sample); c_prompt = A_(c_prompt); c_sample = A_(c_sample)
    B, S, D = x_prompt.shape
    DB = x_sample.shape[0]
    NPG = page_table.shape[1]
    NPHYS = cache_k.shape[1]
    ncores = min(NCORES, B)
    NSEQ = B // ncores
    NS = DB // ncores
    key = (S, NSEQ, NS, NPG, NPHYS)
    if key not in _CACHE:
        _CACHE[key] = build(*key)
    nc = _CACHE[key]
    bf = ml_dtypes.bfloat16
    kk = np.arange(128)
    Ubf = (kk[:, None] >= kk[None, :]).astype(f)
    consts = {
        "Ubf": Ubf.astype(bf), "Lcbf": (kk[:, None] < kk[None, :]).astype(f).astype(bf),
        "mask01": (kk[:, None] < kk[None, :]).astype(f), "tvec": np.tile(np.arange(128, dtype=f)[None, :], (128, 1)),
        "iotap": np.arange(128, dtype=f).reshape(128, 1), "U32": Ubf, "ones32": np.ones((128, 128), f),
        "bmask": np.kron(np.eye(16, dtype=f), np.ones((1, 64), f)),
    }
    bc = A_(b_cond)[0]
    are = A_(ssm_a_re)[0].reshape(32, 128).T
    aim = A_(ssm_a_im)[0].reshape(32, 128).T
    ldt = np.repeat(A_(ssm_log_dt)[0].reshape(32, 2, 1), 64, axis=2).reshape(32, 128).T
    bre, bim, cre, cim = A_(ssm_b_re)[0], A_(ssm_b_im)[0], A_(ssm_c_re)[0], A_(ssm_c_im)[0]
    BTr = np.zeros((128, 32, 128), f); BTi = np.zeros((128, 32, 128), f)
    CTr = np.zeros((128, 32, 128), f); CTi = np.zeros((128, 32, 128), f)
    for G in range(32):
        for g2 in range(2):
            g = 2 * G + g2
            r0 = 32 * (G % 4) + 16 * g2
            BTr[r0:r0 + 16, G, g2 * 64:(g2 + 1) * 64] = bre[g].T
            BTi[r0:r0 + 16, G, g2 * 64:(g2 + 1) * 64] = bim[g].T
            CTr[g2 * 64:(g2 + 1) * 64, G, r0:r0 + 16] = cre[g].T
            CTi[g2 * 64:(g2 + 1) * 64, G, r0:r0 + 16] = cim[g].T
    shared = dict(consts)
    shared.update({
        "w_cond": A_(w_cond)[0], "bcT": np.ascontiguousarray(bc.reshape(24, 128).T), "bgrow": np.tile(bc[None, 2048:3072], (128, 1)),
        "w_in": A_(w_in)[0], "w_glu": A_(w_glu)[0], "w_ao": A_(w_att_out)[0], "w_so": A_(w_ssm_out)[0], "w_out": A_(w_out)[0],
        "bgluT": _fm(A_(b_glu)[0]), "dT": _fm(A_(ssm_d)[0]), "sbb": np.tile(A_(sb_bias)[0][None, :], (128, 1)),
        "a_re": np.ascontiguousarray(are), "a_im": np.ascontiguousarray(aim), "ldt": np.ascontiguousarray(ldt),
        "BTr": BTr, "BTi": BTi, "CTr": CTr, "CTi": CTi,
        "lng": np.tile(A_(ln_g)[0][None, :], (128, 1)), "lnb": np.tile(A_(ln_b)[0][None, :], (128, 1)),
        "cache_k": A_(cache_k)[0].reshape(NPHYS * 128, 1024), "cache_v": A_(cache_v)[0].reshape(NPHYS * 128, 1024),
    })
    shared = {k: np.ascontiguousarray(v) for k, v in shared.items()}
    pt = A_(page_table).astype(np.int32)
    sre_in, sim_in = A_(state_ssm_re)[0], A_(state_ssm_im)[0]
    in_maps = []
    for i in range(ncores):
        seqs = list(range(i * NSEQ, (i + 1) * NSEQ))
        sm = slice(i * NS, (i + 1) * NS)
        m = dict(shared)
        m["xT"] = np.ascontiguousarray(np.stack([x_prompt[s].T.reshape(8, 128, S).transpose(1, 0, 2) for s in seqs]))
        m["xtok"] = np.ascontiguousarray(x_prompt[seqs])
        cols = [c_prompt[s] for s in seqs] + [c_sample[b] for b in range(i * NS, (i + 1) * NS)]
        m["cT"] = np.ascontiguousarray(np.stack([_fm(v) for v in cols], axis=2))
        m["crep"] = np.ascontiguousarray(np.stack([np.repeat(_fm(c_prompt[s])[:, :, None], 128, axis=2) for s in seqs]))
        xs = x_sample[sm, 0, :]
        m["xsT"] = np.ascontiguousarray(np.stack([_fm(v) for v in xs], axis=2))
        m["xstok"] = np.ascontiguousarray(xs)
        m["pt"] = np.ascontiguousarray(pt[sm].reshape(1, -1))
        m["sst_re"] = np.ascontiguousarray(sre_in[sm].reshape(NS, 32, 128).transpose(2, 1, 0))
        m["sst_im"] = np.ascontiguousarray(sim_in[sm].reshape(NS, 32, 128).transpose(2, 1, 0))
        in_maps.append(m)
    res = run_bass_kernel_spmd(nc, in_maps, core_ids=list(range(ncores)))
    R = res.results
    H, Dh = 16, 64
    y = np.concatenate([r["y"] for r in R], 0)
    kp = np.concatenate([r["kp"] for r in R], 0).reshape(1, B, S, H, Dh)
    vp = np.concatenate([r["vp"] for r in R], 0).reshape(1, B, S, H, Dh)
    sre = np.concatenate([r["sre"].transpose(0, 2, 1).reshape(NSEQ, 64, 64) for r in R], 0)[None]
    sim = np.concatenate([r["sim"].transpose(0, 2, 1).reshape(NSEQ, 64, 64) for r in R], 0)[None]
    ys = np.concatenate([r["ys"] for r in R], 0).reshape(DB, 1, D)
    ks = np.concatenate([r["ks"] for r in R], 0).reshape(1, DB, 1, H, Dh)
    vs = np.concatenate([r["vs"] for r in R], 0).reshape(1, DB, 1, H, Dh)
    ssre = np.concatenate([r["ssre"].transpose(2, 1, 0).reshape(NS, 64, 64) for r in R], 0)[None]
    ssim = np.concatenate([r["ssim"].transpose(2, 1, 0).reshape(NS, 64, 64) for r in R], 0)[None]
    outs = (y, ys, kp, vp, sre, sim, ks, vs, ssre, ssim)
    return tuple(np.ascontiguousarray(o.astype(np.float32)) for o in outs)
```
